# Optimizing a Trainium2 kernel written in Bass

```python
import math
import jax, jax.numpy as jnp
from jax import lax
import numpy as np

D_MODEL = 2048
BATCH = 2
SEQ = 16384
DEPTH = 1

HEAD_DIM = 128
MOBA_HEADS = D_MODEL // (2 * HEAD_DIM)
MOBA_BLOCK = 256
MOBA_TOPK = 3
MOBA_Q_CHUNK = 64
DIFF_HEADS = D_MODEL // (4 * HEAD_DIM)
DIFF_V_DIM = 2 * HEAD_DIM
DIFF_Q_BLOCK = 128
ROT_DIM = HEAD_DIM // 4
ROPE_THETA = 500000.0
D_FF = 4 * D_MODEL
PLE_DIM = 256
NORM_EPS = 1e-6
MOBA_WIDTH = MOBA_HEADS * HEAD_DIM
DIFF_QK_WIDTH = DIFF_HEADS * 2 * HEAD_DIM
DIFF_V_WIDTH = DIFF_HEADS * DIFF_V_DIM
IN_WIDTH = 3 * MOBA_WIDTH + 2 * DIFF_QK_WIDTH + DIFF_V_WIDTH + 2 * D_MODEL

kernel_name = "hybrid_moba_diffattn_gated_block"

F32 = jnp.float32


def rmsnorm(x, g):
    xf = x.astype(F32)
    y = xf * lax.rsqrt(jnp.mean(xf * xf, axis=-1, keepdims=True) + NORM_EPS)
    return (y * g.astype(F32)).astype(x.dtype)


def partial_rope(x, pos_f):
    half = ROT_DIM // 2
    inv_freq = 1.0 / (ROPE_THETA ** (jnp.arange(half, dtype=F32) * 2.0 / ROT_DIM))
    ang = pos_f[:, None] * inv_freq[None, :]
    cos, sin = jnp.cos(ang), jnp.sin(ang)
    xf = x.astype(F32)
    x1 = xf[..., :half]
    x2 = xf[..., half:ROT_DIM]
    out = jnp.concatenate([x1 * cos - x2 * sin, x2 * cos + x1 * sin, xf[..., ROT_DIM:]], axis=-1)
    return out.astype(x.dtype)


def moba_attention(q, k, v):
    B, H, S, Dh = q.shape
    nb = -(-S // MOBA_BLOCK)
    topk = min(MOBA_TOPK, nb)
    pad = nb * MOBA_BLOCK - S
    k_p = jnp.pad(k, ((0, 0), (0, 0), (0, pad), (0, 0)))
    v_p = jnp.pad(v, ((0, 0), (0, 0), (0, pad), (0, 0)))
    k_blk = k_p.reshape(B, H, nb, MOBA_BLOCK, Dh)
    v_blk = v_p.reshape(B, H, nb, MOBA_BLOCK, Dh)
    k_mean = jnp.mean(k_blk.astype(F32), axis=3)
    nc = S // MOBA_Q_CHUNK
    q_ch = jnp.moveaxis(q.reshape(B, H, nc, MOBA_Q_CHUNK, Dh), 2, 0)
    bi = jnp.arange(B)[:, None, None, None]
    hi = jnp.arange(H)[None, :, None, None]
    blk_ids = jnp.arange(nb)
    scale = Dh ** -0.5

    def chunk(args):
        ci, qc = args
        q_start = ci * MOBA_Q_CHUNK
        cur = q_start // MOBA_BLOCK
        gate = jnp.einsum('bhqd,bhnd->bhqn', qc.astype(F32), k_mean)
        gate = jnp.where(blk_ids < cur, gate, -jnp.inf)
        _, idx = lax.top_k(gate, topk)
        valid = idx < cur
        k_sel = k_blk[bi, hi, idx]
        v_sel = v_blk[bi, hi, idx]
        s_sel = jnp.einsum('bhqd,bhqjkd->bhqjk', qc, k_sel).astype(F32) * scale
        s_sel = jnp.where(valid[..., None], s_sel, -jnp.inf)
        s_sel = s_sel.reshape(B, H, MOBA_Q_CHUNK, topk * MOBA_BLOCK)
        own0 = cur * MOBA_BLOCK
        k_own = lax.dynamic_slice_in_dim(k_p, own0, MOBA_BLOCK, axis=2)
        v_own = lax.dynamic_slice_in_dim(v_p, own0, MOBA_BLOCK, axis=2)
        s_own = jnp.einsum('bhqd,bhkd->bhqk', qc, k_own).astype(F32) * scale
        q_pos = q_start + jnp.arange(MOBA_Q_CHUNK)
        k_pos = own0 + jnp.arange(MOBA_BLOCK)
        s_own = jnp.where(k_pos[None, :] <= q_pos[:, None], s_own, -jnp.inf)
        probs = jax.nn.softmax(jnp.concatenate([s_sel, s_own], axis=-1), axis=-1).astype(v.dtype)
        p_sel = probs[..., :topk * MOBA_BLOCK].reshape(B, H, MOBA_Q_CHUNK, topk, MOBA_BLOCK)
        p_own = probs[..., topk * MOBA_BLOCK:]
        return (jnp.einsum('bhqjk,bhqjkd->bhqd', p_sel, v_sel)
                + jnp.einsum('bhqk,bhkd->bhqd', p_own, v_own))

    out = lax.map(chunk, (jnp.arange(nc), q_ch))
    return jnp.moveaxis(out, 0, 2).reshape(B, H, S, Dh)


def diff_attention(q, k, v, lam, sub_g, lambda_init):
    B, H, _, S, Dh = q.shape
    nq = S // DIFF_Q_BLOCK
    q_bl = jnp.moveaxis(q.reshape(B, H, 2, nq, DIFF_Q_BLOCK, Dh), 3, 0)
    k_pos = jnp.arange(S)
    scale = Dh ** -0.5

    def block(args):
        bidx, qb = args
        s = jnp.einsum('bhcqd,bhckd->bhcqk', qb, k).astype(F32) * scale
        q_pos = bidx * DIFF_Q_BLOCK + jnp.arange(DIFF_Q_BLOCK)
        s = jnp.where(k_pos[None, :] <= q_pos[:, None], s, -jnp.inf)
        pr = jax.nn.softmax(s, axis=-1)
        attn = (pr[:, :, 0] - lam * pr[:, :, 1]).astype(v.dtype)
        return jnp.einsum('bhqk,bhkd->bhqd', attn, v)

    out = lax.map(block, (jnp.arange(nq), q_bl))
    out = jnp.moveaxis(out, 0, 2).reshape(B, H, S, DIFF_V_DIM)
    return (rmsnorm(out, sub_g) * (1.0 - lambda_init)).astype(v.dtype)


def setup_inputs(seed: int = 0) -> dict:
    key = jax.random.key(seed)
    ks = jax.random.split(key, 24)
    n = lambda k_, shape, s: jax.random.normal(k_, shape, F32) * s
    gain = lambda k_, dim: 1.0 + 0.02 * jax.random.normal(k_, (DEPTH, dim), F32)
    return {
        "x": n(ks[0], (BATCH, SEQ, D_MODEL), 1.0),
        "p": n(ks[1], (DEPTH, BATCH, SEQ, PLE_DIM), 1.0),
        "w_in": n(ks[2], (DEPTH, D_MODEL, IN_WIDTH), D_MODEL ** -0.5),
        "w_br_moba": n(ks[3], (DEPTH, MOBA_WIDTH, D_MODEL), MOBA_WIDTH ** -0.5),
        "w_br_diff": n(ks[4], (DEPTH, DIFF_V_WIDTH, D_MODEL), DIFF_V_WIDTH ** -0.5),
        "w_out": n(ks[5], (DEPTH, D_MODEL, D_MODEL), D_MODEL ** -0.5),
        "lambda_q1": n(ks[6], (DEPTH, HEAD_DIM), 0.1),
        "lambda_k1": n(ks[7], (DEPTH, HEAD_DIM), 0.1),
        "lambda_q2": n(ks[8], (DEPTH, HEAD_DIM), 0.1),
        "lambda_k2": n(ks[9], (DEPTH, HEAD_DIM), 0.1),
        "diff_subln_g": gain(ks[10], DIFF_V_DIM),
        "g_mix_pre": gain(ks[11], D_MODEL),
        "g_mix_post": gain(ks[12], D_MODEL),
        "w_up": n(ks[13], (DEPTH, D_MODEL, D_FF), D_MODEL ** -0.5),
        "w_down": n(ks[14], (DEPTH, D_FF, D_MODEL), D_FF ** -0.5),
        "g_mlp_pre": gain(ks[15], D_MODEL),
        "g_mlp_post": gain(ks[16], D_MODEL),
        "w_ple_proj": n(ks[17], (DEPTH, PLE_DIM, D_MODEL), PLE_DIM ** -0.5),
        "w_ple_gate": n(ks[18], (DEPTH, D_MODEL, D_MODEL), D_MODEL ** -0.5),
        "g_ple_pre": gain(ks[19], D_MODEL),
        "g_ple_post": gain(ks[20], D_MODEL),
    }


def reference(x, p, w_in, w_br_moba, w_br_diff, w_out, lambda_q1, lambda_k1, lambda_q2, lambda_k2,
              diff_subln_g, g_mix_pre, g_mix_post, w_up, w_down, g_mlp_pre, g_mlp_post,
              w_ple_proj, w_ple_gate, g_ple_pre, g_ple_post):
    B, S, _ = x.shape
    pos_f = jnp.arange(S, dtype=F32)
    widths = [MOBA_WIDTH, MOBA_WIDTH, MOBA_WIDTH, DIFF_QK_WIDTH, DIFF_QK_WIDTH, DIFF_V_WIDTH, D_MODEL]
    split_points = [int(v) for v in np.cumsum(widths)]
    h = x
    for i in range(DEPTH):
        lambda_init = 0.8 - 0.6 * math.exp(-0.3 * i)
        u = rmsnorm(h, g_mix_pre[i])
        proj = u @ w_in[i]
        qa, ka, va, qb, kb, vb, ga, gb = jnp.split(proj, split_points, axis=-1)
        to_heads = lambda t: t.reshape(B, S, MOBA_HEADS, HEAD_DIM).transpose(0, 2, 1, 3)
        qa = partial_rope(to_heads(qa), pos_f)
        ka = partial_rope(to_heads(ka), pos_f)
        va = to_heads(va)
        oa = moba_attention(qa, ka, va)
        to_sub = lambda t: t.reshape(B, S, DIFF_HEADS, 2, HEAD_DIM).transpose(0, 2, 3, 1, 4)
        qb = partial_rope(to_sub(qb), pos_f)
        kb = partial_rope(to_sub(kb), pos_f)
        vb = vb.reshape(B, S, DIFF_HEADS, DIFF_V_DIM).transpose(0, 2, 1, 3)
        lam = (jnp.exp(jnp.sum(lambda_q1[i].astype(F32) * lambda_k1[i].astype(F32)))
               - jnp.exp(jnp.sum(lambda_q2[i].astype(F32) * lambda_k2[i].astype(F32)))
               + lambda_init)
        ob = diff_attention(qb, kb, vb, lam, diff_subln_g[i], lambda_init)
        ya = oa.transpose(0, 2, 1, 3).reshape(B, S, MOBA_WIDTH) @ w_br_moba[i]
        yb = ob.transpose(0, 2, 1, 3).reshape(B, S, DIFF_V_WIDTH) @ w_br_diff[i]
        mixed = jax.nn.sigmoid(ga) * ya + jax.nn.sigmoid(gb) * yb
        h = h + rmsnorm(mixed @ w_out[i], g_mix_post[i])
        u2 = rmsnorm(h, g_mlp_pre[i])
        ff = jnp.square(jax.nn.relu(u2 @ w_up[i])) @ w_down[i]
        h = h + rmsnorm(ff, g_mlp_post[i])
        gate = jax.nn.sigmoid(rmsnorm(h, g_ple_pre[i]) @ w_ple_gate[i])
        e = (p[i] @ w_ple_proj[i]) * gate
        h = h + rmsnorm(e, g_ple_post[i])
    return h
```

```python
import math
import numpy as np
from contextlib import ExitStack
import concourse.bass as bass
import concourse.mybir as mybir
from concourse.bass_utils import run_bass_kernel_spmd

F32 = mybir.dt.float32
BF16 = mybir.dt.bfloat16
I32 = mybir.dt.int32
AF = mybir.ActivationFunctionType
ALU = mybir.AluOpType
AX = mybir.AxisListType

SEQ = 16384
DEBUG_DUMP = False
_LAST_RES = [None]
D = 2048
NCH = 16
EPS = 1e-6
BIG = 30000.0
ROPE_THETA = 500000.0
ENGS = ("tensor", "vector", "scalar", "gpsimd", "sync")
EPOCH = 30000


class Op:
    __slots__ = ("eng", "emit", "deps", "dkey", "sig", "has_cons")

    def __init__(self, eng, emit, dkey):
        self.eng = eng
        self.emit = emit
        self.deps = set()
        self.dkey = dkey
        self.sig = None
        self.has_cons = False


class SemPool:
    def __init__(self, nc, es):
        self.nc = nc
        self.es = es
        self.eng_sems = {e: [] for e in ENGS}
        self.eng_cnt = {e: 0 for e in ENGS}
        self.dma_sems = {}
        self.dma_cnt = {}
        self.n = 0

    def _new(self):
        self.n += 1
        return self.es.enter_context(self.nc.semaphore(f"sm{self.n}"))

    def next_eng_sig(self, e):
        c = self.eng_cnt[e]
        ep, v = divmod(c, EPOCH)
        if ep >= len(self.eng_sems[e]):
            self.eng_sems[e].append(self._new())
        self.eng_cnt[e] = c + 1
        return (self.eng_sems[e][ep], v + 1)

    def next_dma_sig(self, key):
        c = self.dma_cnt.get(key, 0)
        ep, v = divmod(c, EPOCH // 16)
        lst = self.dma_sems.setdefault(key, [])
        if ep >= len(lst):
            lst.append(self._new())
        self.dma_cnt[key] = c + 1
        return (lst[ep], 16 * (v + 1))


class Sched:
    def __init__(self, nc, pool, name):
        self.nc = nc
        self.pool = pool
        self.name = name
        self.ops = []
        self.last_w = {}
        self.readers = {}

    def add(self, eng, emit, reads=(), writes=(), dma=None):
        op = Op(eng, emit, dma)
        deps = op.deps
        lw = self.last_w
        rd = self.readers
        for k in reads:
            w = lw.get(k)
            if w is not None:
                deps.add(w)
        for k in writes:
            w = lw.get(k)
            if w is not None:
                deps.add(w)
            r = rd.get(k)
            if r:
                deps.update(r)
        for k in reads:
            rd.setdefault(k, []).append(op)
        for k in writes:
            lw[k] = op
            rd[k] = []
        deps.discard(op)
        if eng == "tensor":
            op.deps = {d for d in deps if not (d.eng == "tensor" and d.dkey is None)}
        for d in op.deps:
            d.has_cons = True
        self.ops.append(op)
        return op

    def dma(self, q, out, in_, reads=(), writes=(), key=None, **kw):
        return self.add(q, lambda e: e.dma_start(out=out, in_=in_, **kw), reads, writes, dma=key)

    def finish(self):
        last_dma = {}
        for op in self.ops:
            if op.dkey is not None:
                last_dma[op.dkey] = op
        fin = Op("sync", None, None)
        fin.deps = set(last_dma.values())
        for d in fin.deps:
            d.has_cons = True
        self.ops.append(fin)
        pool = self.pool
        for op in self.ops:
            if op.dkey is not None:
                op.sig = pool.next_dma_sig(op.dkey)
            elif op.has_cons and op.emit is not None:
                op.sig = pool.next_eng_sig(op.eng)
        per_eng = {e: [] for e in ENGS}
        for op in self.ops:
            per_eng[op.eng].append(op)
        with self.nc.Block(self.name) as block:
            for e in ENGS:
                lst = per_eng[e]
                if not lst:
                    continue

                def body(eng, lst=lst):
                    seen = {}
                    for op in lst:
                        waits = {}
                        for d in op.deps:
                            if d.sig is None:
                                continue
                            s, v = d.sig
                            k = id(s)
                            if seen.get(k, 0) >= v:
                                continue
                            if k not in waits or waits[k][1] < v:
                                waits[k] = (s, v)
                        for k, (s, v) in waits.items():
                            eng.wait_ge(s, v)
                            seen[k] = v
                        if op.emit is None:
                            continue
                        ins = op.emit(eng)
                        if op.sig is not None:
                            ins.then_inc(op.sig[0], 16 if op.dkey is not None else 1)

                getattr(block, e)(body)


def MM(out, lhsT, rhs, start, stop):
    return lambda e: e.matmul(out, lhsT=lhsT, rhs=rhs, start=start, stop=stop)


def TR(out, in_, ident):
    return lambda e: e.transpose(out, in_, ident)


def ACT(out, in_, func, bias=None, scale=None, accum_out=None):
    kw = {}
    if bias is not None:
        kw["bias"] = bias
    if scale is not None:
        kw["scale"] = scale
    if accum_out is not None:
        kw["accum_out"] = accum_out
    return lambda e: e.activation(out=out, in_=in_, func=func, **kw)


def TS(out, in0, s1, s2, op0, op1=None):
    if op1 is None:
        return lambda e: e.tensor_scalar(out=out, in0=in0, scalar1=s1, scalar2=None, op0=op0)
    return lambda e: e.tensor_scalar(out=out, in0=in0, scalar1=s1, scalar2=s2, op0=op0, op1=op1)


def TT(out, in0, in1, op):
    return lambda e: e.tensor_tensor(out=out, in0=in0, in1=in1, op=op)


def STT(out, in0, scalar, in1, op0, op1):
    return lambda e: e.scalar_tensor_tensor(out=out, in0=in0, scalar=scalar, in1=in1, op0=op0, op1=op1)


def CP(out, in_):
    return lambda e: e.tensor_copy(out=out, in_=in_)


def RCP(out, in_):
    return lambda e: e.reciprocal(out=out, in_=in_)


def MS(ap, v):
    return lambda e: e.memset(ap, v)


def bcast_row(handle_ap, n, parts=128, off=0):
    return bass.AP(handle_ap.tensor, off, [[0, parts], [1, n]])


def build(S):
    NT = S // 128
    NB = S // 256
    NQB = NB // 4
    TOWN = S // 4
    NTO = TOWN // 128
    NGA = S // 512
    NGO = TOWN // 512
    SCALE = 1.0 / math.sqrt(128.0)

    nc = bass.Bass("TRN2", target_bir_lowering=False)
    din = lambda n, s: nc.dram_tensor(n, s, F32, kind="ExternalInput").ap()
    xb = din("xb", [S, D])
    xo = din("xo", [TOWN, D])
    po = din("po", [TOWN, 256])
    w_k = din("w_k", [D, 2048])
    w_v = din("w_v", [D, 2048])
    w_q = din("w_q", [D, 2048])
    w_g = din("w_g", [D, 4096])
    w_bm = din("w_bm", [1024, D])
    w_bd = din("w_bd", [1024, D])
    w_out = din("w_out", [D, D])
    w_up = din("w_up", [D, 8192])
    w_down = din("w_down", [8192, D])
    w_pp = din("w_pp", [256, D])
    w_pg = din("w_pg", [D, D])
    lam4 = din("lam4", [4, 128])
    subg = din("subg", [1, 256])
    gains = din("gains", [6, D])
    cinfo = din("cinfo", [1, 4])
    out = nc.dram_tensor("out", [TOWN, D], F32, kind="ExternalOutput").ap()

    dscr = lambda n, s, dt: nc.dram_tensor(n, s, dt, kind="Internal").ap()
    if DEBUG_DUMP:
        dscr = lambda n, s, dt: nc.dram_tensor(n, s, dt, kind="ExternalOutput" if n in ("KT", "VS", "QT", "OT") else "Internal").ap()
    KT = dscr("KT", [16, 128, S], BF16)
    VS = dscr("VS", [S, 2048], BF16)
    QT = dscr("QT", [16, 128, TOWN], BF16)
    OT = dscr("OT", [16, 128, TOWN], BF16)

    with ExitStack() as ges:
        pool = SemPool(nc, ges)

        def common_consts(S_, es, need_rope, own):
            sb = lambda n, s, d: es.enter_context(nc.sbuf_tensor(S_.name + "_" + n, s, d))
            c = {}
            ident = sb("ident", [128, 128], BF16)
            ones_bf = sb("ones_bf", [128, 128], BF16)
            S_.add("vector", MS(ones_bf[:], 1.0), writes=["ones_bf"])
            S_.add("gpsimd", lambda e: e.affine_select(out=ident[:], in_=ones_bf[:], pattern=[[-1, 128]],
                                                       compare_op=ALU.is_equal, fill=0.0, base=0, channel_multiplier=1),
                   reads=["ones_bf"], writes=["ident"])
            c["ident"] = ident
            epst = sb("epst", [128, 1], F32)
            S_.add("vector", MS(epst[:], EPS), writes=["epst"])
            c["eps"] = epst
            ci = sb("ci", [128, 4], F32)
            S_.dma("sync", ci[:], bcast_row(cinfo, 4), writes=["ci"], key="ci")
            c["ci"] = ci
            if not need_rope:
                return c
            pswA = sb("pswA", [32, 32], F32)
            pswB = sb("pswB", [32, 32], F32)
            psw = sb("psw", [32, 32], F32)
            ones_f = sb("ones_f", [32, 32], F32)
            S_.add("vector", MS(ones_f[:], 1.0), writes=["ones_f"])
            for t, base in ((pswA, -16), (pswB, 16)):
                S_.add("gpsimd", lambda e, t=t, base=base: e.affine_select(
                    out=t[:], in_=ones_f[:], pattern=[[-1, 32]], compare_op=ALU.is_equal, fill=0.0,
                    base=base, channel_multiplier=1), reads=["ones_f"], writes=[id(t)])
            S_.add("vector", TT(psw[:], pswA[:], pswB[:], ALU.add), reads=[id(pswA), id(pswB)], writes=["psw"])
            c["psw"] = psw
            ri = sb("ri", [32, 1], I32)
            rf = sb("rf", [32, 1], F32)
            r16 = sb("r16", [32, 1], F32)
            tmp = sb("rtmp", [32, 1], F32)
            invf = sb("invf", [32, 1], F32)
            sgn = sb("sgn", [32, 1], F32)
            S_.add("gpsimd", lambda e: e.iota(ri[:], pattern=[[0, 1]], base=0, channel_multiplier=1), writes=["ri"])
            S_.add("vector", CP(rf[:], ri[:]), reads=["ri"], writes=["rf"])
            S_.add("vector", TS(tmp[:], rf[:], 16.0, -16.0, ALU.is_ge, ALU.mult), reads=["rf"], writes=["rtmp"])
            S_.add("vector", TT(r16[:], rf[:], tmp[:], ALU.add), reads=["rf", "rtmp"], writes=["r16"])
            S_.add("vector", TS(sgn[:], rf[:], 16.0, 2.0, ALU.is_ge, ALU.mult), reads=["rf"], writes=["sgn"])
            S_.add("vector", TS(sgn[:], sgn[:], -1.0, None, ALU.add), reads=["sgn"], writes=["sgn"])
            S_.add("vector", MS(invf[:], 0.0), writes=["invf"])
            for j in range(16):
                cj = float(np.float32(1.0) / (np.float32(ROPE_THETA) ** (np.float32(j * 2.0) / np.float32(32.0))))
                S_.add("vector", TS(tmp[:], r16[:], float(j), cj, ALU.is_equal, ALU.mult), reads=["r16", "rtmp"], writes=["rtmp"])
                S_.add("vector", TT(invf[:], invf[:], tmp[:], ALU.add), reads=["invf", "rtmp"], writes=["invf"])
            c["invf"] = invf
            c["sgn"] = sgn
            ioi = sb("ioi", [32, 512], I32)
            iof = sb("iof", [32, 512], F32)
            if own:
                S_.add("gpsimd", lambda e: e.iota(ioi[:], pattern=[[1024, 2], [1, 256]], base=0, channel_multiplier=0), writes=["ioi"])
                S_.add("vector", CP(iof[:], ioi[:]), reads=["ioi"], writes=["iof"])
                S_.add("vector", TS(iof[:], iof[:], ci[0:32, 1:2], None, ALU.add), reads=["iof", "ci"], writes=["iof"])
            else:
                S_.add("gpsimd", lambda e: e.iota(ioi[:], pattern=[[1, 512]], base=0, channel_multiplier=0), writes=["ioi"])
                S_.add("vector", CP(iof[:], ioi[:]), reads=["ioi"], writes=["iof"])
            c["iof"] = iof
            return c

        def rope_tables(S_, rt, c, base, slot):
            ang, ki, kf, red, sa, ca, cosT, sinT = rt
            k = ("rt", slot)
            TWO_PI = 2.0 * math.pi
            c1 = 6.28125
            c2 = float(np.float32(TWO_PI - c1))
            c3 = float(np.float32(TWO_PI - c1 - float(np.float32(TWO_PI - c1))))
            S_.add("vector", TS(ang[:], c["iof"][:], float(base), c["invf"][:, 0:1], ALU.add, ALU.mult),
                   reads=["iof", "invf"], writes=[k + ("ang",)])
            S_.add("vector", TS(ki[:], ang[:], 1.0 / TWO_PI, None, ALU.mult), reads=[k + ("ang",)], writes=[k + ("ki",)])
            S_.add("vector", CP(kf[:], ki[:]), reads=[k + ("ki",)], writes=[k + ("kf",)])
            S_.add("vector", STT(red[:], kf[:], -c1, ang[:], ALU.mult, ALU.add), reads=[k + ("ang",), k + ("kf",)], writes=[k + ("red",)])
            S_.add("vector", STT(red[:], kf[:], -c2, red[:], ALU.mult, ALU.add), reads=[k + ("red",), k + ("kf",)], writes=[k + ("red",)])
            S_.add("vector", STT(red[:], kf[:], -c3, red[:], ALU.mult, ALU.add), reads=[k + ("red",), k + ("kf",)], writes=[k + ("red",)])
            PI = math.pi
            S_.add("vector", TS(sa[:], red[:], PI, -TWO_PI, ALU.is_gt, ALU.mult), reads=[k + ("red",)], writes=[k + ("sa",)])
            S_.add("vector", TT(sa[:], sa[:], red[:], ALU.add), reads=[k + ("red",), k + ("sa",)], writes=[k + ("sa",)])
            S_.add("vector", TS(sa[:], sa[:], -PI, PI, ALU.max, ALU.min), reads=[k + ("sa",)], writes=[k + ("sa",)])
            S_.add("vector", TS(ca[:], red[:], PI / 2, -TWO_PI, ALU.is_gt, ALU.mult), reads=[k + ("red",)], writes=[k + ("ca",)])
            S_.add("vector", STT(ca[:], red[:], PI / 2, ca[:], ALU.add, ALU.add), reads=[k + ("red",), k + ("ca",)], writes=[k + ("ca",)])
            S_.add("vector", TS(ca[:], ca[:], -PI, PI, ALU.max, ALU.min), reads=[k + ("ca",)], writes=[k + ("ca",)])
            S_.add("scalar", ACT(sinT[:], sa[:], AF.Sin), reads=[k + ("sa",)], writes=[k + ("sin",)])
            S_.add("scalar", ACT(cosT[:], ca[:], AF.Sin), reads=[k + ("ca",)], writes=[k + ("cos",)])
            S_.add("vector", TS(sinT[:], sinT[:], c["sgn"][:, 0:1], None, ALU.mult), reads=[k + ("sin",), "sgn"],
                   writes=[k + ("sin",)])
            return cosT, sinT, k + ("cos",), k + ("sin",)

        def norm_tile(S_, c, xt, xkey, grep, gkey, ub, ubkey, st, stkey, nfeat=D):
            S_.add("scalar", ACT(ub, xt, AF.Square, accum_out=st[:, 0:1]), reads=[xkey], writes=[ubkey, stkey])
            S_.add("scalar", ACT(st[:, 1:2], st[:, 0:1], AF.Sqrt, bias=c["eps"][:, 0:1], scale=1.0 / nfeat),
                   reads=[stkey, "epst"], writes=[stkey])
            S_.add("vector", RCP(st[:, 2:3], st[:, 1:2]), reads=[stkey], writes=[stkey])
            S_.add("vector", STT(ub, xt, st[:, 2:3], grep, ALU.mult, ALU.mult), reads=[xkey, stkey, gkey], writes=[ubkey])

        def transpose_tile(S_, c, ub, ubkey, nch, ptr, ptrkeys, dst_fn, dstkeys):
            nh = (nch + 7) // 8
            for hf in range(nh):
                cn = min(8, nch - hf * 8)
                for cc in range(cn):
                    ch = hf * 8 + cc
                    S_.add("tensor", TR(ptr[hf][:, cc, :], ub[:, ch * 128:(ch + 1) * 128], c["ident"][:]),
                           reads=[ubkey, "ident"], writes=[ptrkeys[hf]])
                eng = "scalar" if hf % 2 == 0 else "vector"
                if eng == "scalar":
                    S_.add("scalar", ACT(dst_fn(hf, cn), ptr[hf][:, 0:cn, :], AF.Copy), reads=[ptrkeys[hf]], writes=dstkeys)
                else:
                    S_.add("vector", CP(dst_fn(hf, cn), ptr[hf][:, 0:cn, :]), reads=[ptrkeys[hf]], writes=dstkeys)

        def phase_proj(name, kind, xsrc, ntiles, wsrc, dst):
            own = (name == "B")
            with ExitStack() as es:
                S_ = Sched(nc, pool, name)
                sb = lambda n, s, d: es.enter_context(nc.sbuf_tensor(S_.name + "_" + n, s, d))
                ps = lambda n, s, d: es.enter_context(nc.psum_tensor(S_.name + "_" + n, s, d))
                c = common_consts(S_, es, kind == "KQ", own)
                w = sb("w", [128, NCH, 2048], BF16)
                wv = wsrc.rearrange("(c p) n -> p c n", p=128)
                for j in range(4):
                    S_.dma("gpsimd", w[:, :, j * 512:(j + 1) * 512], wv[:, :, j * 512:(j + 1) * 512], writes=[("w", j)], key=("w", j))
                wkeys = [("w", j) for j in range(4)]
                grep = sb("grep", [128, D], F32)
                S_.dma("sync", grep[:], bcast_row(gains, D, off=0), writes=["grep"], key="grep")
                xt = [sb(f"xt{i}", [128, D], F32) for i in range(2)]
                ub = [sb(f"ub{i}", [128, D], BF16) for i in range(4)]
                st = [sb(f"st{i}", [128, 4], F32) for i in range(2)]
                uT = [sb(f"uT{i}", [128, NCH, 512], BF16) for i in range(2)]
                ptr = [ps(f"ptr{i}", [128, 8, 128], BF16) for i in range(2)]
                pk = [ps(f"pk{i}", [128, 512], F32) for i in range(3)]
                if kind == "KQ":
                    pp = [ps(f"pp{i}", [128, 512], F32) for i in range(2)]
                    rt = [[sb(f"rt{s}_{n}", [32, 512], I32 if n == 1 else F32) for n in range(8)] for s in range(2)]
                    r32 = [sb(f"r32_{i}", [32, 512], F32) for i in range(2)]
                    t1 = sb("t1", [32, 512], F32)
                    t2 = sb("t2", [32, 512], F32)
                    osb = [sb(f"osb{i}", [128, 512], BF16) for i in range(3)]
                else:
                    vsb = [sb(f"vsb{i}", [128, 2048], BF16) for i in range(2)]
                ngroups = ntiles // 4
                for G in range(ngroups):
                    gs = G % 2
                    for tt in range(4):
                        t = G * 4 + tt
                        xs = t % 2
                        S_.dma("sync", xt[xs][:], xsrc[t * 128:(t + 1) * 128, :], writes=[("xt", xs)], key=("xt", xs))
                        norm_tile(S_, c, xt[xs][:], ("xt", xs), grep[:], "grep", ub[tt][:], ("ub", tt), st[xs], ("st", xs))
                        transpose_tile(S_, c, ub[tt], ("ub", tt), NCH, ptr, [("ptr", 0), ("ptr", 1)],
                                       lambda hf, cn, gs=gs, tt=tt: uT[gs][:, hf * 8:hf * 8 + cn, tt * 128:(tt + 1) * 128],
                                       [("uT", gs, tt)])
                    uTk = [("uT", gs, tt) for tt in range(4)]
                    if kind == "KQ":
                        base = (2048 * G) if own else (512 * G)
                        cosT, sinT, ck, sk = rope_tables(S_, rt[gs], c, base, gs)

                        def fin(h, G=G, cosT=cosT, sinT=sinT, ck=ck, sk=sk):
                            r = h % 2
                            o = h % 3
                            S_.add("tensor", MM(pp[r][0:32, :], c["psw"][:, :], r32[r][:, :], True, True),
                                   reads=[("r32", r), "psw"], writes=[("pp", r)])
                            S_.add("vector", TT(t1[:], r32[r][:], cosT[:], ALU.mult), reads=[("r32", r), ck], writes=["t1"])
                            S_.add("vector", TT(t2[:], pp[r][0:32, :], sinT[:], ALU.mult), reads=[("pp", r), sk], writes=["t2"])
                            S_.add("vector", TT(osb[o][0:32, :], t1[:], t2[:], ALU.add), reads=["t1", "t2", ("osb", o, 0)], writes=[("osb", o, 0)])
                            S_.dma("sync", dst[h][:, G * 512:(G + 1) * 512], osb[o][:], reads=[("osb", o, 0)],
                                   key=("osb", o))

                        for h in range(16):
                            p = h % 3
                            for ch in range(NCH):
                                S_.add("tensor", MM(pk[p][:], w[:, ch, h * 128:(h + 1) * 128], uT[gs][:, ch, :], ch == 0, ch == NCH - 1),
                                       reads=uTk + [("w", h // 4)] if ch in (0, NCH - 1) else (), writes=[("pk", p)])
                            S_.add("scalar", ACT(osb[h % 3][:, :], pk[p][:, :], AF.Copy), reads=[("pk", p)],
                                   writes=[("osb", h % 3, 0)])
                            S_.add("scalar", ACT(r32[h % 2][:], pk[p][0:32, :], AF.Copy), reads=[("pk", p)], writes=[("r32", h % 2)])
                            if h >= 1:
                                fin(h - 1)
                        fin(15)
                    else:
                        for tt in range(4):
                            t = G * 4 + tt
                            vs = t % 2
                            for cg in range(4):
                                p = (tt * 4 + cg) % 3
                                for ch in range(NCH):
                                    S_.add("tensor", MM(pk[p][:], uT[gs][:, ch, tt * 128:(tt + 1) * 128], w[:, ch, cg * 512:(cg + 1) * 512],
                                                        ch == 0, ch == NCH - 1),
                                           reads=[("uT", gs, tt), ("w", cg)] if ch in (0, NCH - 1) else (), writes=[("pk", p)])
                                if cg % 2 == 0:
                                    S_.add("scalar", ACT(vsb[vs][:, cg * 512:(cg + 1) * 512], pk[p][:], AF.Copy), reads=[("pk", p)],
                                           writes=[("vsb", vs, cg)])
                                else:
                                    S_.add("vector", CP(vsb[vs][:, cg * 512:(cg + 1) * 512], pk[p][:]), reads=[("pk", p)],
                                           writes=[("vsb", vs, cg)])
                            S_.dma("sync", dst[t * 128:(t + 1) * 128, :], vsb[vs][:], reads=[("vsb", vs, cg) for cg in range(4)],
                                   key=("vsb", vs))
                S_.finish()

        def attn_consts(S_, es, c):
            sb = lambda n, s, d: es.enter_context(nc.sbuf_tensor(S_.name + "_" + n, s, d))
            ci = c["ci"]
            pmi = sb("pmi", [128, 128], I32)
            pmf = sb("pmf", [128, 128], F32)
            pmask = sb("pmask", [128, 16, 128], BF16)
            for d in range(4):
                for ktl in range(2):
                    for qs in range(2):
                        idx = d * 4 + ktl * 2 + qs
                        base = 256 * d + 128 * ktl - 128 * qs
                        S_.add("gpsimd", lambda e, base=base: e.iota(pmi[:], pattern=[[-1, 128]], base=base, channel_multiplier=1),
                               writes=["pmi"])
                        S_.add("vector", CP(pmf[:], pmi[:]), reads=["pmi"], writes=["pmf"])
                        S_.add("vector", TS(pmf[:], pmf[:], ci[:, 1:2], 0.0, ALU.subtract, ALU.is_gt), reads=["pmf", "ci"], writes=["pmf"])
                        S_.add("vector", TS(pmask[:, idx, :], pmf[:], -BIG, None, ALU.mult), reads=["pmf"], writes=["pmask"])
            c["pmask"] = pmask
            return c

        def phase_moba():
            with ExitStack() as es:
                S_ = Sched(nc, pool, "CM")
                sb = lambda n, s, d: es.enter_context(nc.sbuf_tensor(S_.name + "_" + n, s, d))
                ps = lambda n, s, d: es.enter_context(nc.psum_tensor(S_.name + "_" + n, s, d))
                c = common_consts(S_, es, False, False)
                attn_consts(S_, es, c)
                ci = c["ci"]
                ident = c["ident"]
                pmask = c["pmask"]
                esel = sb("esel", [128, NB, 128], BF16)
                eones = sb("eones", [128, NB, 128], BF16)
                S_.add("vector", MS(eones[:], 1.0), writes=["eones"])
                S_.add("gpsimd", lambda e: e.affine_select(out=esel[:], in_=eones[:], pattern=[[-1, NB], [0, 128]],
                                                           compare_op=ALU.is_equal, fill=0.0, base=0, channel_multiplier=1),
                       reads=["eones"], writes=["esel"])
                ni = sb("ni", [128, NB], I32)
                nf = sb("nf", [128, NB], F32)
                vv = sb("vv", [128, NB], F32)
                gbA = sb("gbA", [128, NQB, NB], F32)
                ownb = sb("ownb", [128, NQB, NB], F32)
                futb = sb("futb", [128, NQB, NB], F32)
                S_.add("gpsimd", lambda e: e.iota(ni[:], pattern=[[1, NB]], base=0, channel_multiplier=0), writes=["ni"])
                S_.add("vector", CP(nf[:], ni[:]), reads=["ni"], writes=["nf"])
                for i in range(NQB):
                    S_.add("vector", TS(vv[:], nf[:], ci[:, 0:1], float(4 * i), ALU.subtract, ALU.subtract), reads=["nf", "ci"], writes=["vv"])
                    S_.add("vector", TS(gbA[:, i, :], vv[:], 0.0, -BIG, ALU.is_ge, ALU.mult), reads=["vv"], writes=["gconst"])
                    S_.add("vector", TS(ownb[:, i, :], vv[:], 0.0, 1.0, ALU.is_equal, ALU.subtract), reads=["vv"], writes=["gconst"])
                    S_.add("vector", TS(ownb[:, i, :], ownb[:, i, :], BIG, None, ALU.mult), reads=["gconst"], writes=["gconst"])
                    S_.add("vector", TS(futb[:, i, :], vv[:], 0.0, -BIG, ALU.is_gt, ALU.mult), reads=["vv"], writes=["gconst"])
                kt_sb = sb("kt_sb", [128, S], BF16)
                vx = sb("vx", [128, NT, 129], BF16)
                qt_sb = sb("qt_sb", [128, TOWN], BF16)
                S_.add("vector", MS(vx[:, :, 128:129], 1.0), writes=["vx1"])
                km = sb("km", [128, NB], F32)
                kmT = sb("kmT", [128, NB], BF16)
                gm = sb("gm", [128, NB], F32)
                mx8 = sb("mx8", [128, 8], F32)
                sel = sb("sel", [128, NB], F32)
                fb = sb("fb", [128, NB], BF16)
                biasT = sb("biasT", [128, 256], BF16)
                S_.add("vector", MS(biasT[:], 0.0), writes=[("biasT", 0), ("biasT", 1)])
                pT = [sb(f"pT{i}", [128, 256], BF16) for i in range(4)]
                rinv = sb("rinv", [128, 2], F32)
                obf = sb("obf", [128, 128], BF16)
                oT = [sb(f"oT{i}", [128, 256], BF16) for i in range(2)]
                pob = [ps(f"po{i}", [128, 512], F32) for i in range(2)]
                pstb = [ps(f"pst{i}", [128, 512], F32) for i in range(4)]
                pst = [pstb[i][:, 0:256] for i in range(4)]
                pg = ps("pg", [128, 512], F32)
                ptb = ps("ptb", [128, 8, 128], BF16)
                VSv = VS.rearrange("(t p) n -> p t n", p=128)
                npst = 0
                nkv = 8 if NT >= 64 else 4
                for h in range(8):
                    nkp = 4
                    for j in range(nkp):
                        a, b_ = j * (S // nkp), (j + 1) * (S // nkp)
                        S_.dma("sync", kt_sb[:, a:b_], KT[h][:, a:b_], writes=[("kt", j)], key=("kt", j))
                    for j in range(nkv):
                        a, b_ = j * (NT // nkv), (j + 1) * (NT // nkv)
                        S_.dma("sync", vx[:, a:b_, 0:128], VSv[:, a:b_, h * 128:(h + 1) * 128], writes=[("vx", j)], key=("vx", j))
                    S_.dma("sync", qt_sb[:], QT[h][:, :], writes=["qt"], key="qt")
                    ktk = [("kt", j) for j in range(nkp)]

                    S_.add("vector", lambda e: e.tensor_reduce(out=km[:], in_=kt_sb[:].rearrange("p (n k) -> p n k", k=256), axis=AX.X, op=ALU.add),
                           reads=ktk, writes=["km"])
                    S_.add("vector", TS(kmT[:], km[:], 1.0 / 256.0, None, ALU.mult), reads=["km"], writes=["kmT"])
                    for i in range(NQB):
                        for qs in range(2):
                            q0 = i * 256 + qs * 128
                            S_.add("tensor", MM(pg[:, 0:NB], qt_sb[:, q0:q0 + 128], kmT[:, :], True, True), reads=["qt", "kmT"], writes=["pg"])
                            S_.add("vector", TT(gm[:], pg[:, 0:NB], gbA[:, i, :], ALU.add), reads=["pg", "gconst"], writes=["gm"])
                            S_.add("vector", lambda e: e.max(out=mx8[:], in_=gm[:]), reads=["gm"], writes=["mx8"])
                            S_.add("vector", TS(sel[:], gm[:], mx8[:, 2:3], 1.0, ALU.is_ge, ALU.subtract), reads=["gm", "mx8"], writes=["sel"])
                            S_.add("vector", STT(sel[:], sel[:], BIG, ownb[:, i, :], ALU.mult, ALU.max), reads=["sel", "gconst"], writes=["sel"])
                            S_.add("vector", TT(fb[:], sel[:], futb[:, i, :], ALU.min), reads=["sel", "gconst"], writes=["fb"])
                            S_.add("tensor", TR(ptb[0:NB, 0, :], fb[:, :], ident[:]), reads=["fb", "ident"], writes=["ptb"])
                            S_.add("scalar", ACT(biasT[0:NB, qs * 128:(qs + 1) * 128], ptb[0:NB, 0, :], AF.Copy), reads=["ptb"], writes=[("biasT", qs)])
                        nkt = 8 * i + 8
                        LA = 2
                        slots = {}
                        for step in range(nkt + LA):
                            if step < nkt:
                                kt = step
                                n = kt // 2
                                d = n - 4 * i
                                p = npst % 4
                                npst += 1
                                slots[kt] = p
                                S_.add("tensor", MM(pst[p], kt_sb[:, kt * 128:(kt + 1) * 128], qt_sb[:, i * 256:(i + 1) * 256], True, False),
                                       reads=[("kt", kt * 128 // (S // nkp)), "qt"], writes=[("pst", p)])
                                S_.add("tensor", MM(pst[p], esel[:, n, :], biasT[:, :], False, d < 0),
                                       reads=["esel", ("biasT", 0), ("biasT", 1)], writes=[("pst", p)])
                                if d >= 0:
                                    for qs in range(2):
                                        S_.add("tensor", MM(pst[p][:, qs * 128:(qs + 1) * 128], ident[:], pmask[:, d * 4 + (kt % 2) * 2 + qs, :],
                                                            False, qs == 1), reads=["ident", "pmask"], writes=[("pst", p)])
                                S_.add("scalar", ACT(pT[p][:], pst[p], AF.Exp, scale=SCALE), reads=[("pst", p)], writes=[("pT", p)])
                            if step >= LA:
                                kt = step - LA
                                p = slots[kt]
                                for qs in range(2):
                                    S_.add("tensor", MM(pob[qs][:, 0:129], pT[p][:, qs * 128:(qs + 1) * 128], vx[:, kt, :], kt == 0, kt == nkt - 1),
                                           reads=[("pT", p), ("vx", kt // (NT // nkv)), "vx1"], writes=[("po", qs)])
                        os_ = i % 2
                        for qs in range(2):
                            S_.add("vector", RCP(rinv[:, qs:qs + 1], pob[qs][:, 128:129]), reads=[("po", qs)], writes=[("rinv", qs)])
                            S_.add("vector", TS(obf[:], pob[qs][:, 0:128], rinv[:, qs:qs + 1], None, ALU.mult), reads=[("po", qs), ("rinv", qs)],
                                   writes=["obf"])
                            S_.add("tensor", TR(ptb[:, 1 + qs, :], obf[:], ident[:]), reads=["obf", "ident"], writes=[("ptbo", qs)])
                            S_.add("scalar", ACT(oT[os_][:, qs * 128:(qs + 1) * 128], ptb[:, 1 + qs, :], AF.Copy), reads=[("ptbo", qs)],
                                   writes=[("oT", os_, qs)])
                        S_.dma("sync", OT[h][:, i * 256:(i + 1) * 256], oT[os_][:], reads=[("oT", os_, 0), ("oT", os_, 1)], key=("oT", os_))
                S_.finish()

        def phase_diff():
            with ExitStack() as es:
                S_ = Sched(nc, pool, "CD")
                sb = lambda n, s, d: es.enter_context(nc.sbuf_tensor(S_.name + "_" + n, s, d))
                ps = lambda n, s, d: es.enter_context(nc.psum_tensor(S_.name + "_" + n, s, d))
                c = common_consts(S_, es, False, False)
                attn_consts(S_, es, c)
                ident = c["ident"]
                pmask = c["pmask"]
                lq = sb("lq", [128, 4, 128], F32)
                S_.dma("sync", lq[:], bass.AP(lam4.tensor, 0, [[0, 128], [128, 4], [1, 128]]), writes=["lq"], key="lq")
                lj = sb("lj", [128, 128], F32)
                ld = sb("ld", [128, 4], F32)
                S_.add("vector", TT(lj[:], lq[:, 0, :], lq[:, 1, :], ALU.mult), reads=["lq"], writes=["lj"])
                S_.add("vector", lambda e: e.tensor_reduce(out=ld[:, 0:1], in_=lj[:], axis=AX.X, op=ALU.add), reads=["lj"], writes=["ld0"])
                S_.add("vector", TT(lj[:], lq[:, 2, :], lq[:, 3, :], ALU.mult), reads=["lq", "ld0"], writes=["lj"])
                S_.add("vector", lambda e: e.tensor_reduce(out=ld[:, 1:2], in_=lj[:], axis=AX.X, op=ALU.add), reads=["lj"], writes=["ld1"])
                S_.add("scalar", ACT(ld[:, 2:4], ld[:, 0:2], AF.Exp), reads=["ld0", "ld1"], writes=["ld2"])
                nlam = sb("nlam", [128, 1], F32)
                S_.add("vector", TT(nlam[:], ld[:, 3:4], ld[:, 2:3], ALU.subtract), reads=["ld2"], writes=["nlam"])
                S_.add("vector", TS(nlam[:], nlam[:], -0.2, None, ALU.add), reads=["nlam"], writes=["nlam"])
                sg = sb("sg", [128, 256], F32)
                S_.dma("sync", sg[:], bcast_row(subg, 256), writes=["sg"], key="sg")
                S_.add("vector", TS(sg[:], sg[:], 0.8, None, ALU.mult), reads=["sg"], writes=["sg"])
                kt_sb = [sb(f"kt_sb{s}", [128, S], BF16) for s in range(2)]
                qt_sb = [sb(f"qt_sb{s}", [128, TOWN], BF16) for s in range(2)]
                vx = sb("vx", [128, NT, 257], BF16)
                S_.add("vector", MS(vx[:, :, 256:257], 1.0), writes=["vx1"])
                pT = [sb(f"pT{i}", [128, 256], BF16) for i in range(4)]
                rinv = sb("rinv", [128, 4], F32)
                tq = sb("tq", [128, 256], F32)
                oq = sb("oq", [128, 256], F32)
                st = sb("stt", [128, 4], F32)
                junk = sb("junk", [128, 256], BF16)
                obf = sb("obf", [128, 256], BF16)
                oT = [sb(f"oT{i}", [128, 2, 256], BF16) for i in range(2)]
                pob = [[ps(f"po{s}{q}", [128, 512], F32) for q in range(2)] for s in range(2)]
                pstb = [ps(f"pst{i}", [128, 512], F32) for i in range(3)]
                pst = [pstb[i][:, 0:256] for i in range(3)]
                ptb = ps("ptb", [128, 8, 128], BF16)
                VSv = VS.rearrange("(t p) n -> p t n", p=128)
                npst = 0
                nkp = 4
                nkv = 8 if NT >= 64 else 4
                for hd in range(4):
                    for s in range(2):
                        for j in range(nkp):
                            a, b_ = j * (S // nkp), (j + 1) * (S // nkp)
                            S_.dma("sync", kt_sb[s][:, a:b_], KT[8 + 2 * hd + s][:, a:b_], writes=[("kt", s, j)], key=("kt", s, j))
                        S_.dma("sync", qt_sb[s][:], QT[8 + 2 * hd + s][:, :], writes=[("qt", s)], key=("qt", s))
                    for j in range(nkv):
                        a, b_ = j * (NT // nkv), (j + 1) * (NT // nkv)
                        S_.dma("sync", vx[:, a:b_, 0:256], VSv[:, a:b_, 1024 + hd * 256:1024 + (hd + 1) * 256], writes=[("vx", j)], key=("vx", j))
                    for i in range(NQB):
                        nkt = 8 * i + 8
                        LA = 2
                        slots = {}
                        nsteps = nkt * 2
                        for step in range(nsteps + LA):
                            if step < nsteps:
                                kt, s = step // 2, step % 2
                                d = kt // 2 - 4 * i
                                p = npst % 3
                                tq_ = npst % 4
                                npst += 1
                                slots[step] = tq_
                                S_.add("tensor", MM(pst[p], kt_sb[s][:, kt * 128:(kt + 1) * 128], qt_sb[s][:, i * 256:(i + 1) * 256], True, d < 0),
                                       reads=[("kt", s, kt * 128 // (S // nkp)), ("qt", s)], writes=[("pst", p)])
                                if d >= 0:
                                    for qs in range(2):
                                        S_.add("tensor", MM(pst[p][:, qs * 128:(qs + 1) * 128], ident[:], pmask[:, d * 4 + (kt % 2) * 2 + qs, :],
                                                            False, qs == 1), reads=["ident", "pmask"], writes=[("pst", p)])
                                S_.add("scalar", ACT(pT[tq_][:], pst[p], AF.Exp, scale=SCALE), reads=[("pst", p)], writes=[("pT", tq_)])
                            if step >= LA:
                                st2 = step - LA
                                kt, s = st2 // 2, st2 % 2
                                p = slots[st2]
                                for qs in range(2):
                                    S_.add("tensor", MM(pob[s][qs][:, 0:257], pT[p][:, qs * 128:(qs + 1) * 128], vx[:, kt, :], kt == 0, kt == nkt - 1),
                                           reads=[("pT", p), ("vx", kt // (NT // nkv)), "vx1"], writes=[("po", s, qs)])
                        os_ = i % 2
                        for qs in range(2):
                            S_.add("vector", RCP(rinv[:, 0:1], pob[0][qs][:, 256:257]), reads=[("po", 0, qs)], writes=["rinv0"])
                            S_.add("vector", RCP(rinv[:, 1:2], pob[1][qs][:, 256:257]), reads=[("po", 1, qs)], writes=["rinv1"])
                            S_.add("vector", TS(rinv[:, 2:3], rinv[:, 1:2], nlam[:, 0:1], None, ALU.mult), reads=["rinv1", "nlam"], writes=["rinv2"])
                            S_.add("vector", TS(tq[:], pob[1][qs][:, 0:256], rinv[:, 2:3], None, ALU.mult), reads=[("po", 1, qs), "rinv2"], writes=["tq"])
                            S_.add("vector", STT(oq[:], pob[0][qs][:, 0:256], rinv[:, 0:1], tq[:], ALU.mult, ALU.add),
                                   reads=[("po", 0, qs), "rinv0", "tq"], writes=["oq"])
                            norm_tile(S_, c, oq[:], "oq", sg[:], "sg", obf[:], "obf", st, "stt", nfeat=256)
                            for cc in range(2):
                                S_.add("tensor", TR(ptb[:, qs * 2 + cc, :], obf[:, cc * 128:(cc + 1) * 128], ident[:]), reads=["obf", "ident"],
                                       writes=[("ptbo", qs, cc)])
                                S_.add("scalar", ACT(oT[os_][:, cc, qs * 128:(qs + 1) * 128], ptb[:, qs * 2 + cc, :], AF.Copy),
                                       reads=[("ptbo", qs, cc)], writes=[("oT", os_, qs, cc)])
                        for cc in range(2):
                            S_.dma("sync", OT[8 + hd * 2 + cc][:, i * 256:(i + 1) * 256], oT[os_][:, cc, :],
                                   reads=[("oT", os_, 0, cc), ("oT", os_, 1, cc)], key=("oT", os_, cc))
                S_.finish()

        def phase_dense():
            with ExitStack() as es:
                S_ = Sched(nc, pool, "DD")
                sb = lambda n, s, d: es.enter_context(nc.sbuf_tensor(S_.name + "_" + n, s, d))
                ps = lambda n, s, d: es.enter_context(nc.psum_tensor(S_.name + "_" + n, s, d))
                c = common_consts(S_, es, False, False)
                R = sb("R", [128, 64, 512], BF16)
                yall = sb("yall", [128, 4, D], F32)
                uT = sb("uT", [128, NCH, 512], BF16)
                wb = [sb(f"wb{i}", [128, NCH, 512], BF16) for i in range(3)]
                ht = [sb(f"ht{i}", [128, D], F32) for i in range(2)]
                ub = sb("ub", [128, D], BF16)
                grep = sb("grep", [128, D], F32)
                st = [sb(f"st{i}", [128, 4], F32) for i in range(2)]
                rl = [sb(f"rl{i}", [128, 512], F32) for i in range(2)]
                pt_f = sb("pt_f", [128, 256], F32)
                pt_b = sb("pt_b", [128, 256], BF16)
                pTt = sb("pTt", [128, 2, 512], BF16)
                sgt = sb("sgt", [128, 512], F32)
                ptr = [ps(f"ptr{i}", [128, 8, 128], BF16) for i in range(2)]
                pk = [ps(f"pk{i}", [128, 512], F32) for i in range(6)]
                H1 = dscr("H1", [TOWN, D], F32)
                H2 = dscr("H2", [TOWN, D], F32)
                wcnt = [0]
                pkc = [0]

                def load_w(src_view, nchunks):
                    sl = wcnt[0] % 3
                    wcnt[0] += 1
                    S_.dma("gpsimd", wb[sl][:, 0:nchunks, :], src_view, writes=[("wb", sl)], key=("wb", sl))
                    return wb[sl], ("wb", sl)

                def load_g(idx):
                    S_.dma("sync", grep[:], bcast_row(gains, D, off=idx * D), writes=["grep"], key="grep")

                def next_pk():
                    p = pkc[0] % 6
                    pkc[0] += 1
                    return p

                def tok_major_layer(lhs_fn, lhs_keys, nk, wsrc_fn, evac_fn):
                    for cg in range(4):
                        nsl = (nk + NCH - 1) // NCH
                        if nsl == 1:
                            wt, wk = load_w(wsrc_fn(cg, 0, nk), nk)
                            for tt in range(4):
                                p = next_pk()
                                for k in range(nk):
                                    S_.add("tensor", MM(pk[p][:], lhs_fn(tt, k), wt[:, k, :], k == 0, k == nk - 1),
                                           reads=lhs_keys(tt) + [wk] if k in (0, nk - 1) else (), writes=[("pk", p)])
                                evac_fn(tt, cg, p)
                        else:
                            pp_ = [next_pk() for _ in range(4)]
                            for sl in range(nsl):
                                wt, wk = load_w(wsrc_fn(cg, sl * NCH, NCH), NCH)
                                for tt in range(4):
                                    for k in range(NCH):
                                        kk = sl * NCH + k
                                        S_.add("tensor", MM(pk[pp_[tt]][:], lhs_fn(tt, kk), wt[:, k, :], kk == 0, kk == nk - 1),
                                               reads=lhs_keys(tt) + [wk] if k in (0, NCH - 1) else (), writes=[("pk", pp_[tt])])
                            for tt in range(4):
                                evac_fn(tt, cg, pp_[tt])

                def post_norm_residual(G, hsrc, hdst, stage):
                    for tt in range(4):
                        t = G * 4 + tt
                        hs = tt % 2
                        S_.dma("sync", ht[hs][:], hsrc[t * 128:(t + 1) * 128, :], writes=[("ht", hs)], key=("ht", hs))
                        s_ = st[hs]
                        S_.add("scalar", ACT(ub[:], yall[:, tt, :], AF.Square, accum_out=s_[:, 0:1]), reads=[("yall", tt)], writes=["ub", ("st", hs)])
                        S_.add("scalar", ACT(s_[:, 1:2], s_[:, 0:1], AF.Sqrt, bias=c["eps"][:, 0:1], scale=1.0 / D), reads=[("st", hs), "epst"],
                               writes=[("st", hs)])
                        S_.add("vector", RCP(s_[:, 2:3], s_[:, 1:2]), reads=[("st", hs)], writes=[("st", hs)])
                        S_.add("vector", STT(yall[:, tt, :], yall[:, tt, :], s_[:, 2:3], grep[:], ALU.mult, ALU.mult),
                               reads=[("yall", tt), ("st", hs), "grep"], writes=[("yall", tt)])
                        S_.add("vector", TT(ht[hs][:], ht[hs][:], yall[:, tt, :], ALU.add), reads=[("ht", hs), ("yall", tt)], writes=[("ht", hs)])
                        yield tt, t, hs

                def prep_from_h(tt, hs, gidx_loaded):
                    norm_tile(S_, c, ht[hs][:], ("ht", hs), grep[:], "grep", ub[:], "ub", st[hs], ("st", hs))
                    transpose_tile(S_, c, ub, "ub", NCH, ptr, [("ptr", 0), ("ptr", 1)],
                                   lambda hf, cn, tt=tt: uT[:, hf * 8:hf * 8 + cn, tt * 128:(tt + 1) * 128], [("uT", tt)])

                uTk = [("uT", tt) for tt in range(4)]
                wg_v = w_g.rearrange("(c p) n -> p c n", p=128)
                wbm_v = w_bm.rearrange("(c p) n -> p c n", p=128)
                wbd_v = w_bd.rearrange("(c p) n -> p c n", p=128)
                wout_v = w_out.rearrange("(c p) n -> p c n", p=128)
                wup_v = w_up.rearrange("(c p) n -> p c n", p=128)
                wdn_v = w_down.rearrange("(c p) n -> p c n", p=128)
                wpg_v = w_pg.rearrange("(c p) n -> p c n", p=128)
                wpp_v = w_pp.rearrange("(c p) n -> p c n", p=128)
                for G in range(NGO):
                    tok = slice(G * 512, (G + 1) * 512)
                    load_g(0)
                    for tt in range(4):
                        t = G * 4 + tt
                        hs = tt % 2
                        S_.dma("sync", ht[hs][:], xo[t * 128:(t + 1) * 128, :], writes=[("ht", hs)], key=("ht", hs))
                        prep_from_h(tt, hs, 0)
                    for cg in range(8):
                        wt, wk = load_w(wg_v[:, :, cg * 512:(cg + 1) * 512], NCH)
                        for jj in range(4):
                            j = cg * 4 + jj
                            p = next_pk()
                            for k in range(NCH):
                                S_.add("tensor", MM(pk[p][:], wt[:, k, jj * 128:(jj + 1) * 128], uT[:, k, :], k == 0, k == NCH - 1),
                                       reads=uTk + [wk] if k in (0, NCH - 1) else (), writes=[("pk", p)])
                            S_.add("scalar", ACT(R[:, j, :], pk[p][:], AF.Sigmoid), reads=[("pk", p)], writes=[("R", j)])
                    S_.dma("sync", R[:, 48:64, :], OT[:, :, tok].rearrange("c p t -> p c t"), writes=[("R", 48 + k) for k in range(16)], key="oab")
                    for cg in range(4):
                        wa, wak = load_w(wbm_v[:, :, cg * 512:(cg + 1) * 512], 8)
                        wd, wdk = load_w(wbd_v[:, :, cg * 512:(cg + 1) * 512], 8)
                        for jj in range(4):
                            j = cg * 4 + jj
                            pa = next_pk()
                            pb = next_pk()
                            for k in range(8):
                                S_.add("tensor", MM(pk[pa][:], wa[:, k, jj * 128:(jj + 1) * 128], R[:, 48 + k, :], k == 0, k == 7),
                                       reads=[("R", 48 + kk) for kk in range(8)] + [wak] if k in (0, 7) else (), writes=[("pk", pa)])
                            for k in range(8):
                                S_.add("tensor", MM(pk[pb][:], wd[:, k, jj * 128:(jj + 1) * 128], R[:, 56 + k, :], k == 0, k == 7),
                                       reads=[("R", 56 + kk) for kk in range(8)] + [wdk] if k in (0, 7) else (), writes=[("pk", pb)])
                            r_ = rl[j % 2]
                            S_.add("vector", TT(r_[:], pk[pa][:], R[:, j, :], ALU.mult), reads=[("pk", pa), ("R", j)], writes=[("rl", j % 2)])
                            S_.add("vector", TT(sgt[:], pk[pb][:], R[:, 16 + j, :], ALU.mult), reads=[("pk", pb), ("R", 16 + j)], writes=["sgt"])
                            S_.add("vector", TT(R[:, 32 + j, :], r_[:], sgt[:], ALU.add), reads=[("rl", j % 2), "sgt"], writes=[("R", 32 + j)])
                    load_g(1)

                    def evac_y(tt, cg, p):
                        if cg % 2 == 0:
                            S_.add("scalar", ACT(yall[:, tt, cg * 512:(cg + 1) * 512], pk[p][:], AF.Copy), reads=[("pk", p)], writes=[("yall", tt)])
                        else:
                            S_.add("vector", CP(yall[:, tt, cg * 512:(cg + 1) * 512], pk[p][:]), reads=[("pk", p)], writes=[("yall", tt)])

                    tok_major_layer(lambda tt, k: R[:, 32 + k, tt * 128:(tt + 1) * 128], lambda tt: [("R", 32 + k) for k in range(16)], NCH,
                                    lambda cg, k0, nk: wout_v[:, k0:k0 + nk, cg * 512:(cg + 1) * 512], evac_y)
                    hts = []
                    for tt, t, hs in post_norm_residual(G, xo, None, 1):
                        S_.dma("sync", H1[t * 128:(t + 1) * 128, :], ht[hs][:], reads=[("ht", hs)], writes=[("H1", t)], key=("h1s", hs))
                        hts.append((tt, hs))
                        if tt == 0:
                            pass
                    load_g(2)
                    for tt in range(4):
                        t = G * 4 + tt
                        hs = tt % 2
                        S_.dma("sync", ht[hs][:], H1[t * 128:(t + 1) * 128, :], reads=[("H1", t)], writes=[("ht", hs)], key=("ht", hs))
                        prep_from_h(tt, hs, 2)
                    for cg in range(16):
                        wt, wk = load_w(wup_v[:, :, cg * 512:(cg + 1) * 512], NCH)
                        for jj in range(4):
                            j = cg * 4 + jj
                            p = next_pk()
                            for k in range(NCH):
                                S_.add("tensor", MM(pk[p][:], wt[:, k, jj * 128:(jj + 1) * 128], uT[:, k, :], k == 0, k == NCH - 1),
                                       reads=uTk + [wk] if k in (0, NCH - 1) else (), writes=[("pk", p)])
                            r_ = rl[j % 2]
                            S_.add("scalar", ACT(r_[:], pk[p][:], AF.Relu), reads=[("pk", p)], writes=[("rl", j % 2)])
                            S_.add("vector", STT(R[:, j, :], pk[p][:], 0.0, r_[:], ALU.max, ALU.mult), reads=[("pk", p), ("rl", j % 2)],
                                   writes=[("R", j)])
                    load_g(3)
                    tok_major_layer(lambda tt, k: R[:, k, tt * 128:(tt + 1) * 128], lambda tt: [("R", k) for k in range(64)], 64,
                                    lambda cg, k0, nk: wdn_v[:, k0:k0 + nk, cg * 512:(cg + 1) * 512], evac_y)
                    for tt, t, hs in post_norm_residual(G, H1, None, 3):
                        S_.dma("sync", H2[t * 128:(t + 1) * 128, :], ht[hs][:], reads=[("ht", hs)], writes=[("H2", t)], key=("h2s", hs))
                    load_g(4)
                    for tt in range(4):
                        t = G * 4 + tt
                        hs = tt % 2
                        S_.dma("sync", ht[hs][:], H2[t * 128:(t + 1) * 128, :], reads=[("H2", t)], writes=[("ht", hs)], key=("ht", hs))
                        prep_from_h(tt, hs, 4)
                        S_.dma("sync", pt_f[:], po[t * 128:(t + 1) * 128, :], writes=["pt_f"], key="pt_f")
                        S_.add("vector", CP(pt_b[:], pt_f[:]), reads=["pt_f"], writes=["pt_b"])
                        transpose_tile(S_, c, pt_b, "pt_b", 2, ptr, [("ptr", 0), ("ptr", 1)],
                                       lambda hf, cn, tt=tt: pTt[:, 0:cn, tt * 128:(tt + 1) * 128], [("pTt", tt)])
                    load_g(5)
                    for cg in range(4):
                        wt, wk = load_w(wpg_v[:, :, cg * 512:(cg + 1) * 512], NCH)
                        wp, wpk = load_w(wpp_v[:, :, cg * 512:(cg + 1) * 512], 2)
                        for tt in range(4):
                            pa = next_pk()
                            pb = next_pk()
                            for k in range(NCH):
                                S_.add("tensor", MM(pk[pa][:], uT[:, k, tt * 128:(tt + 1) * 128], wt[:, k, :], k == 0, k == NCH - 1),
                                       reads=[("uT", tt), wk] if k in (0, NCH - 1) else (), writes=[("pk", pa)])
                            for k in range(2):
                                S_.add("tensor", MM(pk[pb][:], pTt[:, k, tt * 128:(tt + 1) * 128], wp[:, k, :], k == 0, k == 1),
                                       reads=[("pTt", tt), wpk] if k in (0, 1) else (), writes=[("pk", pb)])
                            S_.add("scalar", ACT(sgt[:], pk[pa][:], AF.Sigmoid), reads=[("pk", pa)], writes=["sgt"])
                            S_.add("vector", TT(yall[:, tt, cg * 512:(cg + 1) * 512], pk[pb][:], sgt[:], ALU.mult), reads=[("pk", pb), "sgt"],
                                   writes=[("yall", tt)])
                    for tt, t, hs in post_norm_residual(G, H2, None, 5):
                        S_.dma("sync", out[t * 128:(t + 1) * 128, :], ht[hs][:], reads=[("ht", hs)], key=("outs", hs))
                S_.finish()

        phase_proj("AK", "KQ", xb, NT, w_k, KT)
        phase_proj("AV", "V", xb, NT, w_v, VS)
        phase_proj("B", "KQ", xo, NTO, w_q, QT)
        phase_moba()
        phase_diff()
        phase_dense()
    return nc


def _prep_inputs(S, x, p, w_in, w_br_moba, w_br_diff, w_out, lambda_q1, lambda_k1, lambda_q2, lambda_k2,
                 diff_subln_g, g_mix_pre, g_mix_post, w_up, w_down, g_mlp_pre, g_mlp_post,
                 w_ple_proj, w_ple_gate, g_ple_pre, g_ple_post):
    f = lambda a: np.ascontiguousarray(np.asarray(a, dtype=np.float32))
    x = np.asarray(x, dtype=np.float32)
    p = np.asarray(p, dtype=np.float32)
    w_in = np.asarray(w_in, dtype=np.float32)[0]
    NB = S // 256
    wq = f(np.concatenate([w_in[:, 0:1024], w_in[:, 3072:4096]], axis=1))
    wk = f(np.concatenate([w_in[:, 1024:2048], w_in[:, 4096:5120]], axis=1))
    wv = f(np.concatenate([w_in[:, 2048:3072], w_in[:, 5120:6144]], axis=1))
    wg = f(w_in[:, 6144:10240])
    shared = {
        "w_k": wk, "w_v": wv, "w_q": wq, "w_g": wg,
        "w_bm": f(w_br_moba[0]), "w_bd": f(w_br_diff[0]), "w_out": f(w_out[0]),
        "w_up": f(w_up[0]), "w_down": f(w_down[0]), "w_pp": f(w_ple_proj[0]), "w_pg": f(w_ple_gate[0]),
        "lam4": f(np.stack([np.asarray(lambda_q1)[0], np.asarray(lambda_k1)[0], np.asarray(lambda_q2)[0], np.asarray(lambda_k2)[0]])),
        "subg": f(np.asarray(diff_subln_g)[0:1]),
        "gains": f(np.stack([np.asarray(a)[0] for a in (g_mix_pre, g_mix_post, g_mlp_pre, g_mlp_post, g_ple_pre, g_ple_post)])),
    }
    in_maps = []
    for c in range(8):
        b, g = c // 4, c % 4
        xbb = f(x[b])
        xo = f(xbb.reshape(NB, 256, D)[g::4].reshape(-1, D))
        pob = f(p[0, b].reshape(NB, 256, 256)[g::4].reshape(-1, 256))
        m = dict(shared)
        m["xb"] = xbb
        m["xo"] = xo
        m["po"] = pob
        m["cinfo"] = np.array([[g, 256 * g, 0, 0]], dtype=np.float32)
        in_maps.append(m)
    return in_maps


_NC_CACHE = {}


def kernel(**inputs):
    S = int(np.asarray(inputs["x"]).shape[1])
    if S not in _NC_CACHE:
        _NC_CACHE[S] = build(S)
    nc = _NC_CACHE[S]
    in_maps = _prep_inputs(S, **inputs)
    res = run_bass_kernel_spmd(nc, in_maps, core_ids=list(range(8)))
    _LAST_RES[0] = res
    NB = S // 256
    outp = np.empty((2, S, D), dtype=np.float32)
    for c in range(8):
        b, g = c // 4, c % 4
        o = np.asarray(res.results[c]["out"], dtype=np.float32).reshape(NB // 4, 256, D)
        outp[b].reshape(NB, 256, D)[g::4] = o
    return outp
```

```python
import math
import numpy as np
from contextlib import ExitStack
import concourse.bass as bass
import concourse.mybir as mybir
from concourse.bass_utils import run_bass_kernel_spmd

F32 = mybir.dt.float32
BF16 = mybir.dt.bfloat16
I32 = mybir.dt.int32
AF = mybir.ActivationFunctionType
ALU = mybir.AluOpType
AX = mybir.AxisListType

SEQ = 16384
DEBUG_DUMP = False
_LAST_RES = [None]
D = 2048
NCH = 16
EPS = 1e-6
BIG = 30000.0
ROPE_THETA = 500000.0
ENGS = ("tensor", "vector", "scalar", "gpsimd", "sync")
EPOCH = 30000


class Op:
    __slots__ = ("eng", "emit", "deps", "dkey", "sig", "has_cons")

    def __init__(self, eng, emit, dkey):
        self.eng = eng
        self.emit = emit
        self.deps = set()
        self.dkey = dkey
        self.sig = None
        self.has_cons = False


class SemPool:
    def __init__(self, nc, es):
        self.nc = nc
        self.es = es
        self.eng_sems = {e: [] for e in ENGS}
        self.eng_cnt = {e: 0 for e in ENGS}
        self.dma_sems = {}
        self.dma_cnt = {}
        self.n = 0

    def _new(self):
        self.n += 1
        return self.es.enter_context(self.nc.semaphore(f"sm{self.n}"))

    def next_eng_sig(self, e):
        c = self.eng_cnt[e]
        ep, v = divmod(c, EPOCH)
        if ep >= len(self.eng_sems[e]):
            self.eng_sems[e].append(self._new())
        self.eng_cnt[e] = c + 1
        return (self.eng_sems[e][ep], v + 1)

    def next_dma_sig(self, key):
        c = self.dma_cnt.get(key, 0)
        ep, v = divmod(c, EPOCH // 16)
        lst = self.dma_sems.setdefault(key, [])
        if ep >= len(lst):
            lst.append(self._new())
        self.dma_cnt[key] = c + 1
        return (lst[ep], 16 * (v + 1))


class Sched:
    def __init__(self, nc, pool, name):
        self.nc = nc
        self.pool = pool
        self.name = name
        self.ops = []
        self.last_w = {}
        self.readers = {}

    def add(self, eng, emit, reads=(), writes=(), dma=None):
        op = Op(eng, emit, dma)
        deps = op.deps
        lw = self.last_w
        rd = self.readers
        for k in reads:
            w = lw.get(k)
            if w is not None:
                deps.add(w)
        for k in writes:
            w = lw.get(k)
            if w is not None:
                deps.add(w)
            r = rd.get(k)
            if r:
                deps.update(r)
        for k in reads:
            rd.setdefault(k, []).append(op)
        for k in writes:
            lw[k] = op
            rd[k] = []
        deps.discard(op)
        if eng == "tensor":
            op.deps = {d for d in deps if not (d.eng == "tensor" and d.dkey is None)}
        for d in op.deps:
            d.has_cons = True
        self.ops.append(op)
        return op

    def dma(self, q, out, in_, reads=(), writes=(), key=None, **kw):
        return self.add(q, lambda e: e.dma_start(out=out, in_=in_, **kw), reads, writes, dma=key)

    def finish(self):
        last_dma = {}
        for op in self.ops:
            if op.dkey is not None:
                last_dma[op.dkey] = op
        fin = Op("sync", None, None)
        fin.deps = set(last_dma.values())
        for d in fin.deps:
            d.has_cons = True
        self.ops.append(fin)
        pool = self.pool
        for op in self.ops:
            if op.dkey is not None:
                op.sig = pool.next_dma_sig(op.dkey)
            elif op.has_cons and op.emit is not None:
                op.sig = pool.next_eng_sig(op.eng)
        per_eng = {e: [] for e in ENGS}
        for op in self.ops:
            per_eng[op.eng].append(op)
        with self.nc.Block(self.name) as block:
            for e in ENGS:
                lst = per_eng[e]
                if not lst:
                    continue

                def body(eng, lst=lst):
                    seen = {}
                    for op in lst:
                        waits = {}
                        for d in op.deps:
                            if d.sig is None:
                                continue
                            s, v = d.sig
                            k = id(s)
                            if seen.get(k, 0) >= v:
                                continue
                            if k not in waits or waits[k][1] < v:
                                waits[k] = (s, v)
                        for k, (s, v) in waits.items():
                            eng.wait_ge(s, v)
                            seen[k] = v
                        if op.emit is None:
                            continue
                        ins = op.emit(eng)
                        if op.sig is not None:
                            ins.then_inc(op.sig[0], 16 if op.dkey is not None else 1)

                getattr(block, e)(body)


def MM(out, lhsT, rhs, start, stop):
    return lambda e: e.matmul(out, lhsT=lhsT, rhs=rhs, start=start, stop=stop)


def TR(out, in_, ident):
    return lambda e: e.transpose(out, in_, ident)


def ACT(out, in_, func, bias=None, scale=None, accum_out=None):
    kw = {}
    if bias is not None:
        kw["bias"] = bias
    if scale is not None:
        kw["scale"] = scale
    if accum_out is not None:
        kw["accum_out"] = accum_out
    return lambda e: e.activation(out=out, in_=in_, func=func, **kw)


def TS(out, in0, s1, s2, op0, op1=None):
    if op1 is None:
        return lambda e: e.tensor_scalar(out=out, in0=in0, scalar1=s1, scalar2=None, op0=op0)
    return lambda e: e.tensor_scalar(out=out, in0=in0, scalar1=s1, scalar2=s2, op0=op0, op1=op1)


def TT(out, in0, in1, op):
    return lambda e: e.tensor_tensor(out=out, in0=in0, in1=in1, op=op)


def STT(out, in0, scalar, in1, op0, op1):
    return lambda e: e.scalar_tensor_tensor(out=out, in0=in0, scalar=scalar, in1=in1, op0=op0, op1=op1)


def CP(out, in_):
    return lambda e: e.tensor_copy(out=out, in_=in_)


def RCP(out, in_):
    return lambda e: e.reciprocal(out=out, in_=in_)


def MS(ap, v):
    return lambda e: e.memset(ap, v)


def bcast_row(handle_ap, n, parts=128, off=0):
    return bass.AP(handle_ap.tensor, off, [[0, parts], [1, n]])


def build(S):
    NT = S // 128
    NB = S // 256
    NQB = NB // 4
    TOWN = S // 4
    NTO = TOWN // 128
    NGA = S // 512
    NGO = TOWN // 512
    SCALE = 1.0 / math.sqrt(128.0)

    nc = bass.Bass("TRN2", target_bir_lowering=False)
    din = lambda n, s: nc.dram_tensor(n, s, F32, kind="ExternalInput").ap()
    xb = din("xb", [S, D])
    xo = din("xo", [TOWN, D])
    po = din("po", [TOWN, 256])
    w_k = din("w_k", [D, 2048])
    w_v = din("w_v", [D, 2048])
    w_q = din("w_q", [D, 2048])
    w_g = din("w_g", [D, 4096])
    w_bm = din("w_bm", [1024, D])
    w_bd = din("w_bd", [1024, D])
    w_out = din("w_out", [D, D])
    w_up = din("w_up", [D, 8192])
    w_down = din("w_down", [8192, D])
    w_pp = din("w_pp", [256, D])
    w_pg = din("w_pg", [D, D])
    lam4 = din("lam4", [4, 128])
    subg = din("subg", [1, 256])
    gains = din("gains", [6, D])
    cinfo = din("cinfo", [1, 4])
    out = nc.dram_tensor("out", [TOWN, D], F32, kind="ExternalOutput").ap()

    dscr = lambda n, s, dt: nc.dram_tensor(n, s, dt, kind="Internal").ap()
    if DEBUG_DUMP:
        dscr = lambda n, s, dt: nc.dram_tensor(n, s, dt, kind="ExternalOutput" if n in ("KT", "VS", "QT", "OT") else "Internal").ap()
    KT = dscr("KT", [16, 128, S], BF16)
    VS = dscr("VS", [S, 2048], BF16)
    QT = dscr("QT", [16, 128, TOWN], BF16)
    OT = dscr("OT", [16, 128, TOWN], BF16)

    with ExitStack() as ges:
        pool = SemPool(nc, ges)

        def common_consts(S_, es, need_rope, own):
            sb = lambda n, s, d: es.enter_context(nc.sbuf_tensor(S_.name + "_" + n, s, d))
            c = {}
            ident = sb("ident", [128, 128], BF16)
            ones_bf = sb("ones_bf", [128, 128], BF16)
            S_.add("vector", MS(ones_bf[:], 1.0), writes=["ones_bf"])
            S_.add("gpsimd", lambda e: e.affine_select(out=ident[:], in_=ones_bf[:], pattern=[[-1, 128]],
                                                       compare_op=ALU.is_equal, fill=0.0, base=0, channel_multiplier=1),
                   reads=["ones_bf"], writes=["ident"])
            c["ident"] = ident
            epst = sb("epst", [128, 1], F32)
            S_.add("vector", MS(epst[:], EPS), writes=["epst"])
            c["eps"] = epst
            ci = sb("ci", [128, 4], F32)
            S_.dma("sync", ci[:], bcast_row(cinfo, 4), writes=["ci"], key="ci")
            c["ci"] = ci
            if not need_rope:
                return c
            pswA = sb("pswA", [32, 32], F32)
            pswB = sb("pswB", [32, 32], F32)
            psw = sb("psw", [32, 32], F32)
            ones_f = sb("ones_f", [32, 32], F32)
            S_.add("vector", MS(ones_f[:], 1.0), writes=["ones_f"])
            for t, base in ((pswA, -16), (pswB, 16)):
                S_.add("gpsimd", lambda e, t=t, base=base: e.affine_select(
                    out=t[:], in_=ones_f[:], pattern=[[-1, 32]], compare_op=ALU.is_equal, fill=0.0,
                    base=base, channel_multiplier=1), reads=["ones_f"], writes=[id(t)])
            S_.add("vector", TT(psw[:], pswA[:], pswB[:], ALU.add), reads=[id(pswA), id(pswB)], writes=["psw"])
            c["psw"] = psw
            ri = sb("ri", [32, 1], I32)
            rf = sb("rf", [32, 1], F32)
            r16 = sb("r16", [32, 1], F32)
            tmp = sb("rtmp", [32, 1], F32)
            invf = sb("invf", [32, 1], F32)
            sgn = sb("sgn", [32, 1], F32)
            S_.add("gpsimd", lambda e: e.iota(ri[:], pattern=[[0, 1]], base=0, channel_multiplier=1), writes=["ri"])
            S_.add("vector", CP(rf[:], ri[:]), reads=["ri"], writes=["rf"])
            S_.add("vector", TS(tmp[:], rf[:], 16.0, -16.0, ALU.is_ge, ALU.mult), reads=["rf"], writes=["rtmp"])
            S_.add("vector", TT(r16[:], rf[:], tmp[:], ALU.add), reads=["rf", "rtmp"], writes=["r16"])
            S_.add("vector", TS(sgn[:], rf[:], 16.0, 2.0, ALU.is_ge, ALU.mult), reads=["rf"], writes=["sgn"])
            S_.add("vector", TS(sgn[:], sgn[:], -1.0, None, ALU.add), reads=["sgn"], writes=["sgn"])
            S_.add("vector", MS(invf[:], 0.0), writes=["invf"])
            for j in range(16):
                cj = float(np.float32(1.0) / (np.float32(ROPE_THETA) ** (np.float32(j * 2.0) / np.float32(32.0))))
                S_.add("vector", TS(tmp[:], r16[:], float(j), cj, ALU.is_equal, ALU.mult), reads=["r16", "rtmp"], writes=["rtmp"])
                S_.add("vector", TT(invf[:], invf[:], tmp[:], ALU.add), reads=["invf", "rtmp"], writes=["invf"])
            c["invf"] = invf
            c["sgn"] = sgn
            ioi = sb("ioi", [32, 512], I32)
            iof = sb("iof", [32, 512], F32)
            if own:
                S_.add("gpsimd", lambda e: e.iota(ioi[:], pattern=[[1024, 2], [1, 256]], base=0, channel_multiplier=0), writes=["ioi"])
                S_.add("vector", CP(iof[:], ioi[:]), reads=["ioi"], writes=["iof"])
                S_.add("vector", TS(iof[:], iof[:], ci[0:32, 1:2], None, ALU.add), reads=["iof", "ci"], writes=["iof"])
            else:
                S_.add("gpsimd", lambda e: e.iota(ioi[:], pattern=[[1, 512]], base=0, channel_multiplier=0), writes=["ioi"])
                S_.add("vector", CP(iof[:], ioi[:]), reads=["ioi"], writes=["iof"])
            c["iof"] = iof
            return c

        def rope_tables(S_, rt, c, base, slot):
            ang, ki, kf, red, sa, ca, cosT, sinT = rt
            k = ("rt", slot)
            TWO_PI = 2.0 * math.pi
            c1 = 6.28125
            c2 = float(np.float32(TWO_PI - c1))
            c3 = float(np.float32(TWO_PI - c1 - float(np.float32(TWO_PI - c1))))
            S_.add("vector", TS(ang[:], c["iof"][:], float(base), c["invf"][:, 0:1], ALU.add, ALU.mult),
                   reads=["iof", "invf"], writes=[k + ("ang",)])
            S_.add("vector", TS(ki[:], ang[:], 1.0 / TWO_PI, None, ALU.mult), reads=[k + ("ang",)], writes=[k + ("ki",)])
            S_.add("vector", CP(kf[:], ki[:]), reads=[k + ("ki",)], writes=[k + ("kf",)])
            S_.add("vector", STT(red[:], kf[:], -c1, ang[:], ALU.mult, ALU.add), reads=[k + ("ang",), k + ("kf",)], writes=[k + ("red",)])
            S_.add("vector", STT(red[:], kf[:], -c2, red[:], ALU.mult, ALU.add), reads=[k + ("red",), k + ("kf",)], writes=[k + ("red",)])
            S_.add("vector", STT(red[:], kf[:], -c3, red[:], ALU.mult, ALU.add), reads=[k + ("red",), k + ("kf",)], writes=[k + ("red",)])
            PI = math.pi
            S_.add("vector", TS(sa[:], red[:], PI, -TWO_PI, ALU.is_gt, ALU.mult), reads=[k + ("red",)], writes=[k + ("sa",)])
            S_.add("vector", TT(sa[:], sa[:], red[:], ALU.add), reads=[k + ("red",), k + ("sa",)], writes=[k + ("sa",)])
            S_.add("vector", TS(sa[:], sa[:], -PI, PI, ALU.max, ALU.min), reads=[k + ("sa",)], writes=[k + ("sa",)])
            S_.add("vector", TS(ca[:], red[:], PI / 2, -TWO_PI, ALU.is_gt, ALU.mult), reads=[k + ("red",)], writes=[k + ("ca",)])
            S_.add("vector", STT(ca[:], red[:], PI / 2, ca[:], ALU.add, ALU.add), reads=[k + ("red",), k + ("ca",)], writes=[k + ("ca",)])
            S_.add("vector", TS(ca[:], ca[:], -PI, PI, ALU.max, ALU.min), reads=[k + ("ca",)], writes=[k + ("ca",)])
            S_.add("scalar", ACT(sinT[:], sa[:], AF.Sin), reads=[k + ("sa",)], writes=[k + ("sin",)])
            S_.add("scalar", ACT(cosT[:], ca[:], AF.Sin), reads=[k + ("ca",)], writes=[k + ("cos",)])
            S_.add("vector", TS(sinT[:], sinT[:], c["sgn"][:, 0:1], None, ALU.mult), reads=[k + ("sin",), "sgn"],
                   writes=[k + ("sin",)])
            return cosT, sinT, k + ("cos",), k + ("sin",)

        def norm_tile(S_, c, xt, xkey, grep, gkey, ub, ubkey, st, stkey, nfeat=D):
            S_.add("scalar", ACT(ub, xt, AF.Square, accum_out=st[:, 0:1]), reads=[xkey], writes=[ubkey, stkey])
            S_.add("scalar", ACT(st[:, 1:2], st[:, 0:1], AF.Sqrt, bias=c["eps"][:, 0:1], scale=1.0 / nfeat),
                   reads=[stkey, "epst"], writes=[stkey])
            S_.add("vector", RCP(st[:, 2:3], st[:, 1:2]), reads=[stkey], writes=[stkey])
            S_.add("vector", STT(ub, xt, st[:, 2:3], grep, ALU.mult, ALU.mult), reads=[xkey, stkey, gkey], writes=[ubkey])

        def transpose_tile(S_, c, ub, ubkey, nch, ptr, ptrkeys, dst_fn, dstkeys):
            nh = (nch + 7) // 8
            for hf in range(nh):
                cn = min(8, nch - hf * 8)
                for cc in range(cn):
                    ch = hf * 8 + cc
                    S_.add("tensor", TR(ptr[hf][:, cc, :], ub[:, ch * 128:(ch + 1) * 128], c["ident"][:]),
                           reads=[ubkey, "ident"], writes=[ptrkeys[hf]])
                eng = "scalar" if hf % 2 == 0 else "vector"
                if eng == "scalar":
                    S_.add("scalar", ACT(dst_fn(hf, cn), ptr[hf][:, 0:cn, :], AF.Copy), reads=[ptrkeys[hf]], writes=dstkeys)
                else:
                    S_.add("vector", CP(dst_fn(hf, cn), ptr[hf][:, 0:cn, :]), reads=[ptrkeys[hf]], writes=dstkeys)

        def phase_proj(name, kind, xsrc, ntiles, wsrc, dst):
            own = (name == "B")
            with ExitStack() as es:
                S_ = Sched(nc, pool, name)
                sb = lambda n, s, d: es.enter_context(nc.sbuf_tensor(S_.name + "_" + n, s, d))
                ps = lambda n, s, d: es.enter_context(nc.psum_tensor(S_.name + "_" + n, s, d))
                c = common_consts(S_, es, kind == "KQ", own)
                w = sb("w", [128, NCH, 2048], BF16)
                wv = wsrc.rearrange("(c p) n -> p c n", p=128)
                for j in range(4):
                    S_.dma("gpsimd", w[:, :, j * 512:(j + 1) * 512], wv[:, :, j * 512:(j + 1) * 512], writes=[("w", j)], key=("w", j))
                wkeys = [("w", j) for j in range(4)]
                grep = sb("grep", [128, D], F32)
                S_.dma("sync", grep[:], bcast_row(gains, D, off=0), writes=["grep"], key="grep")
                xt = [sb(f"xt{i}", [128, D], F32) for i in range(2)]
                ub = [sb(f"ub{i}", [128, D], BF16) for i in range(4)]
                st = [sb(f"st{i}", [128, 4], F32) for i in range(2)]
                uT = [sb(f"uT{i}", [128, NCH, 512], BF16) for i in range(2)]
                ptr = [ps(f"ptr{i}", [128, 8, 128], BF16) for i in range(2)]
                pk = [ps(f"pk{i}", [128, 512], F32) for i in range(3)]
                if kind == "KQ":
                    pp = [ps(f"pp{i}", [128, 512], F32) for i in range(2)]
                    rt = [[sb(f"rt{s}_{n}", [32, 512], I32 if n == 1 else F32) for n in range(8)] for s in range(2)]
                    r32 = [sb(f"r32_{i}", [32, 512], F32) for i in range(2)]
                    t1 = sb("t1", [32, 512], F32)
                    t2 = sb("t2", [32, 512], F32)
                    osb = [sb(f"osb{i}", [128, 512], BF16) for i in range(3)]
                else:
                    vsb = [sb(f"vsb{i}", [128, 2048], BF16) for i in range(2)]
                ngroups = ntiles // 4
                for G in range(ngroups):
                    gs = G % 2
                    for tt in range(4):
                        t = G * 4 + tt
                        xs = t % 2
                        S_.dma("sync", xt[xs][:], xsrc[t * 128:(t + 1) * 128, :], writes=[("xt", xs)], key=("xt", xs))
                        norm_tile(S_, c, xt[xs][:], ("xt", xs), grep[:], "grep", ub[tt][:], ("ub", tt), st[xs], ("st", xs))
                        transpose_tile(S_, c, ub[tt], ("ub", tt), NCH, ptr, [("ptr", 0), ("ptr", 1)],
                                       lambda hf, cn, gs=gs, tt=tt: uT[gs][:, hf * 8:hf * 8 + cn, tt * 128:(tt + 1) * 128],
                                       [("uT", gs, tt)])
                    uTk = [("uT", gs, tt) for tt in range(4)]
                    if kind == "KQ":
                        base = (2048 * G) if own else (512 * G)
                        cosT, sinT, ck, sk = rope_tables(S_, rt[gs], c, base, gs)

                        def fin(h, G=G, cosT=cosT, sinT=sinT, ck=ck, sk=sk):
                            r = h % 2
                            o = h % 3
                            S_.add("tensor", MM(pp[r][0:32, :], c["psw"][:, :], r32[r][:, :], True, True),
                                   reads=[("r32", r), "psw"], writes=[("pp", r)])
                            S_.add("vector", TT(t1[:], r32[r][:], cosT[:], ALU.mult), reads=[("r32", r), ck], writes=["t1"])
                            S_.add("vector", TT(t2[:], pp[r][0:32, :], sinT[:], ALU.mult), reads=[("pp", r), sk], writes=["t2"])
                            S_.add("vector", TT(osb[o][0:32, :], t1[:], t2[:], ALU.add), reads=["t1", "t2", ("osb", o, 0)], writes=[("osb", o, 0)])
                            S_.dma("sync", dst[h][:, G * 512:(G + 1) * 512], osb[o][:], reads=[("osb", o, 0)],
                                   key=("osb", o))

                        for h in range(16):
                            p = h % 3
                            for ch in range(NCH):
                                S_.add("tensor", MM(pk[p][:], w[:, ch, h * 128:(h + 1) * 128], uT[gs][:, ch, :], ch == 0, ch == NCH - 1),
                                       reads=uTk + [("w", h // 4)] if ch in (0, NCH - 1) else (), writes=[("pk", p)])
                            S_.add("scalar", ACT(osb[h % 3][:, :], pk[p][:, :], AF.Copy), reads=[("pk", p)],
                                   writes=[("osb", h % 3, 0)])
                            S_.add("scalar", ACT(r32[h % 2][:], pk[p][0:32, :], AF.Copy), reads=[("pk", p)], writes=[("r32", h % 2)])
                            if h >= 1:
                                fin(h - 1)
                        fin(15)
                    else:
                        for tt in range(4):
                            t = G * 4 + tt
                            vs = t % 2
                            for cg in range(4):
                                p = (tt * 4 + cg) % 3
                                for ch in range(NCH):
                                    S_.add("tensor", MM(pk[p][:], uT[gs][:, ch, tt * 128:(tt + 1) * 128], w[:, ch, cg * 512:(cg + 1) * 512],
                                                        ch == 0, ch == NCH - 1),
                                           reads=[("uT", gs, tt), ("w", cg)] if ch in (0, NCH - 1) else (), writes=[("pk", p)])
                                if cg % 2 == 0:
                                    S_.add("scalar", ACT(vsb[vs][:, cg * 512:(cg + 1) * 512], pk[p][:], AF.Copy), reads=[("pk", p)],
                                           writes=[("vsb", vs, cg)])
                                else:
                                    S_.add("vector", CP(vsb[vs][:, cg * 512:(cg + 1) * 512], pk[p][:]), reads=[("pk", p)],
                                           writes=[("vsb", vs, cg)])
                            S_.dma("sync", dst[t * 128:(t + 1) * 128, :], vsb[vs][:], reads=[("vsb", vs, cg) for cg in range(4)],
                                   key=("vsb", vs))
                S_.finish()

        def attn_consts(S_, es, c):
            sb = lambda n, s, d: es.enter_context(nc.sbuf_tensor(S_.name + "_" + n, s, d))
            ci = c["ci"]
            pmi = sb("pmi", [128, 128], I32)
            pmf = sb("pmf", [128, 128], F32)
            pmask = sb("pmask", [128, 16, 128], BF16)
            for d in range(4):
                for ktl in range(2):
                    for qs in range(2):
                        idx = d * 4 + ktl * 2 + qs
                        base = 256 * d + 128 * ktl - 128 * qs
                        S_.add("gpsimd", lambda e, base=base: e.iota(pmi[:], pattern=[[-1, 128]], base=base, channel_multiplier=1),
                               writes=["pmi"])
                        S_.add("vector", CP(pmf[:], pmi[:]), reads=["pmi"], writes=["pmf"])
                        S_.add("vector", TS(pmf[:], pmf[:], ci[:, 1:2], 0.0, ALU.subtract, ALU.is_gt), reads=["pmf", "ci"], writes=["pmf"])
                        S_.add("vector", TS(pmask[:, idx, :], pmf[:], -BIG, None, ALU.mult), reads=["pmf"], writes=["pmask"])
            c["pmask"] = pmask
            return c

        def phase_moba():
            with ExitStack() as es:
                S_ = Sched(nc, pool, "CM")
                sb = lambda n, s, d: es.enter_context(nc.sbuf_tensor(S_.name + "_" + n, s, d))
                ps = lambda n, s, d: es.enter_context(nc.psum_tensor(S_.name + "_" + n, s, d))
                c = common_consts(S_, es, False, False)
                attn_consts(S_, es, c)
                ci = c["ci"]
                ident = c["ident"]
                pmask = c["pmask"]
                esel = sb("esel", [128, NB, 128], BF16)
                eones = sb("eones", [128, NB, 128], BF16)
                S_.add("vector", MS(eones[:], 1.0), writes=["eones"])
                S_.add("gpsimd", lambda e: e.affine_select(out=esel[:], in_=eones[:], pattern=[[-1, NB], [0, 128]],
                                                           compare_op=ALU.is_equal, fill=0.0, base=0, channel_multiplier=1),
                       reads=["eones"], writes=["esel"])
                ni = sb("ni", [128, NB], I32)
                nf = sb("nf", [128, NB], F32)
                vv = sb("vv", [128, NB], F32)
                gbA = sb("gbA", [128, NQB, NB], F32)
                ownb = sb("ownb", [128, NQB, NB], F32)
                futb = sb("futb", [128, NQB, NB], F32)
                S_.add("gpsimd", lambda e: e.iota(ni[:], pattern=[[1, NB]], base=0, channel_multiplier=0), writes=["ni"])
                S_.add("vector", CP(nf[:], ni[:]), reads=["ni"], writes=["nf"])
                for i in range(NQB):
                    S_.add("vector", TS(vv[:], nf[:], ci[:, 0:1], float(4 * i), ALU.subtract, ALU.subtract), reads=["nf", "ci"], writes=["vv"])
                    S_.add("vector", TS(gbA[:, i, :], vv[:], 0.0, -BIG, ALU.is_ge, ALU.mult), reads=["vv"], writes=["gconst"])
                    S_.add("vector", TS(ownb[:, i, :], vv[:], 0.0, 1.0, ALU.is_equal, ALU.subtract), reads=["vv"], writes=["gconst"])
                    S_.add("vector", TS(ownb[:, i, :], ownb[:, i, :], BIG, None, ALU.mult), reads=["gconst"], writes=["gconst"])
                    S_.add("vector", TS(futb[:, i, :], vv[:], 0.0, -BIG, ALU.is_gt, ALU.mult), reads=["vv"], writes=["gconst"])
                kt_sb = sb("kt_sb", [128, S], BF16)
                vx = sb("vx", [128, NT, 129], BF16)
                qt_sb = sb("qt_sb", [128, TOWN], BF16)
                S_.add("vector", MS(vx[:, :, 128:129], 1.0), writes=["vx1"])
                km = sb("km", [128, NB], F32)
                kmT = sb("kmT", [128, NB], BF16)
                gm = [sb(f"gm{q}", [128, NB], F32) for q in range(2)]
                mx8 = [sb(f"mx8{q}", [128, 8], F32) for q in range(2)]
                sel = [sb(f"sel{q}", [128, NB], F32) for q in range(2)]
                fb = [sb(f"fb{q}", [128, NB], BF16) for q in range(2)]
                biasTb = [sb(f"biasT{q}", [128, 256], BF16) for q in range(2)]
                for q in range(2):
                    S_.add("vector", MS(biasTb[q][:], 0.0), writes=[("biasT", q, 0), ("biasT", q, 1)])

                def gateA(i, qs):
                    q0 = i * 256 + qs * 128
                    S_.add("tensor", MM(pg[:, 0:NB], qt_sb[:, q0:q0 + 128], kmT[:, :], True, True), reads=["qt", "kmT"], writes=["pg"])
                    S_.add("vector", TT(gm[qs][:], pg[:, 0:NB], gbA[:, i, :], ALU.add), reads=["pg", "gconst"], writes=[("gm", qs)])
                    S_.add("vector", lambda e: e.max(out=mx8[qs][:], in_=gm[qs][:]), reads=[("gm", qs)], writes=[("mx8", qs)])
                    S_.add("vector", TS(sel[qs][:], gm[qs][:], mx8[qs][:, 2:3], 1.0, ALU.is_ge, ALU.subtract), reads=[("gm", qs), ("mx8", qs)],
                           writes=[("sel", qs)])
                    S_.add("vector", STT(sel[qs][:], sel[qs][:], BIG, ownb[:, i, :], ALU.mult, ALU.max), reads=[("sel", qs), "gconst"],
                           writes=[("sel", qs)])
                    S_.add("vector", TT(fb[qs][:], sel[qs][:], futb[:, i, :], ALU.min), reads=[("sel", qs), "gconst"], writes=[("fb", qs)])

                def gateB(i, qs):
                    b_ = i % 2
                    S_.add("tensor", TR(ptb[0:NB, 0, :], fb[qs][:, :], ident[:]), reads=[("fb", qs), "ident"], writes=["ptb"])
                    S_.add("scalar", ACT(biasTb[b_][0:NB, qs * 128:(qs + 1) * 128], ptb[0:NB, 0, :], AF.Copy), reads=["ptb"],
                           writes=[("biasT", b_, qs)])
                pT = [sb(f"pT{i}", [128, 256], BF16) for i in range(4)]
                rinv = sb("rinv", [128, 2], F32)
                obf = sb("obf", [128, 128], BF16)
                oT = [sb(f"oT{i}", [128, 256], BF16) for i in range(2)]
                pob = [ps(f"po{i}", [128, 512], F32) for i in range(2)]
                pstb = [ps(f"pst{i}", [128, 512], F32) for i in range(4)]
                pst = [pstb[i][:, 0:256] for i in range(4)]
                pg = ps("pg", [128, 512], F32)
                ptb = ps("ptb", [128, 8, 128], BF16)
                VSv = VS.rearrange("(t p) n -> p t n", p=128)
                npst = 0
                nkv = 8 if NT >= 64 else 4
                for h in range(8):
                    nkp = 4
                    for j in range(nkp):
                        a, b_ = j * (S // nkp), (j + 1) * (S // nkp)
                        S_.dma("sync", kt_sb[:, a:b_], KT[h][:, a:b_], writes=[("kt", j)], key=("kt", j))
                    for j in range(nkv):
                        a, b_ = j * (NT // nkv), (j + 1) * (NT // nkv)
                        S_.dma("sync", vx[:, a:b_, 0:128], VSv[:, a:b_, h * 128:(h + 1) * 128], writes=[("vx", j)], key=("vx", j))
                    S_.dma("sync", qt_sb[:], QT[h][:, :], writes=["qt"], key="qt")
                    ktk = [("kt", j) for j in range(nkp)]

                    S_.add("vector", lambda e: e.tensor_reduce(out=km[:], in_=kt_sb[:].rearrange("p (n k) -> p n k", k=256), axis=AX.X, op=ALU.add),
                           reads=ktk, writes=["km"])
                    S_.add("vector", TS(kmT[:], km[:], 1.0 / 256.0, None, ALU.mult), reads=["km"], writes=["kmT"])
                    for i in range(NQB):
                        if i == 0:
                            for qs in range(2):
                                gateA(0, qs)
                                gateB(0, qs)
                        biasT = biasTb[i % 2]
                        nkt = 8 * i + 8
                        LA = 2
                        slots = {}
                        for step in range(nkt + LA):
                            if i + 1 < NQB:
                                if step == 0:
                                    gateA(i + 1, 0)
                                elif step == 2:
                                    gateA(i + 1, 1)
                                elif step == 4:
                                    gateB(i + 1, 0)
                                elif step == 6:
                                    gateB(i + 1, 1)
                            if step < nkt:
                                kt = step
                                n = kt // 2
                                d = n - 4 * i
                                p = npst % 4
                                npst += 1
                                slots[kt] = p
                                S_.add("tensor", MM(pst[p], kt_sb[:, kt * 128:(kt + 1) * 128], qt_sb[:, i * 256:(i + 1) * 256], True, False),
                                       reads=[("kt", kt * 128 // (S // nkp)), "qt"], writes=[("pst", p)])
                                S_.add("tensor", MM(pst[p], esel[:, n, :], biasT[:, :], False, d < 0),
                                       reads=["esel", ("biasT", i % 2, 0), ("biasT", i % 2, 1)], writes=[("pst", p)])
                                if d >= 0:
                                    for qs in range(2):
                                        S_.add("tensor", MM(pst[p][:, qs * 128:(qs + 1) * 128], ident[:], pmask[:, d * 4 + (kt % 2) * 2 + qs, :],
                                                            False, qs == 1), reads=["ident", "pmask"], writes=[("pst", p)])
                                S_.add("scalar", ACT(pT[p][:], pst[p], AF.Exp, scale=SCALE), reads=[("pst", p)], writes=[("pT", p)])
                            if step >= LA:
                                kt = step - LA
                                p = slots[kt]
                                for qs in range(2):
                                    S_.add("tensor", MM(pob[qs][:, 0:129], pT[p][:, qs * 128:(qs + 1) * 128], vx[:, kt, :], kt == 0, kt == nkt - 1),
                                           reads=[("pT", p), ("vx", kt // (NT // nkv)), "vx1"], writes=[("po", qs)])
                        os_ = i % 2
                        for qs in range(2):
                            S_.add("vector", RCP(rinv[:, qs:qs + 1], pob[qs][:, 128:129]), reads=[("po", qs)], writes=[("rinv", qs)])
                            S_.add("vector", TS(obf[:], pob[qs][:, 0:128], rinv[:, qs:qs + 1], None, ALU.mult), reads=[("po", qs), ("rinv", qs)],
                                   writes=["obf"])
                            S_.add("tensor", TR(ptb[:, 1 + qs, :], obf[:], ident[:]), reads=["obf", "ident"], writes=[("ptbo", qs)])
                            S_.add("scalar", ACT(oT[os_][:, qs * 128:(qs + 1) * 128], ptb[:, 1 + qs, :], AF.Copy), reads=[("ptbo", qs)],
                                   writes=[("oT", os_, qs)])
                        S_.dma("sync", OT[h][:, i * 256:(i + 1) * 256], oT[os_][:], reads=[("oT", os_, 0), ("oT", os_, 1)], key=("oT", os_))
                S_.finish()

        def phase_diff():
            with ExitStack() as es:
                S_ = Sched(nc, pool, "CD")
                sb = lambda n, s, d: es.enter_context(nc.sbuf_tensor(S_.name + "_" + n, s, d))
                ps = lambda n, s, d: es.enter_context(nc.psum_tensor(S_.name + "_" + n, s, d))
                c = common_consts(S_, es, False, False)
                attn_consts(S_, es, c)
                ident = c["ident"]
                pmask = c["pmask"]
                lq = sb("lq", [128, 4, 128], F32)
                S_.dma("sync", lq[:], bass.AP(lam4.tensor, 0, [[0, 128], [128, 4], [1, 128]]), writes=["lq"], key="lq")
                lj = sb("lj", [128, 128], F32)
                ld = sb("ld", [128, 4], F32)
                S_.add("vector", TT(lj[:], lq[:, 0, :], lq[:, 1, :], ALU.mult), reads=["lq"], writes=["lj"])
                S_.add("vector", lambda e: e.tensor_reduce(out=ld[:, 0:1], in_=lj[:], axis=AX.X, op=ALU.add), reads=["lj"], writes=["ld0"])
                S_.add("vector", TT(lj[:], lq[:, 2, :], lq[:, 3, :], ALU.mult), reads=["lq", "ld0"], writes=["lj"])
                S_.add("vector", lambda e: e.tensor_reduce(out=ld[:, 1:2], in_=lj[:], axis=AX.X, op=ALU.add), reads=["lj"], writes=["ld1"])
                S_.add("scalar", ACT(ld[:, 2:4], ld[:, 0:2], AF.Exp), reads=["ld0", "ld1"], writes=["ld2"])
                nlam = sb("nlam", [128, 1], F32)
                S_.add("vector", TT(nlam[:], ld[:, 3:4], ld[:, 2:3], ALU.subtract), reads=["ld2"], writes=["nlam"])
                S_.add("vector", TS(nlam[:], nlam[:], -0.2, None, ALU.add), reads=["nlam"], writes=["nlam"])
                sg = sb("sg", [128, 256], F32)
                S_.dma("sync", sg[:], bcast_row(subg, 256), writes=["sg"], key="sg")
                S_.add("vector", TS(sg[:], sg[:], 0.8, None, ALU.mult), reads=["sg"], writes=["sg"])
                kt_sb = [sb(f"kt_sb{s}", [128, S], BF16) for s in range(2)]
                qt_sb = [sb(f"qt_sb{s}", [128, TOWN], BF16) for s in range(2)]
                vx = sb("vx", [128, NT, 257], BF16)
                S_.add("vector", MS(vx[:, :, 256:257], 1.0), writes=["vx1"])
                pT = [sb(f"pT{i}", [128, 256], BF16) for i in range(4)]
                rinv = sb("rinv", [128, 4], F32)
                tq = sb("tq", [128, 256], F32)
                oq = sb("oq", [128, 256], F32)
                st = sb("stt", [128, 4], F32)
                junk = sb("junk", [128, 256], BF16)
                obf = sb("obf", [128, 256], BF16)
                oT = [sb(f"oT{i}", [128, 2, 256], BF16) for i in range(2)]
                pob = [[ps(f"po{s}{q}", [128, 512], F32) for q in range(2)] for s in range(2)]
                pstb = [ps(f"pst{i}", [128, 512], F32) for i in range(3)]
                pst = [pstb[i][:, 0:256] for i in range(3)]
                ptb = ps("ptb", [128, 8, 128], BF16)
                VSv = VS.rearrange("(t p) n -> p t n", p=128)
                npst = 0
                nkp = 4
                nkv = 8 if NT >= 64 else 4
                for hd in range(4):
                    for s in range(2):
                        for j in range(nkp):
                            a, b_ = j * (S // nkp), (j + 1) * (S // nkp)
                            S_.dma("sync", kt_sb[s][:, a:b_], KT[8 + 2 * hd + s][:, a:b_], writes=[("kt", s, j)], key=("kt", s, j))
                        S_.dma("sync", qt_sb[s][:], QT[8 + 2 * hd + s][:, :], writes=[("qt", s)], key=("qt", s))
                    for j in range(nkv):
                        a, b_ = j * (NT // nkv), (j + 1) * (NT // nkv)
                        S_.dma("sync", vx[:, a:b_, 0:256], VSv[:, a:b_, 1024 + hd * 256:1024 + (hd + 1) * 256], writes=[("vx", j)], key=("vx", j))
                    for i in range(NQB):
                        nkt = 8 * i + 8
                        LA = 2
                        slots = {}
                        nsteps = nkt * 2
                        for step in range(nsteps + LA):
                            if step < nsteps:
                                kt, s = step // 2, step % 2
                                d = kt // 2 - 4 * i
                                p = npst % 3
                                tq_ = npst % 4
                                npst += 1
                                slots[step] = tq_
                                S_.add("tensor", MM(pst[p], kt_sb[s][:, kt * 128:(kt + 1) * 128], qt_sb[s][:, i * 256:(i + 1) * 256], True, d < 0),
                                       reads=[("kt", s, kt * 128 // (S // nkp)), ("qt", s)], writes=[("pst", p)])
                                if d >= 0:
                                    for qs in range(2):
                                        S_.add("tensor", MM(pst[p][:, qs * 128:(qs + 1) * 128], ident[:], pmask[:, d * 4 + (kt % 2) * 2 + qs, :],
                                                            False, qs == 1), reads=["ident", "pmask"], writes=[("pst", p)])
                                S_.add("scalar", ACT(pT[tq_][:], pst[p], AF.Exp, scale=SCALE), reads=[("pst", p)], writes=[("pT", tq_)])
                            if step >= LA:
                                st2 = step - LA
                                kt, s = st2 // 2, st2 % 2
                                p = slots[st2]
                                for qs in range(2):
                                    S_.add("tensor", MM(pob[s][qs][:, 0:257], pT[p][:, qs * 128:(qs + 1) * 128], vx[:, kt, :], kt == 0, kt == nkt - 1),
                                           reads=[("pT", p), ("vx", kt // (NT // nkv)), "vx1"], writes=[("po", s, qs)])
                        os_ = i % 2
                        for qs in range(2):
                            S_.add("vector", RCP(rinv[:, 0:1], pob[0][qs][:, 256:257]), reads=[("po", 0, qs)], writes=["rinv0"])
                            S_.add("vector", RCP(rinv[:, 1:2], pob[1][qs][:, 256:257]), reads=[("po", 1, qs)], writes=["rinv1"])
                            S_.add("vector", TS(rinv[:, 2:3], rinv[:, 1:2], nlam[:, 0:1], None, ALU.mult), reads=["rinv1", "nlam"], writes=["rinv2"])
                            S_.add("vector", TS(tq[:], pob[1][qs][:, 0:256], rinv[:, 2:3], None, ALU.mult), reads=[("po", 1, qs), "rinv2"], writes=["tq"])
                            S_.add("vector", STT(oq[:], pob[0][qs][:, 0:256], rinv[:, 0:1], tq[:], ALU.mult, ALU.add),
                                   reads=[("po", 0, qs), "rinv0", "tq"], writes=["oq"])
                            norm_tile(S_, c, oq[:], "oq", sg[:], "sg", obf[:], "obf", st, "stt", nfeat=256)
                            for cc in range(2):
                                S_.add("tensor", TR(ptb[:, qs * 2 + cc, :], obf[:, cc * 128:(cc + 1) * 128], ident[:]), reads=["obf", "ident"],
                                       writes=[("ptbo", qs, cc)])
                                S_.add("scalar", ACT(oT[os_][:, cc, qs * 128:(qs + 1) * 128], ptb[:, qs * 2 + cc, :], AF.Copy),
                                       reads=[("ptbo", qs, cc)], writes=[("oT", os_, qs, cc)])
                        for cc in range(2):
                            S_.dma("sync", OT[8 + hd * 2 + cc][:, i * 256:(i + 1) * 256], oT[os_][:, cc, :],
                                   reads=[("oT", os_, 0, cc), ("oT", os_, 1, cc)], key=("oT", os_, cc))
                S_.finish()

        def phase_dense():
            with ExitStack() as es:
                S_ = Sched(nc, pool, "DD")
                sb = lambda n, s, d: es.enter_context(nc.sbuf_tensor(S_.name + "_" + n, s, d))
                ps = lambda n, s, d: es.enter_context(nc.psum_tensor(S_.name + "_" + n, s, d))
                c = common_consts(S_, es, False, False)
                R = sb("R", [128, 64, 512], BF16)
                yall = sb("yall", [128, 4, D], F32)
                uT = sb("uT", [128, NCH, 512], BF16)
                wb = [sb(f"wb{i}", [128, NCH, 512], BF16) for i in range(3)]
                ht = [sb(f"ht{i}", [128, D], F32) for i in range(2)]
                ub = sb("ub", [128, D], BF16)
                grep = sb("grep", [128, D], F32)
                st = [sb(f"st{i}", [128, 4], F32) for i in range(2)]
                rl = [sb(f"rl{i}", [128, 512], F32) for i in range(2)]
                pt_f = sb("pt_f", [128, 256], F32)
                pt_b = sb("pt_b", [128, 256], BF16)
                pTt = sb("pTt", [128, 2, 512], BF16)
                sgt = sb("sgt", [128, 512], F32)
                ptr = [ps(f"ptr{i}", [128, 8, 128], BF16) for i in range(2)]
                pk = [ps(f"pk{i}", [128, 512], F32) for i in range(6)]
                H1 = dscr("H1", [TOWN, D], F32)
                H2 = dscr("H2", [TOWN, D], F32)
                wcnt = [0]
                pkc = [0]

                def load_w(src_view, nchunks):
                    sl = wcnt[0] % 3
                    wcnt[0] += 1
                    S_.dma("gpsimd", wb[sl][:, 0:nchunks, :], src_view, writes=[("wb", sl)], key=("wb", sl))
                    return wb[sl], ("wb", sl)

                def load_g(idx):
                    S_.dma("sync", grep[:], bcast_row(gains, D, off=idx * D), writes=["grep"], key="grep")

                def next_pk():
                    p = pkc[0] % 6
                    pkc[0] += 1
                    return p

                def tok_major_layer(lhs_fn, lhs_keys, nk, wsrc_fn, evac_fn):
                    for cg in range(4):
                        nsl = (nk + NCH - 1) // NCH
                        if nsl == 1:
                            wt, wk = load_w(wsrc_fn(cg, 0, nk), nk)
                            for tt in range(4):
                                p = next_pk()
                                for k in range(nk):
                                    S_.add("tensor", MM(pk[p][:], lhs_fn(tt, k), wt[:, k, :], k == 0, k == nk - 1),
                                           reads=lhs_keys(tt) + [wk] if k in (0, nk - 1) else (), writes=[("pk", p)])
                                evac_fn(tt, cg, p)
                        else:
                            pp_ = [next_pk() for _ in range(4)]
                            for sl in range(nsl):
                                wt, wk = load_w(wsrc_fn(cg, sl * NCH, NCH), NCH)
                                for tt in range(4):
                                    for k in range(NCH):
                                        kk = sl * NCH + k
                                        S_.add("tensor", MM(pk[pp_[tt]][:], lhs_fn(tt, kk), wt[:, k, :], kk == 0, kk == nk - 1),
                                               reads=lhs_keys(tt) + [wk] if k in (0, NCH - 1) else (), writes=[("pk", pp_[tt])])
                            for tt in range(4):
                                evac_fn(tt, cg, pp_[tt])

                def post_norm_residual(G, hsrc, hdst, stage):
                    for tt in range(4):
                        t = G * 4 + tt
                        hs = tt % 2
                        S_.dma("sync", ht[hs][:], hsrc[t * 128:(t + 1) * 128, :], writes=[("ht", hs)], key=("ht", hs))
                        s_ = st[hs]
                        S_.add("scalar", ACT(ub[:], yall[:, tt, :], AF.Square, accum_out=s_[:, 0:1]), reads=[("yall", tt)], writes=["ub", ("st", hs)])
                        S_.add("scalar", ACT(s_[:, 1:2], s_[:, 0:1], AF.Sqrt, bias=c["eps"][:, 0:1], scale=1.0 / D), reads=[("st", hs), "epst"],
                               writes=[("st", hs)])
                        S_.add("vector", RCP(s_[:, 2:3], s_[:, 1:2]), reads=[("st", hs)], writes=[("st", hs)])
                        S_.add("vector", STT(yall[:, tt, :], yall[:, tt, :], s_[:, 2:3], grep[:], ALU.mult, ALU.mult),
                               reads=[("yall", tt), ("st", hs), "grep"], writes=[("yall", tt)])
                        S_.add("vector", TT(ht[hs][:], ht[hs][:], yall[:, tt, :], ALU.add), reads=[("ht", hs), ("yall", tt)], writes=[("ht", hs)])
                        yield tt, t, hs

                def prep_from_h(tt, hs, gidx_loaded):
                    norm_tile(S_, c, ht[hs][:], ("ht", hs), grep[:], "grep", ub[:], "ub", st[hs], ("st", hs))
                    transpose_tile(S_, c, ub, "ub", NCH, ptr, [("ptr", 0), ("ptr", 1)],
                                   lambda hf, cn, tt=tt: uT[:, hf * 8:hf * 8 + cn, tt * 128:(tt + 1) * 128], [("uT", tt)])

                uTk = [("uT", tt) for tt in range(4)]
                wg_v = w_g.rearrange("(c p) n -> p c n", p=128)
                wbm_v = w_bm.rearrange("(c p) n -> p c n", p=128)
                wbd_v = w_bd.rearrange("(c p) n -> p c n", p=128)
                wout_v = w_out.rearrange("(c p) n -> p c n", p=128)
                wup_v = w_up.rearrange("(c p) n -> p c n", p=128)
                wdn_v = w_down.rearrange("(c p) n -> p c n", p=128)
                wpg_v = w_pg.rearrange("(c p) n -> p c n", p=128)
                wpp_v = w_pp.rearrange("(c p) n -> p c n", p=128)
                for G in range(NGO):
                    tok = slice(G * 512, (G + 1) * 512)
                    load_g(0)
                    for tt in range(4):
                        t = G * 4 + tt
                        hs = tt % 2
                        S_.dma("sync", ht[hs][:], xo[t * 128:(t + 1) * 128, :], writes=[("ht", hs)], key=("ht", hs))
                        prep_from_h(tt, hs, 0)
                    for cg in range(8):
                        wt, wk = load_w(wg_v[:, :, cg * 512:(cg + 1) * 512], NCH)
                        for jj in range(4):
                            j = cg * 4 + jj
                            p = next_pk()
                            for k in range(NCH):
                                S_.add("tensor", MM(pk[p][:], wt[:, k, jj * 128:(jj + 1) * 128], uT[:, k, :], k == 0, k == NCH - 1),
                                       reads=uTk + [wk] if k in (0, NCH - 1) else (), writes=[("pk", p)])
                            S_.add("scalar", ACT(R[:, j, :], pk[p][:], AF.Sigmoid), reads=[("pk", p)], writes=[("R", j)])
                    S_.dma("sync", R[:, 48:64, :], OT[:, :, tok].rearrange("c p t -> p c t"), writes=[("R", 48 + k) for k in range(16)], key="oab")
                    for cg in range(4):
                        wa, wak = load_w(wbm_v[:, :, cg * 512:(cg + 1) * 512], 8)
                        wd, wdk = load_w(wbd_v[:, :, cg * 512:(cg + 1) * 512], 8)
                        for jj in range(4):
                            j = cg * 4 + jj
                            pa = next_pk()
                            pb = next_pk()
                            for k in range(8):
                                S_.add("tensor", MM(pk[pa][:], wa[:, k, jj * 128:(jj + 1) * 128], R[:, 48 + k, :], k == 0, k == 7),
                                       reads=[("R", 48 + kk) for kk in range(8)] + [wak] if k in (0, 7) else (), writes=[("pk", pa)])
                            for k in range(8):
                                S_.add("tensor", MM(pk[pb][:], wd[:, k, jj * 128:(jj + 1) * 128], R[:, 56 + k, :], k == 0, k == 7),
                                       reads=[("R", 56 + kk) for kk in range(8)] + [wdk] if k in (0, 7) else (), writes=[("pk", pb)])
                            r_ = rl[j % 2]
                            S_.add("vector", TT(r_[:], pk[pa][:], R[:, j, :], ALU.mult), reads=[("pk", pa), ("R", j)], writes=[("rl", j % 2)])
                            S_.add("vector", TT(sgt[:], pk[pb][:], R[:, 16 + j, :], ALU.mult), reads=[("pk", pb), ("R", 16 + j)], writes=["sgt"])
                            S_.add("vector", TT(R[:, 32 + j, :], r_[:], sgt[:], ALU.add), reads=[("rl", j % 2), "sgt"], writes=[("R", 32 + j)])
                    load_g(1)

                    def evac_y(tt, cg, p):
                        if cg % 2 == 0:
                            S_.add("scalar", ACT(yall[:, tt, cg * 512:(cg + 1) * 512], pk[p][:], AF.Copy), reads=[("pk", p)], writes=[("yall", tt)])
                        else:
                            S_.add("vector", CP(yall[:, tt, cg * 512:(cg + 1) * 512], pk[p][:]), reads=[("pk", p)], writes=[("yall", tt)])

                    tok_major_layer(lambda tt, k: R[:, 32 + k, tt * 128:(tt + 1) * 128], lambda tt: [("R", 32 + k) for k in range(16)], NCH,
                                    lambda cg, k0, nk: wout_v[:, k0:k0 + nk, cg * 512:(cg + 1) * 512], evac_y)
                    hts = []
                    for tt, t, hs in post_norm_residual(G, xo, None, 1):
                        S_.dma("sync", H1[t * 128:(t + 1) * 128, :], ht[hs][:], reads=[("ht", hs)], writes=[("H1", t)], key=("h1s", hs))
                        hts.append((tt, hs))
                        if tt == 0:
                            pass
                    load_g(2)
                    for tt in range(4):
                        t = G * 4 + tt
                        hs = tt % 2
                        S_.dma("sync", ht[hs][:], H1[t * 128:(t + 1) * 128, :], reads=[("H1", t)], writes=[("ht", hs)], key=("ht", hs))
                        prep_from_h(tt, hs, 2)
                    for cg in range(16):
                        wt, wk = load_w(wup_v[:, :, cg * 512:(cg + 1) * 512], NCH)
                        for jj in range(4):
                            j = cg * 4 + jj
                            p = next_pk()
                            for k in range(NCH):
                                S_.add("tensor", MM(pk[p][:], wt[:, k, jj * 128:(jj + 1) * 128], uT[:, k, :], k == 0, k == NCH - 1),
                                       reads=uTk + [wk] if k in (0, NCH - 1) else (), writes=[("pk", p)])
                            r_ = rl[j % 2]
                            S_.add("scalar", ACT(r_[:], pk[p][:], AF.Relu), reads=[("pk", p)], writes=[("rl", j % 2)])
                            S_.add("vector", STT(R[:, j, :], pk[p][:], 0.0, r_[:], ALU.max, ALU.mult), reads=[("pk", p), ("rl", j % 2)],
                                   writes=[("R", j)])
                    load_g(3)
                    tok_major_layer(lambda tt, k: R[:, k, tt * 128:(tt + 1) * 128], lambda tt: [("R", k) for k in range(64)], 64,
                                    lambda cg, k0, nk: wdn_v[:, k0:k0 + nk, cg * 512:(cg + 1) * 512], evac_y)
                    for tt, t, hs in post_norm_residual(G, H1, None, 3):
                        S_.dma("sync", H2[t * 128:(t + 1) * 128, :], ht[hs][:], reads=[("ht", hs)], writes=[("H2", t)], key=("h2s", hs))
                    load_g(4)
                    for tt in range(4):
                        t = G * 4 + tt
                        hs = tt % 2
                        S_.dma("sync", ht[hs][:], H2[t * 128:(t + 1) * 128, :], reads=[("H2", t)], writes=[("ht", hs)], key=("ht", hs))
                        prep_from_h(tt, hs, 4)
                        S_.dma("sync", pt_f[:], po[t * 128:(t + 1) * 128, :], writes=["pt_f"], key="pt_f")
                        S_.add("vector", CP(pt_b[:], pt_f[:]), reads=["pt_f"], writes=["pt_b"])
                        transpose_tile(S_, c, pt_b, "pt_b", 2, ptr, [("ptr", 0), ("ptr", 1)],
                                       lambda hf, cn, tt=tt: pTt[:, 0:cn, tt * 128:(tt + 1) * 128], [("pTt", tt)])
                    load_g(5)
                    for cg in range(4):
                        wt, wk = load_w(wpg_v[:, :, cg * 512:(cg + 1) * 512], NCH)
                        wp, wpk = load_w(wpp_v[:, :, cg * 512:(cg + 1) * 512], 2)
                        for tt in range(4):
                            pa = next_pk()
                            pb = next_pk()
                            for k in range(NCH):
                                S_.add("tensor", MM(pk[pa][:], uT[:, k, tt * 128:(tt + 1) * 128], wt[:, k, :], k == 0, k == NCH - 1),
                                       reads=[("uT", tt), wk] if k in (0, NCH - 1) else (), writes=[("pk", pa)])
                            for k in range(2):
                                S_.add("tensor", MM(pk[pb][:], pTt[:, k, tt * 128:(tt + 1) * 128], wp[:, k, :], k == 0, k == 1),
                                       reads=[("pTt", tt), wpk] if k in (0, 1) else (), writes=[("pk", pb)])
                            S_.add("scalar", ACT(sgt[:], pk[pa][:], AF.Sigmoid), reads=[("pk", pa)], writes=["sgt"])
                            S_.add("vector", TT(yall[:, tt, cg * 512:(cg + 1) * 512], pk[pb][:], sgt[:], ALU.mult), reads=[("pk", pb), "sgt"],
                                   writes=[("yall", tt)])
                    for tt, t, hs in post_norm_residual(G, H2, None, 5):
                        S_.dma("sync", out[t * 128:(t + 1) * 128, :], ht[hs][:], reads=[("ht", hs)], key=("outs", hs))
                S_.finish()

        phase_proj("AK", "KQ", xb, NT, w_k, KT)
        phase_proj("AV", "V", xb, NT, w_v, VS)
        phase_proj("B", "KQ", xo, NTO, w_q, QT)
        phase_moba()
        phase_diff()
        phase_dense()
    return nc


def _prep_inputs(S, x, p, w_in, w_br_moba, w_br_diff, w_out, lambda_q1, lambda_k1, lambda_q2, lambda_k2,
                 diff_subln_g, g_mix_pre, g_mix_post, w_up, w_down, g_mlp_pre, g_mlp_post,
                 w_ple_proj, w_ple_gate, g_ple_pre, g_ple_post):
    f = lambda a: np.ascontiguousarray(np.asarray(a, dtype=np.float32))
    x = np.asarray(x, dtype=np.float32)
    p = np.asarray(p, dtype=np.float32)
    w_in = np.asarray(w_in, dtype=np.float32)[0]
    NB = S // 256
    wq = f(np.concatenate([w_in[:, 0:1024], w_in[:, 3072:4096]], axis=1))
    wk = f(np.concatenate([w_in[:, 1024:2048], w_in[:, 4096:5120]], axis=1))
    wv = f(np.concatenate([w_in[:, 2048:3072], w_in[:, 5120:6144]], axis=1))
    wg = f(w_in[:, 6144:10240])
    shared = {
        "w_k": wk, "w_v": wv, "w_q": wq, "w_g": wg,
        "w_bm": f(w_br_moba[0]), "w_bd": f(w_br_diff[0]), "w_out": f(w_out[0]),
        "w_up": f(w_up[0]), "w_down": f(w_down[0]), "w_pp": f(w_ple_proj[0]), "w_pg": f(w_ple_gate[0]),
        "lam4": f(np.stack([np.asarray(lambda_q1)[0], np.asarray(lambda_k1)[0], np.asarray(lambda_q2)[0], np.asarray(lambda_k2)[0]])),
        "subg": f(np.asarray(diff_subln_g)[0:1]),
        "gains": f(np.stack([np.asarray(a)[0] for a in (g_mix_pre, g_mix_post, g_mlp_pre, g_mlp_post, g_ple_pre, g_ple_post)])),
    }
    in_maps = []
    for c in range(8):
        b, g = c // 4, c % 4
        xbb = f(x[b])
        xo = f(xbb.reshape(NB, 256, D)[g::4].reshape(-1, D))
        pob = f(p[0, b].reshape(NB, 256, 256)[g::4].reshape(-1, 256))
        m = dict(shared)
        m["xb"] = xbb
        m["xo"] = xo
        m["po"] = pob
        m["cinfo"] = np.array([[g, 256 * g, 0, 0]], dtype=np.float32)
        in_maps.append(m)
    return in_maps


_NC_CACHE = {}


def kernel(**inputs):
    S = int(np.asarray(inputs["x"]).shape[1])
    if S not in _NC_CACHE:
        _NC_CACHE[S] = build(S)
    nc = _NC_CACHE[S]
    in_maps = _prep_inputs(S, **inputs)
    res = run_bass_kernel_spmd(nc, in_maps, core_ids=list(range(8)))
    _LAST_RES[0] = res
    NB = S // 256
    outp = np.empty((2, S, D), dtype=np.float32)
    for c in range(8):
        b, g = c // 4, c % 4
        o = np.asarray(res.results[c]["out"], dtype=np.float32).reshape(NB // 4, 256, D)
        outp[b].reshape(NB, 256, D)[g::4] = o
    return outp
```

```python
import math
import numpy as np
from contextlib import ExitStack
import concourse.bass as bass
import concourse.mybir as mybir
from concourse.bass_utils import run_bass_kernel_spmd

F32 = mybir.dt.float32
BF16 = mybir.dt.bfloat16
I32 = mybir.dt.int32
AF = mybir.ActivationFunctionType
ALU = mybir.AluOpType
AX = mybir.AxisListType

SEQ = 16384
DEBUG_DUMP = False
_LAST_RES = [None]
D = 2048
NCH = 16
EPS = 1e-6
BIG = 30000.0
ROPE_THETA = 500000.0
ENGS = ("tensor", "vector", "scalar", "gpsimd", "sync")
EPOCH = 30000


class Op:
    __slots__ = ("eng", "emit", "deps", "dkey", "sig", "has_cons")

    def __init__(self, eng, emit, dkey):
        self.eng = eng
        self.emit = emit
        self.deps = set()
        self.dkey = dkey
        self.sig = None
        self.has_cons = False


class SemPool:
    def __init__(self, nc, es):
        self.nc = nc
        self.es = es
        self.eng_sems = {e: [] for e in ENGS}
        self.eng_cnt = {e: 0 for e in ENGS}
        self.dma_sems = {}
        self.dma_cnt = {}
        self.n = 0

    def _new(self):
        self.n += 1
        return self.es.enter_context(self.nc.semaphore(f"sm{self.n}"))

    def next_eng_sig(self, e):
        c = self.eng_cnt[e]
        ep, v = divmod(c, EPOCH)
        if ep >= len(self.eng_sems[e]):
            self.eng_sems[e].append(self._new())
        self.eng_cnt[e] = c + 1
        return (self.eng_sems[e][ep], v + 1)

    def next_dma_sig(self, key):
        c = self.dma_cnt.get(key, 0)
        ep, v = divmod(c, EPOCH // 16)
        lst = self.dma_sems.setdefault(key, [])
        if ep >= len(lst):
            lst.append(self._new())
        self.dma_cnt[key] = c + 1
        return (lst[ep], 16 * (v + 1))


class Sched:
    def __init__(self, nc, pool, name):
        self.nc = nc
        self.pool = pool
        self.name = name
        self.ops = []
        self.last_w = {}
        self.readers = {}

    def add(self, eng, emit, reads=(), writes=(), dma=None):
        op = Op(eng, emit, dma)
        deps = op.deps
        lw = self.last_w
        rd = self.readers
        for k in reads:
            w = lw.get(k)
            if w is not None:
                deps.add(w)
        for k in writes:
            w = lw.get(k)
            if w is not None:
                deps.add(w)
            r = rd.get(k)
            if r:
                deps.update(r)
        for k in reads:
            rd.setdefault(k, []).append(op)
        for k in writes:
            lw[k] = op
            rd[k] = []
        deps.discard(op)
        if eng == "tensor":
            op.deps = {d for d in deps if not (d.eng == "tensor" and d.dkey is None)}
        for d in op.deps:
            d.has_cons = True
        self.ops.append(op)
        return op

    def dma(self, q, out, in_, reads=(), writes=(), key=None, **kw):
        return self.add(q, lambda e: e.dma_start(out=out, in_=in_, **kw), reads, writes, dma=key)

    def finish(self):
        last_dma = {}
        for op in self.ops:
            if op.dkey is not None:
                last_dma[op.dkey] = op
        fin = Op("sync", None, None)
        fin.deps = set(last_dma.values())
        for d in fin.deps:
            d.has_cons = True
        self.ops.append(fin)
        pool = self.pool
        for op in self.ops:
            if op.dkey is not None:
                op.sig = pool.next_dma_sig(op.dkey)
            elif op.has_cons and op.emit is not None:
                op.sig = pool.next_eng_sig(op.eng)
        per_eng = {e: [] for e in ENGS}
        for op in self.ops:
            per_eng[op.eng].append(op)
        with self.nc.Block(self.name) as block:
            for e in ENGS:
                lst = per_eng[e]
                if not lst:
                    continue

                def body(eng, lst=lst):
                    seen = {}
                    for op in lst:
                        waits = {}
                        for d in op.deps:
                            if d.sig is None:
                                continue
                            s, v = d.sig
                            k = id(s)
                            if seen.get(k, 0) >= v:
                                continue
                            if k not in waits or waits[k][1] < v:
                                waits[k] = (s, v)
                        for k, (s, v) in waits.items():
                            eng.wait_ge(s, v)
                            seen[k] = v
                        if op.emit is None:
                            continue
                        ins = op.emit(eng)
                        if op.sig is not None:
                            ins.then_inc(op.sig[0], 16 if op.dkey is not None else 1)

                getattr(block, e)(body)


def MM(out, lhsT, rhs, start, stop):
    return lambda e: e.matmul(out, lhsT=lhsT, rhs=rhs, start=start, stop=stop)


def TR(out, in_, ident):
    return lambda e: e.transpose(out, in_, ident)


def ACT(out, in_, func, bias=None, scale=None, accum_out=None):
    kw = {}
    if bias is not None:
        kw["bias"] = bias
    if scale is not None:
        kw["scale"] = scale
    if accum_out is not None:
        kw["accum_out"] = accum_out
    return lambda e: e.activation(out=out, in_=in_, func=func, **kw)


def TS(out, in0, s1, s2, op0, op1=None):
    if op1 is None:
        return lambda e: e.tensor_scalar(out=out, in0=in0, scalar1=s1, scalar2=None, op0=op0)
    return lambda e: e.tensor_scalar(out=out, in0=in0, scalar1=s1, scalar2=s2, op0=op0, op1=op1)


def TT(out, in0, in1, op):
    return lambda e: e.tensor_tensor(out=out, in0=in0, in1=in1, op=op)


def STT(out, in0, scalar, in1, op0, op1):
    return lambda e: e.scalar_tensor_tensor(out=out, in0=in0, scalar=scalar, in1=in1, op0=op0, op1=op1)


def CP(out, in_):
    return lambda e: e.tensor_copy(out=out, in_=in_)


def RCP(out, in_):
    return lambda e: e.reciprocal(out=out, in_=in_)


def MS(ap, v):
    return lambda e: e.memset(ap, v)


def bcast_row(handle_ap, n, parts=128, off=0):
    return bass.AP(handle_ap.tensor, off, [[0, parts], [1, n]])


def build(S):
    NT = S // 128
    NB = S // 256
    NQB = NB // 4
    TOWN = S // 4
    NTO = TOWN // 128
    NGA = S // 512
    NGO = TOWN // 512
    SCALE = 1.0 / math.sqrt(128.0)

    nc = bass.Bass("TRN2", target_bir_lowering=False)
    din = lambda n, s: nc.dram_tensor(n, s, F32, kind="ExternalInput").ap()
    xb = din("xb", [S, D])
    xo = din("xo", [TOWN, D])
    po = din("po", [TOWN, 256])
    w_k = din("w_k", [D, 2048])
    w_v = din("w_v", [D, 2048])
    w_q = din("w_q", [D, 2048])
    w_g = din("w_g", [D, 4096])
    w_bm = din("w_bm", [1024, D])
    w_bd = din("w_bd", [1024, D])
    w_out = din("w_out", [D, D])
    w_up = din("w_up", [D, 8192])
    w_down = din("w_down", [8192, D])
    w_pp = din("w_pp", [256, D])
    w_pg = din("w_pg", [D, D])
    lam4 = din("lam4", [4, 128])
    subg = din("subg", [1, 256])
    gains = din("gains", [6, D])
    cinfo = din("cinfo", [1, 4])
    out = nc.dram_tensor("out", [TOWN, D], F32, kind="ExternalOutput").ap()

    dscr = lambda n, s, dt: nc.dram_tensor(n, s, dt, kind="Internal").ap()
    if DEBUG_DUMP:
        dscr = lambda n, s, dt: nc.dram_tensor(n, s, dt, kind="ExternalOutput" if n in ("KT", "VS", "QT", "OT") else "Internal").ap()
    KT = dscr("KT", [16, 128, S], BF16)
    VS = dscr("VS", [S, 2048], BF16)
    QT = dscr("QT", [16, 128, TOWN], BF16)
    OT = dscr("OT", [16, 128, TOWN], BF16)

    with ExitStack() as ges:
        pool = SemPool(nc, ges)

        def common_consts(S_, es, need_rope, own):
            sb = lambda n, s, d: es.enter_context(nc.sbuf_tensor(S_.name + "_" + n, s, d))
            c = {}
            ident = sb("ident", [128, 128], BF16)
            ones_bf = sb("ones_bf", [128, 128], BF16)
            S_.add("vector", MS(ones_bf[:], 1.0), writes=["ones_bf"])
            S_.add("gpsimd", lambda e: e.affine_select(out=ident[:], in_=ones_bf[:], pattern=[[-1, 128]],
                                                       compare_op=ALU.is_equal, fill=0.0, base=0, channel_multiplier=1),
                   reads=["ones_bf"], writes=["ident"])
            c["ident"] = ident
            epst = sb("epst", [128, 1], F32)
            S_.add("vector", MS(epst[:], EPS), writes=["epst"])
            c["eps"] = epst
            ci = sb("ci", [128, 4], F32)
            S_.dma("sync", ci[:], bcast_row(cinfo, 4), writes=["ci"], key="ci")
            c["ci"] = ci
            if not need_rope:
                return c
            pswA = sb("pswA", [32, 32], F32)
            pswB = sb("pswB", [32, 32], F32)
            psw = sb("psw", [32, 32], F32)
            ones_f = sb("ones_f", [32, 32], F32)
            S_.add("vector", MS(ones_f[:], 1.0), writes=["ones_f"])
            for t, base in ((pswA, -16), (pswB, 16)):
                S_.add("gpsimd", lambda e, t=t, base=base: e.affine_select(
                    out=t[:], in_=ones_f[:], pattern=[[-1, 32]], compare_op=ALU.is_equal, fill=0.0,
                    base=base, channel_multiplier=1), reads=["ones_f"], writes=[id(t)])
            S_.add("vector", TT(psw[:], pswA[:], pswB[:], ALU.add), reads=[id(pswA), id(pswB)], writes=["psw"])
            c["psw"] = psw
            ri = sb("ri", [32, 1], I32)
            rf = sb("rf", [32, 1], F32)
            r16 = sb("r16", [32, 1], F32)
            tmp = sb("rtmp", [32, 1], F32)
            invf = sb("invf", [32, 1], F32)
            sgn = sb("sgn", [32, 1], F32)
            S_.add("gpsimd", lambda e: e.iota(ri[:], pattern=[[0, 1]], base=0, channel_multiplier=1), writes=["ri"])
            S_.add("vector", CP(rf[:], ri[:]), reads=["ri"], writes=["rf"])
            S_.add("vector", TS(tmp[:], rf[:], 16.0, -16.0, ALU.is_ge, ALU.mult), reads=["rf"], writes=["rtmp"])
            S_.add("vector", TT(r16[:], rf[:], tmp[:], ALU.add), reads=["rf", "rtmp"], writes=["r16"])
            S_.add("vector", TS(sgn[:], rf[:], 16.0, 2.0, ALU.is_ge, ALU.mult), reads=["rf"], writes=["sgn"])
            S_.add("vector", TS(sgn[:], sgn[:], -1.0, None, ALU.add), reads=["sgn"], writes=["sgn"])
            S_.add("vector", MS(invf[:], 0.0), writes=["invf"])
            for j in range(16):
                cj = float(np.float32(1.0) / (np.float32(ROPE_THETA) ** (np.float32(j * 2.0) / np.float32(32.0))))
                S_.add("vector", TS(tmp[:], r16[:], float(j), cj, ALU.is_equal, ALU.mult), reads=["r16", "rtmp"], writes=["rtmp"])
                S_.add("vector", TT(invf[:], invf[:], tmp[:], ALU.add), reads=["invf", "rtmp"], writes=["invf"])
            c["invf"] = invf
            c["sgn"] = sgn
            ioi = sb("ioi", [32, 512], I32)
            iof = sb("iof", [32, 512], F32)
            if own:
                S_.add("gpsimd", lambda e: e.iota(ioi[:], pattern=[[1024, 2], [1, 256]], base=0, channel_multiplier=0), writes=["ioi"])
                S_.add("vector", CP(iof[:], ioi[:]), reads=["ioi"], writes=["iof"])
                S_.add("vector", TS(iof[:], iof[:], ci[0:32, 1:2], None, ALU.add), reads=["iof", "ci"], writes=["iof"])
            else:
                S_.add("gpsimd", lambda e: e.iota(ioi[:], pattern=[[1, 512]], base=0, channel_multiplier=0), writes=["ioi"])
                S_.add("vector", CP(iof[:], ioi[:]), reads=["ioi"], writes=["iof"])
            c["iof"] = iof
            return c

        def rope_tables(S_, rt, c, base, slot):
            ang, ki, kf, red, sa, ca, cosT, sinT = rt
            k = ("rt", slot)
            TWO_PI = 2.0 * math.pi
            c1 = 6.28125
            c2 = float(np.float32(TWO_PI - c1))
            c3 = float(np.float32(TWO_PI - c1 - float(np.float32(TWO_PI - c1))))
            S_.add("vector", TS(ang[:], c["iof"][:], float(base), c["invf"][:, 0:1], ALU.add, ALU.mult),
                   reads=["iof", "invf"], writes=[k + ("ang",)])
            S_.add("vector", TS(ki[:], ang[:], 1.0 / TWO_PI, None, ALU.mult), reads=[k + ("ang",)], writes=[k + ("ki",)])
            S_.add("vector", CP(kf[:], ki[:]), reads=[k + ("ki",)], writes=[k + ("kf",)])
            S_.add("vector", STT(red[:], kf[:], -c1, ang[:], ALU.mult, ALU.add), reads=[k + ("ang",), k + ("kf",)], writes=[k + ("red",)])
            S_.add("vector", STT(red[:], kf[:], -c2, red[:], ALU.mult, ALU.add), reads=[k + ("red",), k + ("kf",)], writes=[k + ("red",)])
            S_.add("vector", STT(red[:], kf[:], -c3, red[:], ALU.mult, ALU.add), reads=[k + ("red",), k + ("kf",)], writes=[k + ("red",)])
            PI = math.pi
            S_.add("vector", TS(sa[:], red[:], PI, -TWO_PI, ALU.is_gt, ALU.mult), reads=[k + ("red",)], writes=[k + ("sa",)])
            S_.add("vector", TT(sa[:], sa[:], red[:], ALU.add), reads=[k + ("red",), k + ("sa",)], writes=[k + ("sa",)])
            S_.add("vector", TS(sa[:], sa[:], -PI, PI, ALU.max, ALU.min), reads=[k + ("sa",)], writes=[k + ("sa",)])
            S_.add("vector", TS(ca[:], red[:], PI / 2, -TWO_PI, ALU.is_gt, ALU.mult), reads=[k + ("red",)], writes=[k + ("ca",)])
            S_.add("vector", STT(ca[:], red[:], PI / 2, ca[:], ALU.add, ALU.add), reads=[k + ("red",), k + ("ca",)], writes=[k + ("ca",)])
            S_.add("vector", TS(ca[:], ca[:], -PI, PI, ALU.max, ALU.min), reads=[k + ("ca",)], writes=[k + ("ca",)])
            S_.add("scalar", ACT(sinT[:], sa[:], AF.Sin), reads=[k + ("sa",)], writes=[k + ("sin",)])
            S_.add("scalar", ACT(cosT[:], ca[:], AF.Sin), reads=[k + ("ca",)], writes=[k + ("cos",)])
            S_.add("vector", TS(sinT[:], sinT[:], c["sgn"][:, 0:1], None, ALU.mult), reads=[k + ("sin",), "sgn"],
                   writes=[k + ("sin",)])
            return cosT, sinT, k + ("cos",), k + ("sin",)

        def norm_tile(S_, c, xt, xkey, grep, gkey, ub, ubkey, st, stkey, nfeat=D):
            S_.add("scalar", ACT(ub, xt, AF.Square, accum_out=st[:, 0:1]), reads=[xkey], writes=[ubkey, stkey])
            S_.add("scalar", ACT(st[:, 1:2], st[:, 0:1], AF.Sqrt, bias=c["eps"][:, 0:1], scale=1.0 / nfeat),
                   reads=[stkey, "epst"], writes=[stkey])
            S_.add("vector", RCP(st[:, 2:3], st[:, 1:2]), reads=[stkey], writes=[stkey])
            S_.add("vector", STT(ub, xt, st[:, 2:3], grep, ALU.mult, ALU.mult), reads=[xkey, stkey, gkey], writes=[ubkey])

        def transpose_tile(S_, c, ub, ubkey, nch, ptr, ptrkeys, dst_fn, dstkeys):
            nh = (nch + 7) // 8
            for hf in range(nh):
                cn = min(8, nch - hf * 8)
                for cc in range(cn):
                    ch = hf * 8 + cc
                    S_.add("tensor", TR(ptr[hf][:, cc, :], ub[:, ch * 128:(ch + 1) * 128], c["ident"][:]),
                           reads=[ubkey, "ident"], writes=[ptrkeys[hf]])
                eng = "scalar" if hf % 2 == 0 else "vector"
                if eng == "scalar":
                    S_.add("scalar", ACT(dst_fn(hf, cn), ptr[hf][:, 0:cn, :], AF.Copy), reads=[ptrkeys[hf]], writes=dstkeys)
                else:
                    S_.add("vector", CP(dst_fn(hf, cn), ptr[hf][:, 0:cn, :]), reads=[ptrkeys[hf]], writes=dstkeys)

        def phase_proj(name, kind, xsrc, ntiles, wsrc, dst):
            own = (name == "B")
            with ExitStack() as es:
                S_ = Sched(nc, pool, name)
                sb = lambda n, s, d: es.enter_context(nc.sbuf_tensor(S_.name + "_" + n, s, d))
                ps = lambda n, s, d: es.enter_context(nc.psum_tensor(S_.name + "_" + n, s, d))
                c = common_consts(S_, es, kind == "KQ", own)
                w = sb("w", [128, NCH, 2048], BF16)
                wv = wsrc.rearrange("(c p) n -> p c n", p=128)
                for j in range(4):
                    S_.dma("gpsimd", w[:, :, j * 512:(j + 1) * 512], wv[:, :, j * 512:(j + 1) * 512], writes=[("w", j)], key=("w", j))
                wkeys = [("w", j) for j in range(4)]
                grep = sb("grep", [128, D], F32)
                S_.dma("sync", grep[:], bcast_row(gains, D, off=0), writes=["grep"], key="grep")
                xt = [sb(f"xt{i}", [128, D], F32) for i in range(2)]
                ub = [sb(f"ub{i}", [128, D], BF16) for i in range(4)]
                st = [sb(f"st{i}", [128, 4], F32) for i in range(2)]
                uT = [sb(f"uT{i}", [128, NCH, 512], BF16) for i in range(2)]
                ptr = [ps(f"ptr{i}", [128, 8, 128], BF16) for i in range(2)]
                pk = [ps(f"pk{i}", [128, 512], F32) for i in range(3)]
                if kind == "KQ":
                    pp = [ps(f"pp{i}", [128, 512], F32) for i in range(2)]
                    rt = [[sb(f"rt{s}_{n}", [32, 512], I32 if n == 1 else F32) for n in range(8)] for s in range(2)]
                    r32 = [sb(f"r32_{i}", [32, 512], F32) for i in range(2)]
                    t1 = sb("t1", [32, 512], F32)
                    t2 = sb("t2", [32, 512], F32)
                    osb = [sb(f"osb{i}", [128, 512], BF16) for i in range(3)]
                else:
                    vsb = [sb(f"vsb{i}", [128, 2048], BF16) for i in range(2)]
                ngroups = ntiles // 4
                ropes = {}

                def prepA(G):
                    gs = G % 2
                    for tt in range(4):
                        t = G * 4 + tt
                        xs = t % 2
                        S_.dma("sync", xt[xs][:], xsrc[t * 128:(t + 1) * 128, :], writes=[("xt", xs)], key=("xt", xs))
                        norm_tile(S_, c, xt[xs][:], ("xt", xs), grep[:], "grep", ub[tt][:], ("ub", tt), st[xs], ("st", xs))
                    if kind == "KQ":
                        base = (2048 * G) if own else (512 * G)
                        ropes[G] = rope_tables(S_, rt[gs], c, base, gs)

                def prepB(G):
                    gs = G % 2
                    for tt in range(4):
                        transpose_tile(S_, c, ub[tt], ("ub", tt), NCH, ptr, [("ptr", 0), ("ptr", 1)],
                                       lambda hf, cn, gs=gs, tt=tt: uT[gs][:, hf * 8:hf * 8 + cn, tt * 128:(tt + 1) * 128],
                                       [("uT", gs, tt)])

                prepA(0)
                prepB(0)
                for G in range(ngroups):
                    gs = G % 2
                    uTk = [("uT", gs, tt) for tt in range(4)]
                    if kind == "KQ":
                        cosT, sinT, ck, sk = ropes.pop(G)

                        def fin(h, G=G, cosT=cosT, sinT=sinT, ck=ck, sk=sk):
                            r = h % 2
                            o = h % 3
                            S_.add("tensor", MM(pp[r][0:32, :], c["psw"][:, :], r32[r][:, :], True, True),
                                   reads=[("r32", r), "psw"], writes=[("pp", r)])
                            S_.add("vector", TT(t1[:], r32[r][:], cosT[:], ALU.mult), reads=[("r32", r), ck], writes=["t1"])
                            S_.add("vector", TT(t2[:], pp[r][0:32, :], sinT[:], ALU.mult), reads=[("pp", r), sk], writes=["t2"])
                            S_.add("vector", TT(osb[o][0:32, :], t1[:], t2[:], ALU.add), reads=["t1", "t2", ("osb", o, 0)], writes=[("osb", o, 0)])
                            S_.dma("sync", dst[h][:, G * 512:(G + 1) * 512], osb[o][:], reads=[("osb", o, 0)],
                                   key=("osb", o))

                        for h in range(16):
                            if G + 1 < ngroups:
                                if h == 3:
                                    prepA(G + 1)
                                elif h == 11:
                                    prepB(G + 1)
                            p = h % 3
                            for ch in range(NCH):
                                S_.add("tensor", MM(pk[p][:], w[:, ch, h * 128:(h + 1) * 128], uT[gs][:, ch, :], ch == 0, ch == NCH - 1),
                                       reads=uTk + [("w", h // 4)] if ch in (0, NCH - 1) else (), writes=[("pk", p)])
                            S_.add("scalar", ACT(osb[h % 3][:, :], pk[p][:, :], AF.Copy), reads=[("pk", p)],
                                   writes=[("osb", h % 3, 0)])
                            S_.add("scalar", ACT(r32[h % 2][:], pk[p][0:32, :], AF.Copy), reads=[("pk", p)], writes=[("r32", h % 2)])
                            if h >= 1:
                                fin(h - 1)
                        fin(15)
                    else:
                        for tt in range(4):
                            if G + 1 < ngroups:
                                if tt == 1:
                                    prepA(G + 1)
                                elif tt == 3:
                                    prepB(G + 1)
                            t = G * 4 + tt
                            vs = t % 2
                            for cg in range(4):
                                p = (tt * 4 + cg) % 3
                                for ch in range(NCH):
                                    S_.add("tensor", MM(pk[p][:], uT[gs][:, ch, tt * 128:(tt + 1) * 128], w[:, ch, cg * 512:(cg + 1) * 512],
                                                        ch == 0, ch == NCH - 1),
                                           reads=[("uT", gs, tt), ("w", cg)] if ch in (0, NCH - 1) else (), writes=[("pk", p)])
                                if cg % 2 == 0:
                                    S_.add("scalar", ACT(vsb[vs][:, cg * 512:(cg + 1) * 512], pk[p][:], AF.Copy), reads=[("pk", p)],
                                           writes=[("vsb", vs, cg)])
                                else:
                                    S_.add("vector", CP(vsb[vs][:, cg * 512:(cg + 1) * 512], pk[p][:]), reads=[("pk", p)],
                                           writes=[("vsb", vs, cg)])
                            S_.dma("sync", dst[t * 128:(t + 1) * 128, :], vsb[vs][:], reads=[("vsb", vs, cg) for cg in range(4)],
                                   key=("vsb", vs))
                S_.finish()

        def attn_consts(S_, es, c):
            sb = lambda n, s, d: es.enter_context(nc.sbuf_tensor(S_.name + "_" + n, s, d))
            ci = c["ci"]
            pmi = sb("pmi", [128, 128], I32)
            pmf = sb("pmf", [128, 128], F32)
            pmask = sb("pmask", [128, 16, 128], BF16)
            for d in range(4):
                for ktl in range(2):
                    for qs in range(2):
                        idx = d * 4 + ktl * 2 + qs
                        base = 256 * d + 128 * ktl - 128 * qs
                        S_.add("gpsimd", lambda e, base=base: e.iota(pmi[:], pattern=[[-1, 128]], base=base, channel_multiplier=1),
                               writes=["pmi"])
                        S_.add("vector", CP(pmf[:], pmi[:]), reads=["pmi"], writes=["pmf"])
                        S_.add("vector", TS(pmf[:], pmf[:], ci[:, 1:2], 0.0, ALU.subtract, ALU.is_gt), reads=["pmf", "ci"], writes=["pmf"])
                        S_.add("vector", TS(pmask[:, idx, :], pmf[:], -BIG, None, ALU.mult), reads=["pmf"], writes=["pmask"])
            c["pmask"] = pmask
            return c

        def phase_moba():
            with ExitStack() as es:
                S_ = Sched(nc, pool, "CM")
                sb = lambda n, s, d: es.enter_context(nc.sbuf_tensor(S_.name + "_" + n, s, d))
                ps = lambda n, s, d: es.enter_context(nc.psum_tensor(S_.name + "_" + n, s, d))
                c = common_consts(S_, es, False, False)
                attn_consts(S_, es, c)
                ci = c["ci"]
                ident = c["ident"]
                pmask = c["pmask"]
                esel = sb("esel", [128, NB, 128], BF16)
                eones = sb("eones", [128, NB, 128], BF16)
                S_.add("vector", MS(eones[:], 1.0), writes=["eones"])
                S_.add("gpsimd", lambda e: e.affine_select(out=esel[:], in_=eones[:], pattern=[[-1, NB], [0, 128]],
                                                           compare_op=ALU.is_equal, fill=0.0, base=0, channel_multiplier=1),
                       reads=["eones"], writes=["esel"])
                ni = sb("ni", [128, NB], I32)
                nf = sb("nf", [128, NB], F32)
                vv = sb("vv", [128, NB], F32)
                gbA = sb("gbA", [128, NQB, NB], F32)
                ownb = sb("ownb", [128, NQB, NB], F32)
                futb = sb("futb", [128, NQB, NB], F32)
                S_.add("gpsimd", lambda e: e.iota(ni[:], pattern=[[1, NB]], base=0, channel_multiplier=0), writes=["ni"])
                S_.add("vector", CP(nf[:], ni[:]), reads=["ni"], writes=["nf"])
                for i in range(NQB):
                    S_.add("vector", TS(vv[:], nf[:], ci[:, 0:1], float(4 * i), ALU.subtract, ALU.subtract), reads=["nf", "ci"], writes=["vv"])
                    S_.add("vector", TS(gbA[:, i, :], vv[:], 0.0, -BIG, ALU.is_ge, ALU.mult), reads=["vv"], writes=["gconst"])
                    S_.add("vector", TS(ownb[:, i, :], vv[:], 0.0, 1.0, ALU.is_equal, ALU.subtract), reads=["vv"], writes=["gconst"])
                    S_.add("vector", TS(ownb[:, i, :], ownb[:, i, :], BIG, None, ALU.mult), reads=["gconst"], writes=["gconst"])
                    S_.add("vector", TS(futb[:, i, :], vv[:], 0.0, -BIG, ALU.is_gt, ALU.mult), reads=["vv"], writes=["gconst"])
                kt_sb = sb("kt_sb", [128, S], BF16)
                vx = sb("vx", [128, NT, 129], BF16)
                qt_sb = sb("qt_sb", [128, TOWN], BF16)
                S_.add("vector", MS(vx[:, :, 128:129], 1.0), writes=["vx1"])
                km = sb("km", [128, NB], F32)
                kmT = sb("kmT", [128, NB], BF16)
                gm = [sb(f"gm{q}", [128, NB], F32) for q in range(2)]
                mx8 = [sb(f"mx8{q}", [128, 8], F32) for q in range(2)]
                sel = [sb(f"sel{q}", [128, NB], F32) for q in range(2)]
                fb = [sb(f"fb{q}", [128, NB], BF16) for q in range(2)]
                biasTb = [sb(f"biasT{q}", [128, 256], BF16) for q in range(2)]
                for q in range(2):
                    S_.add("vector", MS(biasTb[q][:], 0.0), writes=[("biasT", q, 0), ("biasT", q, 1)])

                def gateA(i, qs):
                    q0 = i * 256 + qs * 128
                    S_.add("tensor", MM(pg[:, 0:NB], qt_sb[:, q0:q0 + 128], kmT[:, :], True, True), reads=["qt", "kmT"], writes=["pg"])
                    S_.add("vector", TT(gm[qs][:], pg[:, 0:NB], gbA[:, i, :], ALU.add), reads=["pg", "gconst"], writes=[("gm", qs)])
                    S_.add("vector", lambda e: e.max(out=mx8[qs][:], in_=gm[qs][:]), reads=[("gm", qs)], writes=[("mx8", qs)])
                    S_.add("vector", TS(sel[qs][:], gm[qs][:], mx8[qs][:, 2:3], 1.0, ALU.is_ge, ALU.subtract), reads=[("gm", qs), ("mx8", qs)],
                           writes=[("sel", qs)])
                    S_.add("vector", STT(sel[qs][:], sel[qs][:], BIG, ownb[:, i, :], ALU.mult, ALU.max), reads=[("sel", qs), "gconst"],
                           writes=[("sel", qs)])
                    S_.add("vector", TT(fb[qs][:], sel[qs][:], futb[:, i, :], ALU.min), reads=[("sel", qs), "gconst"], writes=[("fb", qs)])

                def gateB(i, qs):
                    b_ = i % 2
                    S_.add("tensor", TR(ptb[0:NB, 0, :], fb[qs][:, :], ident[:]), reads=[("fb", qs), "ident"], writes=["ptb"])
                    S_.add("scalar", ACT(biasTb[b_][0:NB, qs * 128:(qs + 1) * 128], ptb[0:NB, 0, :], AF.Copy), reads=["ptb"],
                           writes=[("biasT", b_, qs)])
                pT = [sb(f"pT{i}", [128, 256], BF16) for i in range(4)]
                rinv = sb("rinv", [128, 2], F32)
                obf = sb("obf", [128, 128], BF16)
                oT = [sb(f"oT{i}", [128, 256], BF16) for i in range(2)]
                pob = [ps(f"po{i}", [128, 512], F32) for i in range(2)]
                pstb = [ps(f"pst{i}", [128, 512], F32) for i in range(4)]
                pst = [pstb[i][:, 0:256] for i in range(4)]
                pg = ps("pg", [128, 512], F32)
                ptb = ps("ptb", [128, 8, 128], BF16)
                VSv = VS.rearrange("(t p) n -> p t n", p=128)
                npst = 0
                nkv = 8 if NT >= 64 else 4
                for h in range(8):
                    nkp = 4
                    for j in range(nkp):
                        a, b_ = j * (S // nkp), (j + 1) * (S // nkp)
                        S_.dma("sync", kt_sb[:, a:b_], KT[h][:, a:b_], writes=[("kt", j)], key=("kt", j))
                    for j in range(nkv):
                        a, b_ = j * (NT // nkv), (j + 1) * (NT // nkv)
                        S_.dma("sync", vx[:, a:b_, 0:128], VSv[:, a:b_, h * 128:(h + 1) * 128], writes=[("vx", j)], key=("vx", j))
                    S_.dma("sync", qt_sb[:], QT[h][:, :], writes=["qt"], key="qt")
                    ktk = [("kt", j) for j in range(nkp)]

                    S_.add("vector", lambda e: e.tensor_reduce(out=km[:], in_=kt_sb[:].rearrange("p (n k) -> p n k", k=256), axis=AX.X, op=ALU.add),
                           reads=ktk, writes=["km"])
                    S_.add("vector", TS(kmT[:], km[:], 1.0 / 256.0, None, ALU.mult), reads=["km"], writes=["kmT"])
                    for i in range(NQB):
                        if i == 0:
                            for qs in range(2):
                                gateA(0, qs)
                                gateB(0, qs)
                        biasT = biasTb[i % 2]
                        nkt = 8 * i + 8
                        LA = 2
                        slots = {}
                        for step in range(nkt + LA):
                            if i + 1 < NQB:
                                if step == 0:
                                    gateA(i + 1, 0)
                                elif step == 2:
                                    gateA(i + 1, 1)
                                elif step == 4:
                                    gateB(i + 1, 0)
                                elif step == 6:
                                    gateB(i + 1, 1)
                            if step < nkt:
                                kt = step
                                n = kt // 2
                                d = n - 4 * i
                                p = npst % 4
                                npst += 1
                                slots[kt] = p
                                S_.add("tensor", MM(pst[p], kt_sb[:, kt * 128:(kt + 1) * 128], qt_sb[:, i * 256:(i + 1) * 256], True, False),
                                       reads=[("kt", kt * 128 // (S // nkp)), "qt"], writes=[("pst", p)])
                                S_.add("tensor", MM(pst[p], esel[:, n, :], biasT[:, :], False, d < 0),
                                       reads=["esel", ("biasT", i % 2, 0), ("biasT", i % 2, 1)], writes=[("pst", p)])
                                if d >= 0:
                                    for qs in range(2):
                                        S_.add("tensor", MM(pst[p][:, qs * 128:(qs + 1) * 128], ident[:], pmask[:, d * 4 + (kt % 2) * 2 + qs, :],
                                                            False, qs == 1), reads=["ident", "pmask"], writes=[("pst", p)])
                                S_.add("scalar", ACT(pT[p][:], pst[p], AF.Exp, scale=SCALE), reads=[("pst", p)], writes=[("pT", p)])
                            if step >= LA:
                                kt = step - LA
                                p = slots[kt]
                                for qs in range(2):
                                    S_.add("tensor", MM(pob[qs][:, 0:129], pT[p][:, qs * 128:(qs + 1) * 128], vx[:, kt, :], kt == 0, kt == nkt - 1),
                                           reads=[("pT", p), ("vx", kt // (NT // nkv)), "vx1"], writes=[("po", qs)])
                        os_ = i % 2
                        for qs in range(2):
                            S_.add("vector", RCP(rinv[:, qs:qs + 1], pob[qs][:, 128:129]), reads=[("po", qs)], writes=[("rinv", qs)])
                            S_.add("vector", TS(obf[:], pob[qs][:, 0:128], rinv[:, qs:qs + 1], None, ALU.mult), reads=[("po", qs), ("rinv", qs)],
                                   writes=["obf"])
                            S_.add("tensor", TR(ptb[:, 1 + qs, :], obf[:], ident[:]), reads=["obf", "ident"], writes=[("ptbo", qs)])
                            S_.add("scalar", ACT(oT[os_][:, qs * 128:(qs + 1) * 128], ptb[:, 1 + qs, :], AF.Copy), reads=[("ptbo", qs)],
                                   writes=[("oT", os_, qs)])
                        S_.dma("sync", OT[h][:, i * 256:(i + 1) * 256], oT[os_][:], reads=[("oT", os_, 0), ("oT", os_, 1)], key=("oT", os_))
                S_.finish()

        def phase_diff():
            with ExitStack() as es:
                S_ = Sched(nc, pool, "CD")
                sb = lambda n, s, d: es.enter_context(nc.sbuf_tensor(S_.name + "_" + n, s, d))
                ps = lambda n, s, d: es.enter_context(nc.psum_tensor(S_.name + "_" + n, s, d))
                c = common_consts(S_, es, False, False)
                attn_consts(S_, es, c)
                ident = c["ident"]
                pmask = c["pmask"]
                lq = sb("lq", [128, 4, 128], F32)
                S_.dma("sync", lq[:], bass.AP(lam4.tensor, 0, [[0, 128], [128, 4], [1, 128]]), writes=["lq"], key="lq")
                lj = sb("lj", [128, 128], F32)
                ld = sb("ld", [128, 4], F32)
                S_.add("vector", TT(lj[:], lq[:, 0, :], lq[:, 1, :], ALU.mult), reads=["lq"], writes=["lj"])
                S_.add("vector", lambda e: e.tensor_reduce(out=ld[:, 0:1], in_=lj[:], axis=AX.X, op=ALU.add), reads=["lj"], writes=["ld0"])
                S_.add("vector", TT(lj[:], lq[:, 2, :], lq[:, 3, :], ALU.mult), reads=["lq", "ld0"], writes=["lj"])
                S_.add("vector", lambda e: e.tensor_reduce(out=ld[:, 1:2], in_=lj[:], axis=AX.X, op=ALU.add), reads=["lj"], writes=["ld1"])
                S_.add("scalar", ACT(ld[:, 2:4], ld[:, 0:2], AF.Exp), reads=["ld0", "ld1"], writes=["ld2"])
                nlam = sb("nlam", [128, 1], F32)
                S_.add("vector", TT(nlam[:], ld[:, 3:4], ld[:, 2:3], ALU.subtract), reads=["ld2"], writes=["nlam"])
                S_.add("vector", TS(nlam[:], nlam[:], -0.2, None, ALU.add), reads=["nlam"], writes=["nlam"])
                sg = sb("sg", [128, 256], F32)
                S_.dma("sync", sg[:], bcast_row(subg, 256), writes=["sg"], key="sg")
                S_.add("vector", TS(sg[:], sg[:], 0.8, None, ALU.mult), reads=["sg"], writes=["sg"])
                kt_sb = [sb(f"kt_sb{s}", [128, S], BF16) for s in range(2)]
                qt_sb = [sb(f"qt_sb{s}", [128, TOWN], BF16) for s in range(2)]
                vx = sb("vx", [128, NT, 257], BF16)
                S_.add("vector", MS(vx[:, :, 256:257], 1.0), writes=["vx1"])
                pT = [sb(f"pT{i}", [128, 256], BF16) for i in range(4)]
                rinv = sb("rinv", [128, 4], F32)
                tq = sb("tq", [128, 256], F32)
                oq = sb("oq", [128, 256], F32)
                st = sb("stt", [128, 4], F32)
                junk = sb("junk", [128, 256], BF16)
                obf = sb("obf", [128, 256], BF16)
                oT = [sb(f"oT{i}", [128, 2, 256], BF16) for i in range(2)]
                pob = [[ps(f"po{s}{q}", [128, 512], F32) for q in range(2)] for s in range(2)]
                pstb = [ps(f"pst{i}", [128, 512], F32) for i in range(3)]
                pst = [pstb[i][:, 0:256] for i in range(3)]
                ptb = ps("ptb", [128, 8, 128], BF16)
                VSv = VS.rearrange("(t p) n -> p t n", p=128)
                npst = 0
                nkp = 4
                nkv = 8 if NT >= 64 else 4
                for hd in range(4):
                    for s in range(2):
                        for j in range(nkp):
                            a, b_ = j * (S // nkp), (j + 1) * (S // nkp)
                            S_.dma("sync", kt_sb[s][:, a:b_], KT[8 + 2 * hd + s][:, a:b_], writes=[("kt", s, j)], key=("kt", s, j))
                        S_.dma("sync", qt_sb[s][:], QT[8 + 2 * hd + s][:, :], writes=[("qt", s)], key=("qt", s))
                    for j in range(nkv):
                        a, b_ = j * (NT // nkv), (j + 1) * (NT // nkv)
                        S_.dma("sync", vx[:, a:b_, 0:256], VSv[:, a:b_, 1024 + hd * 256:1024 + (hd + 1) * 256], writes=[("vx", j)], key=("vx", j))
                    for i in range(NQB):
                        nkt = 8 * i + 8
                        LA = 2
                        slots = {}
                        nsteps = nkt * 2
                        for step in range(nsteps + LA):
                            if step < nsteps:
                                kt, s = step // 2, step % 2
                                d = kt // 2 - 4 * i
                                p = npst % 3
                                tq_ = npst % 4
                                npst += 1
                                slots[step] = tq_
                                S_.add("tensor", MM(pst[p], kt_sb[s][:, kt * 128:(kt + 1) * 128], qt_sb[s][:, i * 256:(i + 1) * 256], True, d < 0),
                                       reads=[("kt", s, kt * 128 // (S // nkp)), ("qt", s)], writes=[("pst", p)])
                                if d >= 0:
                                    for qs in range(2):
                                        S_.add("tensor", MM(pst[p][:, qs * 128:(qs + 1) * 128], ident[:], pmask[:, d * 4 + (kt % 2) * 2 + qs, :],
                                                            False, qs == 1), reads=["ident", "pmask"], writes=[("pst", p)])
                                S_.add("scalar", ACT(pT[tq_][:], pst[p], AF.Exp, scale=SCALE), reads=[("pst", p)], writes=[("pT", tq_)])
                            if step >= LA:
                                st2 = step - LA
                                kt, s = st2 // 2, st2 % 2
                                p = slots[st2]
                                for qs in range(2):
                                    S_.add("tensor", MM(pob[s][qs][:, 0:257], pT[p][:, qs * 128:(qs + 1) * 128], vx[:, kt, :], kt == 0, kt == nkt - 1),
                                           reads=[("pT", p), ("vx", kt // (NT // nkv)), "vx1"], writes=[("po", s, qs)])
                        os_ = i % 2
                        for qs in range(2):
                            S_.add("vector", RCP(rinv[:, 0:1], pob[0][qs][:, 256:257]), reads=[("po", 0, qs)], writes=["rinv0"])
                            S_.add("vector", RCP(rinv[:, 1:2], pob[1][qs][:, 256:257]), reads=[("po", 1, qs)], writes=["rinv1"])
                            S_.add("vector", TS(rinv[:, 2:3], rinv[:, 1:2], nlam[:, 0:1], None, ALU.mult), reads=["rinv1", "nlam"], writes=["rinv2"])
                            S_.add("vector", TS(tq[:], pob[1][qs][:, 0:256], rinv[:, 2:3], None, ALU.mult), reads=[("po", 1, qs), "rinv2"], writes=["tq"])
                            S_.add("vector", STT(oq[:], pob[0][qs][:, 0:256], rinv[:, 0:1], tq[:], ALU.mult, ALU.add),
                                   reads=[("po", 0, qs), "rinv0", "tq"], writes=["oq"])
                            norm_tile(S_, c, oq[:], "oq", sg[:], "sg", obf[:], "obf", st, "stt", nfeat=256)
                            for cc in range(2):
                                S_.add("tensor", TR(ptb[:, qs * 2 + cc, :], obf[:, cc * 128:(cc + 1) * 128], ident[:]), reads=["obf", "ident"],
                                       writes=[("ptbo", qs, cc)])
                                S_.add("scalar", ACT(oT[os_][:, cc, qs * 128:(qs + 1) * 128], ptb[:, qs * 2 + cc, :], AF.Copy),
                                       reads=[("ptbo", qs, cc)], writes=[("oT", os_, qs, cc)])
                        for cc in range(2):
                            S_.dma("sync", OT[8 + hd * 2 + cc][:, i * 256:(i + 1) * 256], oT[os_][:, cc, :],
                                   reads=[("oT", os_, 0, cc), ("oT", os_, 1, cc)], key=("oT", os_, cc))
                S_.finish()

        def phase_dense():
            with ExitStack() as es:
                S_ = Sched(nc, pool, "DD")
                sb = lambda n, s, d: es.enter_context(nc.sbuf_tensor(S_.name + "_" + n, s, d))
                ps = lambda n, s, d: es.enter_context(nc.psum_tensor(S_.name + "_" + n, s, d))
                c = common_consts(S_, es, False, False)
                R = sb("R", [128, 64, 512], BF16)
                yall = sb("yall", [128, 4, D], F32)
                uT = sb("uT", [128, NCH, 512], BF16)
                wb = [sb(f"wb{i}", [128, NCH, 512], BF16) for i in range(3)]
                ht = [sb(f"ht{i}", [128, D], F32) for i in range(2)]
                ub = sb("ub", [128, D], BF16)
                grep = sb("grep", [128, D], F32)
                st = [sb(f"st{i}", [128, 4], F32) for i in range(2)]
                rl = [sb(f"rl{i}", [128, 512], F32) for i in range(2)]
                pt_f = sb("pt_f", [128, 256], F32)
                pt_b = sb("pt_b", [128, 256], BF16)
                pTt = sb("pTt", [128, 2, 512], BF16)
                sgt = sb("sgt", [128, 512], F32)
                ptr = [ps(f"ptr{i}", [128, 8, 128], BF16) for i in range(2)]
                pk = [ps(f"pk{i}", [128, 512], F32) for i in range(6)]
                H1 = dscr("H1", [TOWN, D], F32)
                H2 = dscr("H2", [TOWN, D], F32)
                wcnt = [0]
                pkc = [0]

                def load_w(src_view, nchunks):
                    sl = wcnt[0] % 3
                    wcnt[0] += 1
                    S_.dma("gpsimd", wb[sl][:, 0:nchunks, :], src_view, writes=[("wb", sl)], key=("wb", sl))
                    return wb[sl], ("wb", sl)

                def load_g(idx):
                    S_.dma("sync", grep[:], bcast_row(gains, D, off=idx * D), writes=["grep"], key="grep")

                def next_pk():
                    p = pkc[0] % 6
                    pkc[0] += 1
                    return p

                def tok_major_layer(lhs_fn, lhs_keys, nk, wsrc_fn, evac_fn):
                    for cg in range(4):
                        nsl = (nk + NCH - 1) // NCH
                        if nsl == 1:
                            wt, wk = load_w(wsrc_fn(cg, 0, nk), nk)
                            for tt in range(4):
                                p = next_pk()
                                for k in range(nk):
                                    S_.add("tensor", MM(pk[p][:], lhs_fn(tt, k), wt[:, k, :], k == 0, k == nk - 1),
                                           reads=lhs_keys(tt) + [wk] if k in (0, nk - 1) else (), writes=[("pk", p)])
                                evac_fn(tt, cg, p)
                        else:
                            pp_ = [next_pk() for _ in range(4)]
                            for sl in range(nsl):
                                wt, wk = load_w(wsrc_fn(cg, sl * NCH, NCH), NCH)
                                for tt in range(4):
                                    for k in range(NCH):
                                        kk = sl * NCH + k
                                        S_.add("tensor", MM(pk[pp_[tt]][:], lhs_fn(tt, kk), wt[:, k, :], kk == 0, kk == nk - 1),
                                               reads=lhs_keys(tt) + [wk] if k in (0, NCH - 1) else (), writes=[("pk", pp_[tt])])
                            for tt in range(4):
                                evac_fn(tt, cg, pp_[tt])

                def post_norm_residual(G, hsrc, hdst, stage):
                    for tt in range(4):
                        t = G * 4 + tt
                        hs = tt % 2
                        S_.dma("sync", ht[hs][:], hsrc[t * 128:(t + 1) * 128, :], writes=[("ht", hs)], key=("ht", hs))
                        s_ = st[hs]
                        S_.add("scalar", ACT(ub[:], yall[:, tt, :], AF.Square, accum_out=s_[:, 0:1]), reads=[("yall", tt)], writes=["ub", ("st", hs)])
                        S_.add("scalar", ACT(s_[:, 1:2], s_[:, 0:1], AF.Sqrt, bias=c["eps"][:, 0:1], scale=1.0 / D), reads=[("st", hs), "epst"],
                               writes=[("st", hs)])
                        S_.add("vector", RCP(s_[:, 2:3], s_[:, 1:2]), reads=[("st", hs)], writes=[("st", hs)])
                        S_.add("vector", STT(yall[:, tt, :], yall[:, tt, :], s_[:, 2:3], grep[:], ALU.mult, ALU.mult),
                               reads=[("yall", tt), ("st", hs), "grep"], writes=[("yall", tt)])
                        S_.add("vector", TT(ht[hs][:], ht[hs][:], yall[:, tt, :], ALU.add), reads=[("ht", hs), ("yall", tt)], writes=[("ht", hs)])
                        yield tt, t, hs

                def prep_from_h(tt, hs, gidx_loaded):
                    norm_tile(S_, c, ht[hs][:], ("ht", hs), grep[:], "grep", ub[:], "ub", st[hs], ("st", hs))
                    transpose_tile(S_, c, ub, "ub", NCH, ptr, [("ptr", 0), ("ptr", 1)],
                                   lambda hf, cn, tt=tt: uT[:, hf * 8:hf * 8 + cn, tt * 128:(tt + 1) * 128], [("uT", tt)])

                uTk = [("uT", tt) for tt in range(4)]
                wg_v = w_g.rearrange("(c p) n -> p c n", p=128)
                wbm_v = w_bm.rearrange("(c p) n -> p c n", p=128)
                wbd_v = w_bd.rearrange("(c p) n -> p c n", p=128)
                wout_v = w_out.rearrange("(c p) n -> p c n", p=128)
                wup_v = w_up.rearrange("(c p) n -> p c n", p=128)
                wdn_v = w_down.rearrange("(c p) n -> p c n", p=128)
                wpg_v = w_pg.rearrange("(c p) n -> p c n", p=128)
                wpp_v = w_pp.rearrange("(c p) n -> p c n", p=128)
                for G in range(NGO):
                    tok = slice(G * 512, (G + 1) * 512)
                    load_g(0)
                    for tt in range(4):
                        t = G * 4 + tt
                        hs = tt % 2
                        S_.dma("sync", ht[hs][:], xo[t * 128:(t + 1) * 128, :], writes=[("ht", hs)], key=("ht", hs))
                        prep_from_h(tt, hs, 0)
                    for cg in range(8):
                        wt, wk = load_w(wg_v[:, :, cg * 512:(cg + 1) * 512], NCH)
                        for jj in range(4):
                            j = cg * 4 + jj
                            p = next_pk()
                            for k in range(NCH):
                                S_.add("tensor", MM(pk[p][:], wt[:, k, jj * 128:(jj + 1) * 128], uT[:, k, :], k == 0, k == NCH - 1),
                                       reads=uTk + [wk] if k in (0, NCH - 1) else (), writes=[("pk", p)])
                            S_.add("scalar", ACT(R[:, j, :], pk[p][:], AF.Sigmoid), reads=[("pk", p)], writes=[("R", j)])
                    S_.dma("sync", R[:, 48:64, :], OT[:, :, tok].rearrange("c p t -> p c t"), writes=[("R", 48 + k) for k in range(16)], key="oab")
                    for cg in range(4):
                        wa, wak = load_w(wbm_v[:, :, cg * 512:(cg + 1) * 512], 8)
                        wd, wdk = load_w(wbd_v[:, :, cg * 512:(cg + 1) * 512], 8)
                        for jj in range(4):
                            j = cg * 4 + jj
                            pa = next_pk()
                            pb = next_pk()
                            for k in range(8):
                                S_.add("tensor", MM(pk[pa][:], wa[:, k, jj * 128:(jj + 1) * 128], R[:, 48 + k, :], k == 0, k == 7),
                                       reads=[("R", 48 + kk) for kk in range(8)] + [wak] if k in (0, 7) else (), writes=[("pk", pa)])
                            for k in range(8):
                                S_.add("tensor", MM(pk[pb][:], wd[:, k, jj * 128:(jj + 1) * 128], R[:, 56 + k, :], k == 0, k == 7),
                                       reads=[("R", 56 + kk) for kk in range(8)] + [wdk] if k in (0, 7) else (), writes=[("pk", pb)])
                            r_ = rl[j % 2]
                            S_.add("vector", TT(r_[:], pk[pa][:], R[:, j, :], ALU.mult), reads=[("pk", pa), ("R", j)], writes=[("rl", j % 2)])
                            S_.add("vector", TT(sgt[:], pk[pb][:], R[:, 16 + j, :], ALU.mult), reads=[("pk", pb), ("R", 16 + j)], writes=["sgt"])
                            S_.add("vector", TT(R[:, 32 + j, :], r_[:], sgt[:], ALU.add), reads=[("rl", j % 2), "sgt"], writes=[("R", 32 + j)])
                    load_g(1)

                    def evac_y(tt, cg, p):
                        if cg % 2 == 0:
                            S_.add("scalar", ACT(yall[:, tt, cg * 512:(cg + 1) * 512], pk[p][:], AF.Copy), reads=[("pk", p)], writes=[("yall", tt)])
                        else:
                            S_.add("vector", CP(yall[:, tt, cg * 512:(cg + 1) * 512], pk[p][:]), reads=[("pk", p)], writes=[("yall", tt)])

                    tok_major_layer(lambda tt, k: R[:, 32 + k, tt * 128:(tt + 1) * 128], lambda tt: [("R", 32 + k) for k in range(16)], NCH,
                                    lambda cg, k0, nk: wout_v[:, k0:k0 + nk, cg * 512:(cg + 1) * 512], evac_y)
                    hts = []
                    for tt, t, hs in post_norm_residual(G, xo, None, 1):
                        S_.dma("sync", H1[t * 128:(t + 1) * 128, :], ht[hs][:], reads=[("ht", hs)], writes=[("H1", t)], key=("h1s", hs))
                        hts.append((tt, hs))
                        if tt == 0:
                            pass
                    load_g(2)
                    for tt in range(4):
                        t = G * 4 + tt
                        hs = tt % 2
                        S_.dma("sync", ht[hs][:], H1[t * 128:(t + 1) * 128, :], reads=[("H1", t)], writes=[("ht", hs)], key=("ht", hs))
                        prep_from_h(tt, hs, 2)
                    for cg in range(16):
                        wt, wk = load_w(wup_v[:, :, cg * 512:(cg + 1) * 512], NCH)
                        for jj in range(4):
                            j = cg * 4 + jj
                            p = next_pk()
                            for k in range(NCH):
                                S_.add("tensor", MM(pk[p][:], wt[:, k, jj * 128:(jj + 1) * 128], uT[:, k, :], k == 0, k == NCH - 1),
                                       reads=uTk + [wk] if k in (0, NCH - 1) else (), writes=[("pk", p)])
                            r_ = rl[j % 2]
                            S_.add("scalar", ACT(r_[:], pk[p][:], AF.Relu), reads=[("pk", p)], writes=[("rl", j % 2)])
                            S_.add("vector", STT(R[:, j, :], pk[p][:], 0.0, r_[:], ALU.max, ALU.mult), reads=[("pk", p), ("rl", j % 2)],
                                   writes=[("R", j)])
                    load_g(3)
                    tok_major_layer(lambda tt, k: R[:, k, tt * 128:(tt + 1) * 128], lambda tt: [("R", k) for k in range(64)], 64,
                                    lambda cg, k0, nk: wdn_v[:, k0:k0 + nk, cg * 512:(cg + 1) * 512], evac_y)
                    for tt, t, hs in post_norm_residual(G, H1, None, 3):
                        S_.dma("sync", H2[t * 128:(t + 1) * 128, :], ht[hs][:], reads=[("ht", hs)], writes=[("H2", t)], key=("h2s", hs))
                    load_g(4)
                    for tt in range(4):
                        t = G * 4 + tt
                        hs = tt % 2
                        S_.dma("sync", ht[hs][:], H2[t * 128:(t + 1) * 128, :], reads=[("H2", t)], writes=[("ht", hs)], key=("ht", hs))
                        prep_from_h(tt, hs, 4)
                        S_.dma("sync", pt_f[:], po[t * 128:(t + 1) * 128, :], writes=["pt_f"], key="pt_f")
                        S_.add("vector", CP(pt_b[:], pt_f[:]), reads=["pt_f"], writes=["pt_b"])
                        transpose_tile(S_, c, pt_b, "pt_b", 2, ptr, [("ptr", 0), ("ptr", 1)],
                                       lambda hf, cn, tt=tt: pTt[:, 0:cn, tt * 128:(tt + 1) * 128], [("pTt", tt)])
                    load_g(5)
                    for cg in range(4):
                        wt, wk = load_w(wpg_v[:, :, cg * 512:(cg + 1) * 512], NCH)
                        wp, wpk = load_w(wpp_v[:, :, cg * 512:(cg + 1) * 512], 2)
                        for tt in range(4):
                            pa = next_pk()
                            pb = next_pk()
                            for k in range(NCH):
                                S_.add("tensor", MM(pk[pa][:], uT[:, k, tt * 128:(tt + 1) * 128], wt[:, k, :], k == 0, k == NCH - 1),
                                       reads=[("uT", tt), wk] if k in (0, NCH - 1) else (), writes=[("pk", pa)])
                            for k in range(2):
                                S_.add("tensor", MM(pk[pb][:], pTt[:, k, tt * 128:(tt + 1) * 128], wp[:, k, :], k == 0, k == 1),
                                       reads=[("pTt", tt), wpk] if k in (0, 1) else (), writes=[("pk", pb)])
                            S_.add("scalar", ACT(sgt[:], pk[pa][:], AF.Sigmoid), reads=[("pk", pa)], writes=["sgt"])
                            S_.add("vector", TT(yall[:, tt, cg * 512:(cg + 1) * 512], pk[pb][:], sgt[:], ALU.mult), reads=[("pk", pb), "sgt"],
                                   writes=[("yall", tt)])
                    for tt, t, hs in post_norm_residual(G, H2, None, 5):
                        S_.dma("sync", out[t * 128:(t + 1) * 128, :], ht[hs][:], reads=[("ht", hs)], key=("outs", hs))
                S_.finish()

        phase_proj("AK", "KQ", xb, NT, w_k, KT)
        phase_proj("AV", "V", xb, NT, w_v, VS)
        phase_proj("B", "KQ", xo, NTO, w_q, QT)
        phase_moba()
        phase_diff()
        phase_dense()
    return nc


def _prep_inputs(S, x, p, w_in, w_br_moba, w_br_diff, w_out, lambda_q1, lambda_k1, lambda_q2, lambda_k2,
                 diff_subln_g, g_mix_pre, g_mix_post, w_up, w_down, g_mlp_pre, g_mlp_post,
                 w_ple_proj, w_ple_gate, g_ple_pre, g_ple_post):
    f = lambda a: np.ascontiguousarray(np.asarray(a, dtype=np.float32))
    x = np.asarray(x, dtype=np.float32)
    p = np.asarray(p, dtype=np.float32)
    w_in = np.asarray(w_in, dtype=np.float32)[0]
    NB = S // 256
    wq = f(np.concatenate([w_in[:, 0:1024], w_in[:, 3072:4096]], axis=1))
    wk = f(np.concatenate([w_in[:, 1024:2048], w_in[:, 4096:5120]], axis=1))
    wv = f(np.concatenate([w_in[:, 2048:3072], w_in[:, 5120:6144]], axis=1))
    wg = f(w_in[:, 6144:10240])
    shared = {
        "w_k": wk, "w_v": wv, "w_q": wq, "w_g": wg,
        "w_bm": f(w_br_moba[0]), "w_bd": f(w_br_diff[0]), "w_out": f(w_out[0]),
        "w_up": f(w_up[0]), "w_down": f(w_down[0]), "w_pp": f(w_ple_proj[0]), "w_pg": f(w_ple_gate[0]),
        "lam4": f(np.stack([np.asarray(lambda_q1)[0], np.asarray(lambda_k1)[0], np.asarray(lambda_q2)[0], np.asarray(lambda_k2)[0]])),
        "subg": f(np.asarray(diff_subln_g)[0:1]),
        "gains": f(np.stack([np.asarray(a)[0] for a in (g_mix_pre, g_mix_post, g_mlp_pre, g_mlp_post, g_ple_pre, g_ple_post)])),
    }
    in_maps = []
    for c in range(8):
        b, g = c // 4, c % 4
        xbb = f(x[b])
        xo = f(xbb.reshape(NB, 256, D)[g::4].reshape(-1, D))
        pob = f(p[0, b].reshape(NB, 256, 256)[g::4].reshape(-1, 256))
        m = dict(shared)
        m["xb"] = xbb
        m["xo"] = xo
        m["po"] = pob
        m["cinfo"] = np.array([[g, 256 * g, 0, 0]], dtype=np.float32)
        in_maps.append(m)
    return in_maps


_NC_CACHE = {}


def kernel(**inputs):
    S = int(np.asarray(inputs["x"]).shape[1])
    if S not in _NC_CACHE:
        _NC_CACHE[S] = build(S)
    nc = _NC_CACHE[S]
    in_maps = _prep_inputs(S, **inputs)
    res = run_bass_kernel_spmd(nc, in_maps, core_ids=list(range(8)))
    _LAST_RES[0] = res
    NB = S // 256
    outp = np.empty((2, S, D), dtype=np.float32)
    for c in range(8):
        b, g = c // 4, c % 4
        o = np.asarray(res.results[c]["out"], dtype=np.float32).reshape(NB // 4, 256, D)
        outp[b].reshape(NB, 256, D)[g::4] = o
    return outp
```

```python
import math
import numpy as np
from contextlib import ExitStack
import concourse.bass as bass
import concourse.mybir as mybir
from concourse.bass_utils import run_bass_kernel_spmd

F32 = mybir.dt.float32
BF16 = mybir.dt.bfloat16
I32 = mybir.dt.int32
AF = mybir.ActivationFunctionType
ALU = mybir.AluOpType
AX = mybir.AxisListType

SEQ = 16384
DEBUG_DUMP = False
_LAST_RES = [None]
D = 2048
NCH = 16
EPS = 1e-6
BIG = 30000.0
ROPE_THETA = 500000.0
ENGS = ("tensor", "vector", "scalar", "gpsimd", "sync")
EPOCH = 30000


class Op:
    __slots__ = ("eng", "emit", "deps", "dkey", "sig", "has_cons")

    def __init__(self, eng, emit, dkey):
        self.eng = eng
        self.emit = emit
        self.deps = set()
        self.dkey = dkey
        self.sig = None
        self.has_cons = False


class SemPool:
    def __init__(self, nc, es):
        self.nc = nc
        self.es = es
        self.eng_sems = {e: [] for e in ENGS}
        self.eng_cnt = {e: 0 for e in ENGS}
        self.dma_sems = {}
        self.dma_cnt = {}
        self.n = 0

    def _new(self):
        self.n += 1
        return self.es.enter_context(self.nc.semaphore(f"sm{self.n}"))

    def next_eng_sig(self, e):
        c = self.eng_cnt[e]
        ep, v = divmod(c, EPOCH)
        if ep >= len(self.eng_sems[e]):
            self.eng_sems[e].append(self._new())
        self.eng_cnt[e] = c + 1
        return (self.eng_sems[e][ep], v + 1)

    def next_dma_sig(self, key):
        c = self.dma_cnt.get(key, 0)
        ep, v = divmod(c, EPOCH // 16)
        lst = self.dma_sems.setdefault(key, [])
        if ep >= len(lst):
            lst.append(self._new())
        self.dma_cnt[key] = c + 1
        return (lst[ep], 16 * (v + 1))


class Sched:
    def __init__(self, nc, pool, name):
        self.nc = nc
        self.pool = pool
        self.name = name
        self.ops = []
        self.last_w = {}
        self.readers = {}

    def add(self, eng, emit, reads=(), writes=(), dma=None):
        op = Op(eng, emit, dma)
        deps = op.deps
        lw = self.last_w
        rd = self.readers
        for k in reads:
            w = lw.get(k)
            if w is not None:
                deps.add(w)
        for k in writes:
            w = lw.get(k)
            if w is not None:
                deps.add(w)
            r = rd.get(k)
            if r:
                deps.update(r)
        for k in reads:
            rd.setdefault(k, []).append(op)
        for k in writes:
            lw[k] = op
            rd[k] = []
        deps.discard(op)
        if eng == "tensor":
            op.deps = {d for d in deps if not (d.eng == "tensor" and d.dkey is None)}
        for d in op.deps:
            d.has_cons = True
        self.ops.append(op)
        return op

    def dma(self, q, out, in_, reads=(), writes=(), key=None, **kw):
        return self.add(q, lambda e: e.dma_start(out=out, in_=in_, **kw), reads, writes, dma=key)

    def finish(self):
        last_dma = {}
        for op in self.ops:
            if op.dkey is not None:
                last_dma[op.dkey] = op
        fin = Op("sync", None, None)
        fin.deps = set(last_dma.values())
        for d in fin.deps:
            d.has_cons = True
        self.ops.append(fin)
        pool = self.pool
        for op in self.ops:
            if op.dkey is not None:
                op.sig = pool.next_dma_sig(op.dkey)
            elif op.has_cons and op.emit is not None:
                op.sig = pool.next_eng_sig(op.eng)
        per_eng = {e: [] for e in ENGS}
        for op in self.ops:
            per_eng[op.eng].append(op)
        with self.nc.Block(self.name) as block:
            for e in ENGS:
                lst = per_eng[e]
                if not lst:
                    continue

                def body(eng, lst=lst):
                    seen = {}
                    for op in lst:
                        waits = {}
                        for d in op.deps:
                            if d.sig is None:
                                continue
                            s, v = d.sig
                            k = id(s)
                            if seen.get(k, 0) >= v:
                                continue
                            if k not in waits or waits[k][1] < v:
                                waits[k] = (s, v)
                        for k, (s, v) in waits.items():
                            eng.wait_ge(s, v)
                            seen[k] = v
                        if op.emit is None:
                            continue
                        ins = op.emit(eng)
                        if op.sig is not None:
                            ins.then_inc(op.sig[0], 16 if op.dkey is not None else 1)

                getattr(block, e)(body)


def MM(out, lhsT, rhs, start, stop):
    return lambda e: e.matmul(out, lhsT=lhsT, rhs=rhs, start=start, stop=stop)


def TR(out, in_, ident):
    return lambda e: e.transpose(out, in_, ident)


def ACT(out, in_, func, bias=None, scale=None, accum_out=None):
    kw = {}
    if bias is not None:
        kw["bias"] = bias
    if scale is not None:
        kw["scale"] = scale
    if accum_out is not None:
        kw["accum_out"] = accum_out
    return lambda e: e.activation(out=out, in_=in_, func=func, **kw)


def TS(out, in0, s1, s2, op0, op1=None):
    if op1 is None:
        return lambda e: e.tensor_scalar(out=out, in0=in0, scalar1=s1, scalar2=None, op0=op0)
    return lambda e: e.tensor_scalar(out=out, in0=in0, scalar1=s1, scalar2=s2, op0=op0, op1=op1)


def TT(out, in0, in1, op):
    return lambda e: e.tensor_tensor(out=out, in0=in0, in1=in1, op=op)


def STT(out, in0, scalar, in1, op0, op1):
    return lambda e: e.scalar_tensor_tensor(out=out, in0=in0, scalar=scalar, in1=in1, op0=op0, op1=op1)


def CP(out, in_):
    return lambda e: e.tensor_copy(out=out, in_=in_)


def RCP(out, in_):
    return lambda e: e.reciprocal(out=out, in_=in_)


def MS(ap, v):
    return lambda e: e.memset(ap, v)


def bcast_row(handle_ap, n, parts=128, off=0):
    return bass.AP(handle_ap.tensor, off, [[0, parts], [1, n]])


def build(S):
    NT = S // 128
    NB = S // 256
    NQB = NB // 4
    TOWN = S // 4
    NTO = TOWN // 128
    NGA = S // 512
    NGO = TOWN // 512
    SCALE = 1.0 / math.sqrt(128.0)

    nc = bass.Bass("TRN2", target_bir_lowering=False)
    din = lambda n, s: nc.dram_tensor(n, s, F32, kind="ExternalInput").ap()
    xb = din("xb", [S, D])
    xo = din("xo", [TOWN, D])
    po = din("po", [TOWN, 256])
    w_k = din("w_k", [D, 2048])
    w_v = din("w_v", [D, 2048])
    w_q = din("w_q", [D, 2048])
    w_g = din("w_g", [D, 4096])
    w_bm = din("w_bm", [1024, D])
    w_bd = din("w_bd", [1024, D])
    w_out = din("w_out", [D, D])
    w_up = din("w_up", [D, 8192])
    w_down = din("w_down", [8192, D])
    w_pp = din("w_pp", [256, D])
    w_pg = din("w_pg", [D, D])
    lam4 = din("lam4", [4, 128])
    subg = din("subg", [1, 256])
    gains = din("gains", [6, D])
    cinfo = din("cinfo", [1, 4])
    out = nc.dram_tensor("out", [TOWN, D], F32, kind="ExternalOutput").ap()

    dscr = lambda n, s, dt: nc.dram_tensor(n, s, dt, kind="Internal").ap()
    if DEBUG_DUMP:
        dscr = lambda n, s, dt: nc.dram_tensor(n, s, dt, kind="ExternalOutput" if n in ("KT", "VS", "QT", "OT") else "Internal").ap()
    KT = dscr("KT", [16, 128, S], BF16)
    VS = dscr("VS", [S, 2048], BF16)
    QT = dscr("QT", [16, 128, TOWN], BF16)
    OT = dscr("OT", [16, 128, TOWN], BF16)

    with ExitStack() as ges:
        pool = SemPool(nc, ges)

        def common_consts(S_, es, need_rope, own):
            sb = lambda n, s, d: es.enter_context(nc.sbuf_tensor(S_.name + "_" + n, s, d))
            c = {}
            ident = sb("ident", [128, 128], BF16)
            ones_bf = sb("ones_bf", [128, 128], BF16)
            S_.add("vector", MS(ones_bf[:], 1.0), writes=["ones_bf"])
            S_.add("gpsimd", lambda e: e.affine_select(out=ident[:], in_=ones_bf[:], pattern=[[-1, 128]],
                                                       compare_op=ALU.is_equal, fill=0.0, base=0, channel_multiplier=1),
                   reads=["ones_bf"], writes=["ident"])
            c["ident"] = ident
            epst = sb("epst", [128, 1], F32)
            S_.add("vector", MS(epst[:], EPS), writes=["epst"])
            c["eps"] = epst
            ci = sb("ci", [128, 4], F32)
            S_.dma("sync", ci[:], bcast_row(cinfo, 4), writes=["ci"], key="ci")
            c["ci"] = ci
            if not need_rope:
                return c
            pswA = sb("pswA", [32, 32], F32)
            pswB = sb("pswB", [32, 32], F32)
            psw = sb("psw", [32, 32], F32)
            ones_f = sb("ones_f", [32, 32], F32)
            S_.add("vector", MS(ones_f[:], 1.0), writes=["ones_f"])
            for t, base in ((pswA, -16), (pswB, 16)):
                S_.add("gpsimd", lambda e, t=t, base=base: e.affine_select(
                    out=t[:], in_=ones_f[:], pattern=[[-1, 32]], compare_op=ALU.is_equal, fill=0.0,
                    base=base, channel_multiplier=1), reads=["ones_f"], writes=[id(t)])
            S_.add("vector", TT(psw[:], pswA[:], pswB[:], ALU.add), reads=[id(pswA), id(pswB)], writes=["psw"])
            c["psw"] = psw
            ri = sb("ri", [32, 1], I32)
            rf = sb("rf", [32, 1], F32)
            r16 = sb("r16", [32, 1], F32)
            tmp = sb("rtmp", [32, 1], F32)
            invf = sb("invf", [32, 1], F32)
            sgn = sb("sgn", [32, 1], F32)
            S_.add("gpsimd", lambda e: e.iota(ri[:], pattern=[[0, 1]], base=0, channel_multiplier=1), writes=["ri"])
            S_.add("vector", CP(rf[:], ri[:]), reads=["ri"], writes=["rf"])
            S_.add("vector", TS(tmp[:], rf[:], 16.0, -16.0, ALU.is_ge, ALU.mult), reads=["rf"], writes=["rtmp"])
            S_.add("vector", TT(r16[:], rf[:], tmp[:], ALU.add), reads=["rf", "rtmp"], writes=["r16"])
            S_.add("vector", TS(sgn[:], rf[:], 16.0, 2.0, ALU.is_ge, ALU.mult), reads=["rf"], writes=["sgn"])
            S_.add("vector", TS(sgn[:], sgn[:], -1.0, None, ALU.add), reads=["sgn"], writes=["sgn"])
            S_.add("vector", MS(invf[:], 0.0), writes=["invf"])
            for j in range(16):
                cj = float(np.float32(1.0) / (np.float32(ROPE_THETA) ** (np.float32(j * 2.0) / np.float32(32.0))))
                S_.add("vector", TS(tmp[:], r16[:], float(j), cj, ALU.is_equal, ALU.mult), reads=["r16", "rtmp"], writes=["rtmp"])
                S_.add("vector", TT(invf[:], invf[:], tmp[:], ALU.add), reads=["invf", "rtmp"], writes=["invf"])
            c["invf"] = invf
            c["sgn"] = sgn
            ioi = sb("ioi", [32, 512], I32)
            iof = sb("iof", [32, 512], F32)
            if own:
                S_.add("gpsimd", lambda e: e.iota(ioi[:], pattern=[[1024, 2], [1, 256]], base=0, channel_multiplier=0), writes=["ioi"])
                S_.add("vector", CP(iof[:], ioi[:]), reads=["ioi"], writes=["iof"])
                S_.add("vector", TS(iof[:], iof[:], ci[0:32, 1:2], None, ALU.add), reads=["iof", "ci"], writes=["iof"])
            else:
                S_.add("gpsimd", lambda e: e.iota(ioi[:], pattern=[[1, 512]], base=0, channel_multiplier=0), writes=["ioi"])
                S_.add("vector", CP(iof[:], ioi[:]), reads=["ioi"], writes=["iof"])
            c["iof"] = iof
            return c

        def rope_tables(S_, rt, c, base, slot):
            ang, ki, kf, red, sa, ca, cosT, sinT = rt
            k = ("rt", slot)
            TWO_PI = 2.0 * math.pi
            c1 = 6.28125
            c2 = float(np.float32(TWO_PI - c1))
            c3 = float(np.float32(TWO_PI - c1 - float(np.float32(TWO_PI - c1))))
            S_.add("vector", TS(ang[:], c["iof"][:], float(base), c["invf"][:, 0:1], ALU.add, ALU.mult),
                   reads=["iof", "invf"], writes=[k + ("ang",)])
            S_.add("vector", TS(ki[:], ang[:], 1.0 / TWO_PI, None, ALU.mult), reads=[k + ("ang",)], writes=[k + ("ki",)])
            S_.add("vector", CP(kf[:], ki[:]), reads=[k + ("ki",)], writes=[k + ("kf",)])
            S_.add("vector", STT(red[:], kf[:], -c1, ang[:], ALU.mult, ALU.add), reads=[k + ("ang",), k + ("kf",)], writes=[k + ("red",)])
            S_.add("vector", STT(red[:], kf[:], -c2, red[:], ALU.mult, ALU.add), reads=[k + ("red",), k + ("kf",)], writes=[k + ("red",)])
            S_.add("vector", STT(red[:], kf[:], -c3, red[:], ALU.mult, ALU.add), reads=[k + ("red",), k + ("kf",)], writes=[k + ("red",)])
            PI = math.pi
            S_.add("vector", TS(sa[:], red[:], PI, -TWO_PI, ALU.is_gt, ALU.mult), reads=[k + ("red",)], writes=[k + ("sa",)])
            S_.add("vector", TT(sa[:], sa[:], red[:], ALU.add), reads=[k + ("red",), k + ("sa",)], writes=[k + ("sa",)])
            S_.add("vector", TS(sa[:], sa[:], -PI, PI, ALU.max, ALU.min), reads=[k + ("sa",)], writes=[k + ("sa",)])
            S_.add("vector", TS(ca[:], red[:], PI / 2, -TWO_PI, ALU.is_gt, ALU.mult), reads=[k + ("red",)], writes=[k + ("ca",)])
            S_.add("vector", STT(ca[:], red[:], PI / 2, ca[:], ALU.add, ALU.add), reads=[k + ("red",), k + ("ca",)], writes=[k + ("ca",)])
            S_.add("vector", TS(ca[:], ca[:], -PI, PI, ALU.max, ALU.min), reads=[k + ("ca",)], writes=[k + ("ca",)])
            S_.add("scalar", ACT(sinT[:], sa[:], AF.Sin), reads=[k + ("sa",)], writes=[k + ("sin",)])
            S_.add("scalar", ACT(cosT[:], ca[:], AF.Sin), reads=[k + ("ca",)], writes=[k + ("cos",)])
            S_.add("vector", TS(sinT[:], sinT[:], c["sgn"][:, 0:1], None, ALU.mult), reads=[k + ("sin",), "sgn"],
                   writes=[k + ("sin",)])
            return cosT, sinT, k + ("cos",), k + ("sin",)

        def norm_tile(S_, c, xt, xkey, grep, gkey, ub, ubkey, st, stkey, nfeat=D):
            S_.add("scalar", ACT(ub, xt, AF.Square, accum_out=st[:, 0:1]), reads=[xkey], writes=[ubkey, stkey])
            S_.add("scalar", ACT(st[:, 1:2], st[:, 0:1], AF.Sqrt, bias=c["eps"][:, 0:1], scale=1.0 / nfeat),
                   reads=[stkey, "epst"], writes=[stkey])
            S_.add("vector", RCP(st[:, 2:3], st[:, 1:2]), reads=[stkey], writes=[stkey])
            S_.add("vector", STT(ub, xt, st[:, 2:3], grep, ALU.mult, ALU.mult), reads=[xkey, stkey, gkey], writes=[ubkey])

        def transpose_tile(S_, c, ub, ubkey, nch, ptr, ptrkeys, dst_fn, dstkeys):
            nh = (nch + 7) // 8
            for hf in range(nh):
                cn = min(8, nch - hf * 8)
                for cc in range(cn):
                    ch = hf * 8 + cc
                    S_.add("tensor", TR(ptr[hf][:, cc, :], ub[:, ch * 128:(ch + 1) * 128], c["ident"][:]),
                           reads=[ubkey, "ident"], writes=[ptrkeys[hf]])
                eng = "scalar" if hf % 2 == 0 else "vector"
                if eng == "scalar":
                    S_.add("scalar", ACT(dst_fn(hf, cn), ptr[hf][:, 0:cn, :], AF.Copy), reads=[ptrkeys[hf]], writes=dstkeys)
                else:
                    S_.add("vector", CP(dst_fn(hf, cn), ptr[hf][:, 0:cn, :]), reads=[ptrkeys[hf]], writes=dstkeys)

        def phase_proj(name, kind, xsrc, ntiles, wsrc, dst):
            own = (name == "B")
            with ExitStack() as es:
                S_ = Sched(nc, pool, name)
                sb = lambda n, s, d: es.enter_context(nc.sbuf_tensor(S_.name + "_" + n, s, d))
                ps = lambda n, s, d: es.enter_context(nc.psum_tensor(S_.name + "_" + n, s, d))
                c = common_consts(S_, es, kind == "KQ", own)
                w = sb("w", [128, NCH, 2048], BF16)
                wv = wsrc.rearrange("(c p) n -> p c n", p=128)
                for j in range(4):
                    S_.dma("gpsimd", w[:, :, j * 512:(j + 1) * 512], wv[:, :, j * 512:(j + 1) * 512], writes=[("w", j)], key=("w", j))
                wkeys = [("w", j) for j in range(4)]
                grep = sb("grep", [128, D], F32)
                S_.dma("sync", grep[:], bcast_row(gains, D, off=0), writes=["grep"], key="grep")
                xt = [sb(f"xt{i}", [128, D], F32) for i in range(2)]
                ub = [sb(f"ub{i}", [128, D], BF16) for i in range(4)]
                st = [sb(f"st{i}", [128, 4], F32) for i in range(2)]
                uT = [sb(f"uT{i}", [128, NCH, 512], BF16) for i in range(2)]
                ptr = [ps(f"ptr{i}", [128, 8, 128], BF16) for i in range(2)]
                pk = [ps(f"pk{i}", [128, 512], F32) for i in range(3)]
                if kind == "KQ":
                    pp = [ps(f"pp{i}", [128, 512], F32) for i in range(2)]
                    rt = [[sb(f"rt{s}_{n}", [32, 512], I32 if n == 1 else F32) for n in range(8)] for s in range(2)]
                    r32 = [sb(f"r32_{i}", [32, 512], F32) for i in range(2)]
                    t1 = sb("t1", [32, 512], F32)
                    t2 = sb("t2", [32, 512], F32)
                    osb = [sb(f"osb{i}", [128, 512], BF16) for i in range(3)]
                else:
                    vsb = [sb(f"vsb{i}", [128, 2048], BF16) for i in range(2)]
                ngroups = ntiles // 4
                ropes = {}

                def prepA(G):
                    gs = G % 2
                    for tt in range(4):
                        t = G * 4 + tt
                        xs = t % 2
                        S_.dma("sync", xt[xs][:], xsrc[t * 128:(t + 1) * 128, :], writes=[("xt", xs)], key=("xt", xs))
                        norm_tile(S_, c, xt[xs][:], ("xt", xs), grep[:], "grep", ub[tt][:], ("ub", tt), st[xs], ("st", xs))
                    if kind == "KQ":
                        base = (2048 * G) if own else (512 * G)
                        ropes[G] = rope_tables(S_, rt[gs], c, base, gs)

                def prepB(G):
                    gs = G % 2
                    for tt in range(4):
                        transpose_tile(S_, c, ub[tt], ("ub", tt), NCH, ptr, [("ptr", 0), ("ptr", 1)],
                                       lambda hf, cn, gs=gs, tt=tt: uT[gs][:, hf * 8:hf * 8 + cn, tt * 128:(tt + 1) * 128],
                                       [("uT", gs, tt)])

                prepA(0)
                prepB(0)
                for G in range(ngroups):
                    gs = G % 2
                    uTk = [("uT", gs, tt) for tt in range(4)]
                    if kind == "KQ":
                        cosT, sinT, ck, sk = ropes.pop(G)

                        def fin(h, G=G, cosT=cosT, sinT=sinT, ck=ck, sk=sk):
                            r = h % 2
                            o = h % 3
                            S_.add("tensor", MM(pp[r][0:32, :], c["psw"][:, :], r32[r][:, :], True, True),
                                   reads=[("r32", r), "psw"], writes=[("pp", r)])
                            S_.add("vector", TT(t1[:], r32[r][:], cosT[:], ALU.mult), reads=[("r32", r), ck], writes=["t1"])
                            S_.add("vector", TT(t2[:], pp[r][0:32, :], sinT[:], ALU.mult), reads=[("pp", r), sk], writes=["t2"])
                            S_.add("vector", TT(osb[o][0:32, :], t1[:], t2[:], ALU.add), reads=["t1", "t2", ("osb", o, 0)], writes=[("osb", o, 0)])
                            S_.dma("sync", dst[h][:, G * 512:(G + 1) * 512], osb[o][:], reads=[("osb", o, 0)],
                                   key=("osb", o))

                        for h in range(16):
                            if G + 1 < ngroups:
                                if h == 3:
                                    prepA(G + 1)
                                elif h == 11:
                                    prepB(G + 1)
                            p = h % 3
                            for ch in range(NCH):
                                S_.add("tensor", MM(pk[p][:], w[:, ch, h * 128:(h + 1) * 128], uT[gs][:, ch, :], ch == 0, ch == NCH - 1),
                                       reads=uTk + [("w", h // 4)] if ch in (0, NCH - 1) else (), writes=[("pk", p)])
                            S_.add("scalar", ACT(osb[h % 3][:, :], pk[p][:, :], AF.Copy), reads=[("pk", p)],
                                   writes=[("osb", h % 3, 0)])
                            S_.add("scalar", ACT(r32[h % 2][:], pk[p][0:32, :], AF.Copy), reads=[("pk", p)], writes=[("r32", h % 2)])
                            if h >= 1:
                                fin(h - 1)
                        fin(15)
                    else:
                        for tt in range(4):
                            if G + 1 < ngroups:
                                if tt == 1:
                                    prepA(G + 1)
                                elif tt == 3:
                                    prepB(G + 1)
                            t = G * 4 + tt
                            vs = t % 2
                            for cg in range(4):
                                p = (tt * 4 + cg) % 3
                                for ch in range(NCH):
                                    S_.add("tensor", MM(pk[p][:], uT[gs][:, ch, tt * 128:(tt + 1) * 128], w[:, ch, cg * 512:(cg + 1) * 512],
                                                        ch == 0, ch == NCH - 1),
                                           reads=[("uT", gs, tt), ("w", cg)] if ch in (0, NCH - 1) else (), writes=[("pk", p)])
                                if cg % 2 == 0:
                                    S_.add("scalar", ACT(vsb[vs][:, cg * 512:(cg + 1) * 512], pk[p][:], AF.Copy), reads=[("pk", p)],
                                           writes=[("vsb", vs, cg)])
                                else:
                                    S_.add("vector", CP(vsb[vs][:, cg * 512:(cg + 1) * 512], pk[p][:]), reads=[("pk", p)],
                                           writes=[("vsb", vs, cg)])
                            S_.dma("sync", dst[t * 128:(t + 1) * 128, :], vsb[vs][:], reads=[("vsb", vs, cg) for cg in range(4)],
                                   key=("vsb", vs))
                S_.finish()

        def attn_consts(S_, es, c):
            sb = lambda n, s, d: es.enter_context(nc.sbuf_tensor(S_.name + "_" + n, s, d))
            ci = c["ci"]
            pmi = sb("pmi", [128, 128], I32)
            pmf = sb("pmf", [128, 128], F32)
            pmask = sb("pmask", [128, 16, 128], BF16)
            for d in range(4):
                for ktl in range(2):
                    for qs in range(2):
                        idx = d * 4 + ktl * 2 + qs
                        base = 256 * d + 128 * ktl - 128 * qs
                        S_.add("gpsimd", lambda e, base=base: e.iota(pmi[:], pattern=[[-1, 128]], base=base, channel_multiplier=1),
                               writes=["pmi"])
                        S_.add("vector", CP(pmf[:], pmi[:]), reads=["pmi"], writes=["pmf"])
                        S_.add("vector", TS(pmf[:], pmf[:], ci[:, 1:2], 0.0, ALU.subtract, ALU.is_gt), reads=["pmf", "ci"], writes=["pmf"])
                        S_.add("vector", TS(pmask[:, idx, :], pmf[:], -BIG, None, ALU.mult), reads=["pmf"], writes=["pmask"])
            c["pmask"] = pmask
            return c

        def phase_moba():
            with ExitStack() as es:
                S_ = Sched(nc, pool, "CM")
                sb = lambda n, s, d: es.enter_context(nc.sbuf_tensor(S_.name + "_" + n, s, d))
                ps = lambda n, s, d: es.enter_context(nc.psum_tensor(S_.name + "_" + n, s, d))
                c = common_consts(S_, es, False, False)
                attn_consts(S_, es, c)
                ci = c["ci"]
                ident = c["ident"]
                pmask = c["pmask"]
                esel = sb("esel", [128, NB, 128], BF16)
                eones = sb("eones", [128, NB, 128], BF16)
                S_.add("vector", MS(eones[:], 1.0), writes=["eones"])
                S_.add("gpsimd", lambda e: e.affine_select(out=esel[:], in_=eones[:], pattern=[[-1, NB], [0, 128]],
                                                           compare_op=ALU.is_equal, fill=0.0, base=0, channel_multiplier=1),
                       reads=["eones"], writes=["esel"])
                ni = sb("ni", [128, NB], I32)
                nf = sb("nf", [128, NB], F32)
                vv = sb("vv", [128, NB], F32)
                gbA = sb("gbA", [128, NQB, NB], F32)
                ownb = sb("ownb", [128, NQB, NB], F32)
                futb = sb("futb", [128, NQB, NB], F32)
                S_.add("gpsimd", lambda e: e.iota(ni[:], pattern=[[1, NB]], base=0, channel_multiplier=0), writes=["ni"])
                S_.add("vector", CP(nf[:], ni[:]), reads=["ni"], writes=["nf"])
                for i in range(NQB):
                    S_.add("vector", TS(vv[:], nf[:], ci[:, 0:1], float(4 * i), ALU.subtract, ALU.subtract), reads=["nf", "ci"], writes=["vv"])
                    S_.add("vector", TS(gbA[:, i, :], vv[:], 0.0, -BIG, ALU.is_ge, ALU.mult), reads=["vv"], writes=["gconst"])
                    S_.add("vector", TS(ownb[:, i, :], vv[:], 0.0, 1.0, ALU.is_equal, ALU.subtract), reads=["vv"], writes=["gconst"])
                    S_.add("vector", TS(ownb[:, i, :], ownb[:, i, :], BIG, None, ALU.mult), reads=["gconst"], writes=["gconst"])
                    S_.add("vector", TS(futb[:, i, :], vv[:], 0.0, -BIG, ALU.is_gt, ALU.mult), reads=["vv"], writes=["gconst"])
                kt_sb = sb("kt_sb", [128, S], BF16)
                vx = sb("vx", [128, NT, 129], BF16)
                qt_sb = sb("qt_sb", [128, TOWN], BF16)
                S_.add("vector", MS(vx[:, :, 128:129], 1.0), writes=["vx1"])
                km = sb("km", [128, NB], F32)
                kmT = sb("kmT", [128, NB], BF16)
                gm = [sb(f"gm{q}", [128, NB], F32) for q in range(2)]
                mx8 = [sb(f"mx8{q}", [128, 8], F32) for q in range(2)]
                sel = [sb(f"sel{q}", [128, NB], F32) for q in range(2)]
                fb = [sb(f"fb{q}", [128, NB], BF16) for q in range(2)]
                biasTb = [sb(f"biasT{q}", [128, 256], BF16) for q in range(2)]
                for q in range(2):
                    S_.add("vector", MS(biasTb[q][:], 0.0), writes=[("biasT", q, 0), ("biasT", q, 1)])

                def gateA(i, qs):
                    q0 = i * 256 + qs * 128
                    S_.add("tensor", MM(pg[:, 0:NB], qt_sb[:, q0:q0 + 128], kmT[:, :], True, True), reads=["qt", "kmT"], writes=["pg"])
                    S_.add("vector", TT(gm[qs][:], pg[:, 0:NB], gbA[:, i, :], ALU.add), reads=["pg", "gconst"], writes=[("gm", qs)])
                    S_.add("vector", lambda e: e.max(out=mx8[qs][:], in_=gm[qs][:]), reads=[("gm", qs)], writes=[("mx8", qs)])
                    S_.add("vector", TS(sel[qs][:], gm[qs][:], mx8[qs][:, 2:3], 1.0, ALU.is_ge, ALU.subtract), reads=[("gm", qs), ("mx8", qs)],
                           writes=[("sel", qs)])
                    S_.add("vector", STT(sel[qs][:], sel[qs][:], BIG, ownb[:, i, :], ALU.mult, ALU.max), reads=[("sel", qs), "gconst"],
                           writes=[("sel", qs)])
                    S_.add("vector", TT(fb[qs][:], sel[qs][:], futb[:, i, :], ALU.min), reads=[("sel", qs), "gconst"], writes=[("fb", qs)])

                def gateB(i, qs):
                    b_ = i % 2
                    S_.add("tensor", TR(ptb[0:NB, 0, :], fb[qs][:, :], ident[:]), reads=[("fb", qs), "ident"], writes=["ptb"])
                    S_.add("scalar", ACT(biasTb[b_][0:NB, qs * 128:(qs + 1) * 128], ptb[0:NB, 0, :], AF.Copy), reads=["ptb"],
                           writes=[("biasT", b_, qs)])
                pT = [sb(f"pT{i}", [128, 256], BF16) for i in range(4)]
                rinv = sb("rinv", [128, 2], F32)
                obf2 = [sb(f"obf{i}", [128, 128], BF16) for i in range(2)]
                oT = [sb(f"oT{i}", [128, 256], BF16) for i in range(2)]
                pob = [ps(f"po{i}", [128, 512], F32) for i in range(2)]
                pending = [None]

                def finB(h_, i_, qs):
                    os_ = i_ % 2
                    S_.add("tensor", TR(ptb[:, 1 + qs, :], obf2[qs][:], ident[:]), reads=[("obf", qs), "ident"], writes=[("ptbo", qs)])
                    S_.add("scalar", ACT(oT[os_][:, qs * 128:(qs + 1) * 128], ptb[:, 1 + qs, :], AF.Copy), reads=[("ptbo", qs)],
                           writes=[("oT", os_, qs)])
                    if qs == 1:
                        S_.dma("sync", OT[h_][:, i_ * 256:(i_ + 1) * 256], oT[os_][:], reads=[("oT", os_, 0), ("oT", os_, 1)], key=("oT", os_))
                pstb = [ps(f"pst{i}", [128, 512], F32) for i in range(4)]
                pst = [pstb[i][:, 0:256] for i in range(4)]
                pg = ps("pg", [128, 512], F32)
                ptb = ps("ptb", [128, 8, 128], BF16)
                VSv = VS.rearrange("(t p) n -> p t n", p=128)
                npst = 0
                nkv = 8 if NT >= 64 else 4
                for h in range(8):
                    nkp = 4
                    for j in range(nkp):
                        a, b_ = j * (S // nkp), (j + 1) * (S // nkp)
                        S_.dma("sync", kt_sb[:, a:b_], KT[h][:, a:b_], writes=[("kt", j)], key=("kt", j))
                    for j in range(nkv):
                        a, b_ = j * (NT // nkv), (j + 1) * (NT // nkv)
                        S_.dma("sync", vx[:, a:b_, 0:128], VSv[:, a:b_, h * 128:(h + 1) * 128], writes=[("vx", j)], key=("vx", j))
                    S_.dma("sync", qt_sb[:], QT[h][:, :], writes=["qt"], key="qt")
                    ktk = [("kt", j) for j in range(nkp)]

                    S_.add("vector", lambda e: e.tensor_reduce(out=km[:], in_=kt_sb[:].rearrange("p (n k) -> p n k", k=256), axis=AX.X, op=ALU.add),
                           reads=ktk, writes=["km"])
                    S_.add("vector", TS(kmT[:], km[:], 1.0 / 256.0, None, ALU.mult), reads=["km"], writes=["kmT"])
                    for i in range(NQB):
                        if i == 0:
                            for qs in range(2):
                                gateA(0, qs)
                                gateB(0, qs)
                        biasT = biasTb[i % 2]
                        nkt = 8 * i + 8
                        LA = 2
                        slots = {}
                        for step in range(nkt + LA):
                            if pending[0] is not None and step in (3, 5):
                                finB(pending[0][0], pending[0][1], 0 if step == 3 else 1)
                                if step == 5:
                                    pending[0] = None
                            if i + 1 < NQB:
                                if step == 0:
                                    gateA(i + 1, 0)
                                elif step == 2:
                                    gateA(i + 1, 1)
                                elif step == 4:
                                    gateB(i + 1, 0)
                                elif step == 6:
                                    gateB(i + 1, 1)
                            if step < nkt:
                                kt = step
                                n = kt // 2
                                d = n - 4 * i
                                p = npst % 4
                                npst += 1
                                slots[kt] = p
                                S_.add("tensor", MM(pst[p], kt_sb[:, kt * 128:(kt + 1) * 128], qt_sb[:, i * 256:(i + 1) * 256], True, False),
                                       reads=[("kt", kt * 128 // (S // nkp)), "qt"], writes=[("pst", p)])
                                S_.add("tensor", MM(pst[p], esel[:, n, :], biasT[:, :], False, d < 0),
                                       reads=["esel", ("biasT", i % 2, 0), ("biasT", i % 2, 1)], writes=[("pst", p)])
                                if d >= 0:
                                    for qs in range(2):
                                        S_.add("tensor", MM(pst[p][:, qs * 128:(qs + 1) * 128], ident[:], pmask[:, d * 4 + (kt % 2) * 2 + qs, :],
                                                            False, qs == 1), reads=["ident", "pmask"], writes=[("pst", p)])
                                S_.add("scalar", ACT(pT[p][:], pst[p], AF.Exp, scale=SCALE), reads=[("pst", p)], writes=[("pT", p)])
                            if step >= LA:
                                kt = step - LA
                                p = slots[kt]
                                for qs in range(2):
                                    S_.add("tensor", MM(pob[qs][:, 0:129], pT[p][:, qs * 128:(qs + 1) * 128], vx[:, kt, :], kt == 0, kt == nkt - 1),
                                           reads=[("pT", p), ("vx", kt // (NT // nkv)), "vx1"], writes=[("po", qs)])
                        for qs in range(2):
                            S_.add("vector", RCP(rinv[:, qs:qs + 1], pob[qs][:, 128:129]), reads=[("po", qs)], writes=[("rinv", qs)])
                            S_.add("vector", TS(obf2[qs][:], pob[qs][:, 0:128], rinv[:, qs:qs + 1], None, ALU.mult), reads=[("po", qs), ("rinv", qs)],
                                   writes=[("obf", qs)])
                        if i == NQB - 1:
                            finB(h, i, 0)
                            finB(h, i, 1)
                        else:
                            pending[0] = (h, i)
                S_.finish()

        def phase_diff():
            with ExitStack() as es:
                S_ = Sched(nc, pool, "CD")
                sb = lambda n, s, d: es.enter_context(nc.sbuf_tensor(S_.name + "_" + n, s, d))
                ps = lambda n, s, d: es.enter_context(nc.psum_tensor(S_.name + "_" + n, s, d))
                c = common_consts(S_, es, False, False)
                attn_consts(S_, es, c)
                ident = c["ident"]
                pmask = c["pmask"]
                lq = sb("lq", [128, 4, 128], F32)
                S_.dma("sync", lq[:], bass.AP(lam4.tensor, 0, [[0, 128], [128, 4], [1, 128]]), writes=["lq"], key="lq")
                lj = sb("lj", [128, 128], F32)
                ld = sb("ld", [128, 4], F32)
                S_.add("vector", TT(lj[:], lq[:, 0, :], lq[:, 1, :], ALU.mult), reads=["lq"], writes=["lj"])
                S_.add("vector", lambda e: e.tensor_reduce(out=ld[:, 0:1], in_=lj[:], axis=AX.X, op=ALU.add), reads=["lj"], writes=["ld0"])
                S_.add("vector", TT(lj[:], lq[:, 2, :], lq[:, 3, :], ALU.mult), reads=["lq", "ld0"], writes=["lj"])
                S_.add("vector", lambda e: e.tensor_reduce(out=ld[:, 1:2], in_=lj[:], axis=AX.X, op=ALU.add), reads=["lj"], writes=["ld1"])
                S_.add("scalar", ACT(ld[:, 2:4], ld[:, 0:2], AF.Exp), reads=["ld0", "ld1"], writes=["ld2"])
                nlam = sb("nlam", [128, 1], F32)
                S_.add("vector", TT(nlam[:], ld[:, 3:4], ld[:, 2:3], ALU.subtract), reads=["ld2"], writes=["nlam"])
                S_.add("vector", TS(nlam[:], nlam[:], -0.2, None, ALU.add), reads=["nlam"], writes=["nlam"])
                sg = sb("sg", [128, 256], F32)
                S_.dma("sync", sg[:], bcast_row(subg, 256), writes=["sg"], key="sg")
                S_.add("vector", TS(sg[:], sg[:], 0.8, None, ALU.mult), reads=["sg"], writes=["sg"])
                kt_sb = [sb(f"kt_sb{s}", [128, S], BF16) for s in range(2)]
                qt_sb = [sb(f"qt_sb{s}", [128, TOWN], BF16) for s in range(2)]
                vx = sb("vx", [128, NT, 257], BF16)
                S_.add("vector", MS(vx[:, :, 256:257], 1.0), writes=["vx1"])
                pT = [sb(f"pT{i}", [128, 256], BF16) for i in range(4)]
                rinv = sb("rinv", [128, 4], F32)
                tq = sb("tq", [128, 256], F32)
                oq = sb("oq", [128, 256], F32)
                st = sb("stt", [128, 4], F32)
                junk = sb("junk", [128, 256], BF16)
                obf2 = [sb(f"obf{i}", [128, 256], BF16) for i in range(2)]
                oT = [sb(f"oT{i}", [128, 2, 256], BF16) for i in range(2)]
                pending = [None]

                def finB(hd_, i_, qs):
                    os_ = i_ % 2
                    for cc in range(2):
                        S_.add("tensor", TR(ptb[:, qs * 2 + cc, :], obf2[qs][:, cc * 128:(cc + 1) * 128], ident[:]), reads=[("obf", qs), "ident"],
                               writes=[("ptbo", qs, cc)])
                        S_.add("scalar", ACT(oT[os_][:, cc, qs * 128:(qs + 1) * 128], ptb[:, qs * 2 + cc, :], AF.Copy),
                               reads=[("ptbo", qs, cc)], writes=[("oT", os_, qs, cc)])
                    if qs == 1:
                        for cc in range(2):
                            S_.dma("sync", OT[8 + hd_ * 2 + cc][:, i_ * 256:(i_ + 1) * 256], oT[os_][:, cc, :],
                                   reads=[("oT", os_, 0, cc), ("oT", os_, 1, cc)], key=("oT", os_, cc))
                pob = [[ps(f"po{s}{q}", [128, 512], F32) for q in range(2)] for s in range(2)]
                pstb = [ps(f"pst{i}", [128, 512], F32) for i in range(3)]
                pst = [pstb[i][:, 0:256] for i in range(3)]
                ptb = ps("ptb", [128, 8, 128], BF16)
                VSv = VS.rearrange("(t p) n -> p t n", p=128)
                npst = 0
                nkp = 4
                nkv = 8 if NT >= 64 else 4
                for hd in range(4):
                    for s in range(2):
                        for j in range(nkp):
                            a, b_ = j * (S // nkp), (j + 1) * (S // nkp)
                            S_.dma("sync", kt_sb[s][:, a:b_], KT[8 + 2 * hd + s][:, a:b_], writes=[("kt", s, j)], key=("kt", s, j))
                        S_.dma("sync", qt_sb[s][:], QT[8 + 2 * hd + s][:, :], writes=[("qt", s)], key=("qt", s))
                    for j in range(nkv):
                        a, b_ = j * (NT // nkv), (j + 1) * (NT // nkv)
                        S_.dma("sync", vx[:, a:b_, 0:256], VSv[:, a:b_, 1024 + hd * 256:1024 + (hd + 1) * 256], writes=[("vx", j)], key=("vx", j))
                    for i in range(NQB):
                        nkt = 8 * i + 8
                        LA = 2
                        slots = {}
                        nsteps = nkt * 2
                        for step in range(nsteps + LA):
                            if pending[0] is not None and step in (3, 5):
                                finB(pending[0][0], pending[0][1], 0 if step == 3 else 1)
                                if step == 5:
                                    pending[0] = None
                            if step < nsteps:
                                kt, s = step // 2, step % 2
                                d = kt // 2 - 4 * i
                                p = npst % 3
                                tq_ = npst % 4
                                npst += 1
                                slots[step] = tq_
                                S_.add("tensor", MM(pst[p], kt_sb[s][:, kt * 128:(kt + 1) * 128], qt_sb[s][:, i * 256:(i + 1) * 256], True, d < 0),
                                       reads=[("kt", s, kt * 128 // (S // nkp)), ("qt", s)], writes=[("pst", p)])
                                if d >= 0:
                                    for qs in range(2):
                                        S_.add("tensor", MM(pst[p][:, qs * 128:(qs + 1) * 128], ident[:], pmask[:, d * 4 + (kt % 2) * 2 + qs, :],
                                                            False, qs == 1), reads=["ident", "pmask"], writes=[("pst", p)])
                                S_.add("scalar", ACT(pT[tq_][:], pst[p], AF.Exp, scale=SCALE), reads=[("pst", p)], writes=[("pT", tq_)])
                            if step >= LA:
                                st2 = step - LA
                                kt, s = st2 // 2, st2 % 2
                                p = slots[st2]
                                for qs in range(2):
                                    S_.add("tensor", MM(pob[s][qs][:, 0:257], pT[p][:, qs * 128:(qs + 1) * 128], vx[:, kt, :], kt == 0, kt == nkt - 1),
                                           reads=[("pT", p), ("vx", kt // (NT // nkv)), "vx1"], writes=[("po", s, qs)])
                        os_ = i % 2
                        for qs in range(2):
                            S_.add("vector", RCP(rinv[:, 0:1], pob[0][qs][:, 256:257]), reads=[("po", 0, qs)], writes=["rinv0"])
                            S_.add("vector", RCP(rinv[:, 1:2], pob[1][qs][:, 256:257]), reads=[("po", 1, qs)], writes=["rinv1"])
                            S_.add("vector", TS(rinv[:, 2:3], rinv[:, 1:2], nlam[:, 0:1], None, ALU.mult), reads=["rinv1", "nlam"], writes=["rinv2"])
                            S_.add("vector", TS(tq[:], pob[1][qs][:, 0:256], rinv[:, 2:3], None, ALU.mult), reads=[("po", 1, qs), "rinv2"], writes=["tq"])
                            S_.add("vector", STT(oq[:], pob[0][qs][:, 0:256], rinv[:, 0:1], tq[:], ALU.mult, ALU.add),
                                   reads=[("po", 0, qs), "rinv0", "tq"], writes=["oq"])
                            norm_tile(S_, c, oq[:], "oq", sg[:], "sg", obf2[qs][:], ("obf", qs), st, "stt", nfeat=256)
                        if i == NQB - 1:
                            finB(hd, i, 0)
                            finB(hd, i, 1)
                        else:
                            pending[0] = (hd, i)
                S_.finish()

        def phase_dense():
            with ExitStack() as es:
                S_ = Sched(nc, pool, "DD")
                sb = lambda n, s, d: es.enter_context(nc.sbuf_tensor(S_.name + "_" + n, s, d))
                ps = lambda n, s, d: es.enter_context(nc.psum_tensor(S_.name + "_" + n, s, d))
                c = common_consts(S_, es, False, False)
                R = sb("R", [128, 64, 512], BF16)
                yall = sb("yall", [128, 4, D], F32)
                uT = sb("uT", [128, NCH, 512], BF16)
                wb = [sb(f"wb{i}", [128, NCH, 512], BF16) for i in range(3)]
                ht = [sb(f"ht{i}", [128, D], F32) for i in range(2)]
                ub = sb("ub", [128, D], BF16)
                grep = sb("grep", [128, D], F32)
                st = [sb(f"st{i}", [128, 4], F32) for i in range(2)]
                rl = [sb(f"rl{i}", [128, 512], F32) for i in range(2)]
                pt_f = sb("pt_f", [128, 256], F32)
                pt_b = sb("pt_b", [128, 256], BF16)
                pTt = sb("pTt", [128, 2, 512], BF16)
                sgt = sb("sgt", [128, 512], F32)
                ptr = [ps(f"ptr{i}", [128, 8, 128], BF16) for i in range(2)]
                pk = [ps(f"pk{i}", [128, 512], F32) for i in range(6)]
                H1 = dscr("H1", [TOWN, D], F32)
                H2 = dscr("H2", [TOWN, D], F32)
                wcnt = [0]
                pkc = [0]

                def load_w(src_view, nchunks):
                    sl = wcnt[0] % 3
                    wcnt[0] += 1
                    S_.dma("gpsimd", wb[sl][:, 0:nchunks, :], src_view, writes=[("wb", sl)], key=("wb", sl))
                    return wb[sl], ("wb", sl)

                def load_g(idx):
                    S_.dma("sync", grep[:], bcast_row(gains, D, off=idx * D), writes=["grep"], key="grep")

                def next_pk():
                    p = pkc[0] % 6
                    pkc[0] += 1
                    return p

                def tok_major_layer(lhs_fn, lhs_keys, nk, wsrc_fn, evac_fn):
                    for cg in range(4):
                        nsl = (nk + NCH - 1) // NCH
                        if nsl == 1:
                            wt, wk = load_w(wsrc_fn(cg, 0, nk), nk)
                            for tt in range(4):
                                p = next_pk()
                                for k in range(nk):
                                    S_.add("tensor", MM(pk[p][:], lhs_fn(tt, k), wt[:, k, :], k == 0, k == nk - 1),
                                           reads=lhs_keys(tt) + [wk] if k in (0, nk - 1) else (), writes=[("pk", p)])
                                evac_fn(tt, cg, p)
                        else:
                            pp_ = [next_pk() for _ in range(4)]
                            for sl in range(nsl):
                                wt, wk = load_w(wsrc_fn(cg, sl * NCH, NCH), NCH)
                                for tt in range(4):
                                    for k in range(NCH):
                                        kk = sl * NCH + k
                                        S_.add("tensor", MM(pk[pp_[tt]][:], lhs_fn(tt, kk), wt[:, k, :], kk == 0, kk == nk - 1),
                                               reads=lhs_keys(tt) + [wk] if k in (0, NCH - 1) else (), writes=[("pk", pp_[tt])])
                            for tt in range(4):
                                evac_fn(tt, cg, pp_[tt])

                def post_norm_residual(G, hsrc, hdst, stage):
                    for tt in range(4):
                        t = G * 4 + tt
                        hs = tt % 2
                        S_.dma("sync", ht[hs][:], hsrc[t * 128:(t + 1) * 128, :], writes=[("ht", hs)], key=("ht", hs))
                        s_ = st[hs]
                        S_.add("scalar", ACT(ub[:], yall[:, tt, :], AF.Square, accum_out=s_[:, 0:1]), reads=[("yall", tt)], writes=["ub", ("st", hs)])
                        S_.add("scalar", ACT(s_[:, 1:2], s_[:, 0:1], AF.Sqrt, bias=c["eps"][:, 0:1], scale=1.0 / D), reads=[("st", hs), "epst"],
                               writes=[("st", hs)])
                        S_.add("vector", RCP(s_[:, 2:3], s_[:, 1:2]), reads=[("st", hs)], writes=[("st", hs)])
                        S_.add("vector", STT(yall[:, tt, :], yall[:, tt, :], s_[:, 2:3], grep[:], ALU.mult, ALU.mult),
                               reads=[("yall", tt), ("st", hs), "grep"], writes=[("yall", tt)])
                        S_.add("vector", TT(ht[hs][:], ht[hs][:], yall[:, tt, :], ALU.add), reads=[("ht", hs), ("yall", tt)], writes=[("ht", hs)])
                        yield tt, t, hs

                def prep_from_h(tt, hs, gidx_loaded):
                    norm_tile(S_, c, ht[hs][:], ("ht", hs), grep[:], "grep", ub[:], "ub", st[hs], ("st", hs))
                    transpose_tile(S_, c, ub, "ub", NCH, ptr, [("ptr", 0), ("ptr", 1)],
                                   lambda hf, cn, tt=tt: uT[:, hf * 8:hf * 8 + cn, tt * 128:(tt + 1) * 128], [("uT", tt)])

                uTk = [("uT", tt) for tt in range(4)]
                wg_v = w_g.rearrange("(c p) n -> p c n", p=128)
                wbm_v = w_bm.rearrange("(c p) n -> p c n", p=128)
                wbd_v = w_bd.rearrange("(c p) n -> p c n", p=128)
                wout_v = w_out.rearrange("(c p) n -> p c n", p=128)
                wup_v = w_up.rearrange("(c p) n -> p c n", p=128)
                wdn_v = w_down.rearrange("(c p) n -> p c n", p=128)
                wpg_v = w_pg.rearrange("(c p) n -> p c n", p=128)
                wpp_v = w_pp.rearrange("(c p) n -> p c n", p=128)
                for G in range(NGO):
                    tok = slice(G * 512, (G + 1) * 512)
                    load_g(0)
                    for tt in range(4):
                        t = G * 4 + tt
                        hs = tt % 2
                        S_.dma("sync", ht[hs][:], xo[t * 128:(t + 1) * 128, :], writes=[("ht", hs)], key=("ht", hs))
                        prep_from_h(tt, hs, 0)
                    for cg in range(8):
                        wt, wk = load_w(wg_v[:, :, cg * 512:(cg + 1) * 512], NCH)
                        for jj in range(4):
                            j = cg * 4 + jj
                            p = next_pk()
                            for k in range(NCH):
                                S_.add("tensor", MM(pk[p][:], wt[:, k, jj * 128:(jj + 1) * 128], uT[:, k, :], k == 0, k == NCH - 1),
                                       reads=uTk + [wk] if k in (0, NCH - 1) else (), writes=[("pk", p)])
                            S_.add("scalar", ACT(R[:, j, :], pk[p][:], AF.Sigmoid), reads=[("pk", p)], writes=[("R", j)])
                    S_.dma("sync", R[:, 48:64, :], OT[:, :, tok].rearrange("c p t -> p c t"), writes=[("R", 48 + k) for k in range(16)], key="oab")
                    for cg in range(4):
                        wa, wak = load_w(wbm_v[:, :, cg * 512:(cg + 1) * 512], 8)
                        wd, wdk = load_w(wbd_v[:, :, cg * 512:(cg + 1) * 512], 8)
                        for jj in range(4):
                            j = cg * 4 + jj
                            pa = next_pk()
                            pb = next_pk()
                            for k in range(8):
                                S_.add("tensor", MM(pk[pa][:], wa[:, k, jj * 128:(jj + 1) * 128], R[:, 48 + k, :], k == 0, k == 7),
                                       reads=[("R", 48 + kk) for kk in range(8)] + [wak] if k in (0, 7) else (), writes=[("pk", pa)])
                            for k in range(8):
                                S_.add("tensor", MM(pk[pb][:], wd[:, k, jj * 128:(jj + 1) * 128], R[:, 56 + k, :], k == 0, k == 7),
                                       reads=[("R", 56 + kk) for kk in range(8)] + [wdk] if k in (0, 7) else (), writes=[("pk", pb)])
                            r_ = rl[j % 2]
                            S_.add("vector", TT(r_[:], pk[pa][:], R[:, j, :], ALU.mult), reads=[("pk", pa), ("R", j)], writes=[("rl", j % 2)])
                            S_.add("vector", TT(sgt[:], pk[pb][:], R[:, 16 + j, :], ALU.mult), reads=[("pk", pb), ("R", 16 + j)], writes=["sgt"])
                            S_.add("vector", TT(R[:, 32 + j, :], r_[:], sgt[:], ALU.add), reads=[("rl", j % 2), "sgt"], writes=[("R", 32 + j)])
                    load_g(1)

                    def evac_y(tt, cg, p):
                        if cg % 2 == 0:
                            S_.add("scalar", ACT(yall[:, tt, cg * 512:(cg + 1) * 512], pk[p][:], AF.Copy), reads=[("pk", p)], writes=[("yall", tt)])
                        else:
                            S_.add("vector", CP(yall[:, tt, cg * 512:(cg + 1) * 512], pk[p][:]), reads=[("pk", p)], writes=[("yall", tt)])

                    tok_major_layer(lambda tt, k: R[:, 32 + k, tt * 128:(tt + 1) * 128], lambda tt: [("R", 32 + k) for k in range(16)], NCH,
                                    lambda cg, k0, nk: wout_v[:, k0:k0 + nk, cg * 512:(cg + 1) * 512], evac_y)
                    hts = []
                    for tt, t, hs in post_norm_residual(G, xo, None, 1):
                        S_.dma("sync", H1[t * 128:(t + 1) * 128, :], ht[hs][:], reads=[("ht", hs)], writes=[("H1", t)], key=("h1s", hs))
                        hts.append((tt, hs))
                        if tt == 0:
                            pass
                    load_g(2)
                    for tt in range(4):
                        t = G * 4 + tt
                        hs = tt % 2
                        S_.dma("sync", ht[hs][:], H1[t * 128:(t + 1) * 128, :], reads=[("H1", t)], writes=[("ht", hs)], key=("ht", hs))
                        prep_from_h(tt, hs, 2)
                    for cg in range(16):
                        wt, wk = load_w(wup_v[:, :, cg * 512:(cg + 1) * 512], NCH)
                        for jj in range(4):
                            j = cg * 4 + jj
                            p = next_pk()
                            for k in range(NCH):
                                S_.add("tensor", MM(pk[p][:], wt[:, k, jj * 128:(jj + 1) * 128], uT[:, k, :], k == 0, k == NCH - 1),
                                       reads=uTk + [wk] if k in (0, NCH - 1) else (), writes=[("pk", p)])
                            r_ = rl[j % 2]
                            S_.add("scalar", ACT(r_[:], pk[p][:], AF.Relu), reads=[("pk", p)], writes=[("rl", j % 2)])
                            S_.add("vector", STT(R[:, j, :], pk[p][:], 0.0, r_[:], ALU.max, ALU.mult), reads=[("pk", p), ("rl", j % 2)],
                                   writes=[("R", j)])
                    load_g(3)
                    tok_major_layer(lambda tt, k: R[:, k, tt * 128:(tt + 1) * 128], lambda tt: [("R", k) for k in range(64)], 64,
                                    lambda cg, k0, nk: wdn_v[:, k0:k0 + nk, cg * 512:(cg + 1) * 512], evac_y)
                    for tt, t, hs in post_norm_residual(G, H1, None, 3):
                        S_.dma("sync", H2[t * 128:(t + 1) * 128, :], ht[hs][:], reads=[("ht", hs)], writes=[("H2", t)], key=("h2s", hs))
                    load_g(4)
                    for tt in range(4):
                        t = G * 4 + tt
                        hs = tt % 2
                        S_.dma("sync", ht[hs][:], H2[t * 128:(t + 1) * 128, :], reads=[("H2", t)], writes=[("ht", hs)], key=("ht", hs))
                        prep_from_h(tt, hs, 4)
                        S_.dma("sync", pt_f[:], po[t * 128:(t + 1) * 128, :], writes=["pt_f"], key="pt_f")
                        S_.add("vector", CP(pt_b[:], pt_f[:]), reads=["pt_f"], writes=["pt_b"])
                        transpose_tile(S_, c, pt_b, "pt_b", 2, ptr, [("ptr", 0), ("ptr", 1)],
                                       lambda hf, cn, tt=tt: pTt[:, 0:cn, tt * 128:(tt + 1) * 128], [("pTt", tt)])
                    load_g(5)
                    for cg in range(4):
                        wt, wk = load_w(wpg_v[:, :, cg * 512:(cg + 1) * 512], NCH)
                        wp, wpk = load_w(wpp_v[:, :, cg * 512:(cg + 1) * 512], 2)
                        for tt in range(4):
                            pa = next_pk()
                            pb = next_pk()
                            for k in range(NCH):
                                S_.add("tensor", MM(pk[pa][:], uT[:, k, tt * 128:(tt + 1) * 128], wt[:, k, :], k == 0, k == NCH - 1),
                                       reads=[("uT", tt), wk] if k in (0, NCH - 1) else (), writes=[("pk", pa)])
                            for k in range(2):
                                S_.add("tensor", MM(pk[pb][:], pTt[:, k, tt * 128:(tt + 1) * 128], wp[:, k, :], k == 0, k == 1),
                                       reads=[("pTt", tt), wpk] if k in (0, 1) else (), writes=[("pk", pb)])
                            S_.add("scalar", ACT(sgt[:], pk[pa][:], AF.Sigmoid), reads=[("pk", pa)], writes=["sgt"])
                            S_.add("vector", TT(yall[:, tt, cg * 512:(cg + 1) * 512], pk[pb][:], sgt[:], ALU.mult), reads=[("pk", pb), "sgt"],
                                   writes=[("yall", tt)])
                    for tt, t, hs in post_norm_residual(G, H2, None, 5):
                        S_.dma("sync", out[t * 128:(t + 1) * 128, :], ht[hs][:], reads=[("ht", hs)], key=("outs", hs))
                S_.finish()

        phase_proj("AK", "KQ", xb, NT, w_k, KT)
        phase_proj("AV", "V", xb, NT, w_v, VS)
        phase_proj("B", "KQ", xo, NTO, w_q, QT)
        phase_moba()
        phase_diff()
        phase_dense()
    return nc


def _prep_inputs(S, x, p, w_in, w_br_moba, w_br_diff, w_out, lambda_q1, lambda_k1, lambda_q2, lambda_k2,
                 diff_subln_g, g_mix_pre, g_mix_post, w_up, w_down, g_mlp_pre, g_mlp_post,
                 w_ple_proj, w_ple_gate, g_ple_pre, g_ple_post):
    f = lambda a: np.ascontiguousarray(np.asarray(a, dtype=np.float32))
    x = np.asarray(x, dtype=np.float32)
    p = np.asarray(p, dtype=np.float32)
    w_in = np.asarray(w_in, dtype=np.float32)[0]
    NB = S // 256
    wq = f(np.concatenate([w_in[:, 0:1024], w_in[:, 3072:4096]], axis=1))
    wk = f(np.concatenate([w_in[:, 1024:2048], w_in[:, 4096:5120]], axis=1))
    wv = f(np.concatenate([w_in[:, 2048:3072], w_in[:, 5120:6144]], axis=1))
    wg = f(w_in[:, 6144:10240])
    shared = {
        "w_k": wk, "w_v": wv, "w_q": wq, "w_g": wg,
        "w_bm": f(w_br_moba[0]), "w_bd": f(w_br_diff[0]), "w_out": f(w_out[0]),
        "w_up": f(w_up[0]), "w_down": f(w_down[0]), "w_pp": f(w_ple_proj[0]), "w_pg": f(w_ple_gate[0]),
        "lam4": f(np.stack([np.asarray(lambda_q1)[0], np.asarray(lambda_k1)[0], np.asarray(lambda_q2)[0], np.asarray(lambda_k2)[0]])),
        "subg": f(np.asarray(diff_subln_g)[0:1]),
        "gains": f(np.stack([np.asarray(a)[0] for a in (g_mix_pre, g_mix_post, g_mlp_pre, g_mlp_post, g_ple_pre, g_ple_post)])),
    }
    in_maps = []
    for c in range(8):
        b, g = c // 4, c % 4
        xbb = f(x[b])
        xo = f(xbb.reshape(NB, 256, D)[g::4].reshape(-1, D))
        pob = f(p[0, b].reshape(NB, 256, 256)[g::4].reshape(-1, 256))
        m = dict(shared)
        m["xb"] = xbb
        m["xo"] = xo
        m["po"] = pob
        m["cinfo"] = np.array([[g, 256 * g, 0, 0]], dtype=np.float32)
        in_maps.append(m)
    return in_maps


_NC_CACHE = {}


def kernel(**inputs):
    S = int(np.asarray(inputs["x"]).shape[1])
    if S not in _NC_CACHE:
        _NC_CACHE[S] = build(S)
    nc = _NC_CACHE[S]
    in_maps = _prep_inputs(S, **inputs)
    res = run_bass_kernel_spmd(nc, in_maps, core_ids=list(range(8)))
    _LAST_RES[0] = res
    NB = S // 256
    outp = np.empty((2, S, D), dtype=np.float32)
    for c in range(8):
        b, g = c // 4, c % 4
        o = np.asarray(res.results[c]["out"], dtype=np.float32).reshape(NB // 4, 256, D)
        outp[b].reshape(NB, 256, D)[g::4] = o
    return outp
```

```python
import math
import numpy as np
from contextlib import ExitStack
import concourse.bass as bass
import concourse.mybir as mybir
from concourse.bass_utils import run_bass_kernel_spmd

F32 = mybir.dt.float32
BF16 = mybir.dt.bfloat16
I32 = mybir.dt.int32
AF = mybir.ActivationFunctionType
ALU = mybir.AluOpType
AX = mybir.AxisListType

SEQ = 16384
DEBUG_DUMP = False
_LAST_RES = [None]
D = 2048
NCH = 16
EPS = 1e-6
BIG = 30000.0
ROPE_THETA = 500000.0
ENGS = ("tensor", "vector", "scalar", "gpsimd", "sync")
EPOCH = 30000


class Op:
    __slots__ = ("eng", "emit", "deps", "dkey", "sig", "has_cons")

    def __init__(self, eng, emit, dkey):
        self.eng = eng
        self.emit = emit
        self.deps = set()
        self.dkey = dkey
        self.sig = None
        self.has_cons = False


class SemPool:
    def __init__(self, nc, es):
        self.nc = nc
        self.es = es
        self.eng_sems = {e: [] for e in ENGS}
        self.eng_cnt = {e: 0 for e in ENGS}
        self.dma_sems = {}
        self.dma_cnt = {}
        self.n = 0

    def _new(self):
        self.n += 1
        return self.es.enter_context(self.nc.semaphore(f"sm{self.n}"))

    def next_eng_sig(self, e):
        c = self.eng_cnt[e]
        ep, v = divmod(c, EPOCH)
        if ep >= len(self.eng_sems[e]):
            self.eng_sems[e].append(self._new())
        self.eng_cnt[e] = c + 1
        return (self.eng_sems[e][ep], v + 1)

    def next_dma_sig(self, key):
        c = self.dma_cnt.get(key, 0)
        ep, v = divmod(c, EPOCH // 16)
        lst = self.dma_sems.setdefault(key, [])
        if ep >= len(lst):
            lst.append(self._new())
        self.dma_cnt[key] = c + 1
        return (lst[ep], 16 * (v + 1))


class Sched:
    def __init__(self, nc, pool, name):
        self.nc = nc
        self.pool = pool
        self.name = name
        self.ops = []
        self.last_w = {}
        self.readers = {}

    def add(self, eng, emit, reads=(), writes=(), dma=None):
        op = Op(eng, emit, dma)
        deps = op.deps
        lw = self.last_w
        rd = self.readers
        for k in reads:
            w = lw.get(k)
            if w is not None:
                deps.add(w)
        for k in writes:
            w = lw.get(k)
            if w is not None:
                deps.add(w)
            r = rd.get(k)
            if r:
                deps.update(r)
        for k in reads:
            rd.setdefault(k, []).append(op)
        for k in writes:
            lw[k] = op
            rd[k] = []
        deps.discard(op)
        if eng == "tensor":
            op.deps = {d for d in deps if not (d.eng == "tensor" and d.dkey is None)}
        for d in op.deps:
            d.has_cons = True
        self.ops.append(op)
        return op

    def dma(self, q, out, in_, reads=(), writes=(), key=None, **kw):
        return self.add(q, lambda e: e.dma_start(out=out, in_=in_, **kw), reads, writes, dma=key)

    def finish(self):
        last_dma = {}
        for op in self.ops:
            if op.dkey is not None:
                last_dma[op.dkey] = op
        fin = Op("sync", None, None)
        fin.deps = set(last_dma.values())
        for d in fin.deps:
            d.has_cons = True
        self.ops.append(fin)
        pool = self.pool
        for op in self.ops:
            if op.dkey is not None:
                op.sig = pool.next_dma_sig(op.dkey)
            elif op.has_cons and op.emit is not None:
                op.sig = pool.next_eng_sig(op.eng)
        per_eng = {e: [] for e in ENGS}
        for op in self.ops:
            per_eng[op.eng].append(op)
        with self.nc.Block(self.name) as block:
            for e in ENGS:
                lst = per_eng[e]
                if not lst:
                    continue

                def body(eng, lst=lst):
                    seen = {}
                    for op in lst:
                        waits = {}
                        for d in op.deps:
                            if d.sig is None:
                                continue
                            s, v = d.sig
                            k = id(s)
                            if seen.get(k, 0) >= v:
                                continue
                            if k not in waits or waits[k][1] < v:
                                waits[k] = (s, v)
                        for k, (s, v) in waits.items():
                            eng.wait_ge(s, v)
                            seen[k] = v
                        if op.emit is None:
                            continue
                        ins = op.emit(eng)
                        if op.sig is not None:
                            ins.then_inc(op.sig[0], 16 if op.dkey is not None else 1)

                getattr(block, e)(body)


def MM(out, lhsT, rhs, start, stop):
    return lambda e: e.matmul(out, lhsT=lhsT, rhs=rhs, start=start, stop=stop)


def TR(out, in_, ident):
    return lambda e: e.transpose(out, in_, ident)


def ACT(out, in_, func, bias=None, scale=None, accum_out=None):
    kw = {}
    if bias is not None:
        kw["bias"] = bias
    if scale is not None:
        kw["scale"] = scale
    if accum_out is not None:
        kw["accum_out"] = accum_out
    return lambda e: e.activation(out=out, in_=in_, func=func, **kw)


def TS(out, in0, s1, s2, op0, op1=None):
    if op1 is None:
        return lambda e: e.tensor_scalar(out=out, in0=in0, scalar1=s1, scalar2=None, op0=op0)
    return lambda e: e.tensor_scalar(out=out, in0=in0, scalar1=s1, scalar2=s2, op0=op0, op1=op1)


def TT(out, in0, in1, op):
    return lambda e: e.tensor_tensor(out=out, in0=in0, in1=in1, op=op)


def STT(out, in0, scalar, in1, op0, op1):
    return lambda e: e.scalar_tensor_tensor(out=out, in0=in0, scalar=scalar, in1=in1, op0=op0, op1=op1)


def CP(out, in_):
    return lambda e: e.tensor_copy(out=out, in_=in_)


def RCP(out, in_):
    return lambda e: e.reciprocal(out=out, in_=in_)


def MS(ap, v):
    return lambda e: e.memset(ap, v)


def bcast_row(handle_ap, n, parts=128, off=0):
    return bass.AP(handle_ap.tensor, off, [[0, parts], [1, n]])


def build(S):
    NT = S // 128
    NB = S // 256
    NQB = NB // 4
    TOWN = S // 4
    NTO = TOWN // 128
    NGA = S // 512
    NGO = TOWN // 512
    SCALE = 1.0 / math.sqrt(128.0)

    nc = bass.Bass("TRN2", target_bir_lowering=False)
    din = lambda n, s: nc.dram_tensor(n, s, F32, kind="ExternalInput").ap()
    xb = din("xb", [S, D])
    xo = din("xo", [TOWN, D])
    po = din("po", [TOWN, 256])
    w_k = din("w_k", [D, 2048])
    w_v = din("w_v", [D, 2048])
    w_q = din("w_q", [D, 2048])
    w_g = din("w_g", [D, 4096])
    w_bm = din("w_bm", [1024, D])
    w_bd = din("w_bd", [1024, D])
    w_out = din("w_out", [D, D])
    w_up = din("w_up", [D, 8192])
    w_down = din("w_down", [8192, D])
    w_pp = din("w_pp", [256, D])
    w_pg = din("w_pg", [D, D])
    lam4 = din("lam4", [4, 128])
    subg = din("subg", [1, 256])
    gains = din("gains", [6, D])
    cinfo = din("cinfo", [1, 4])
    out = nc.dram_tensor("out", [TOWN, D], F32, kind="ExternalOutput").ap()

    dscr = lambda n, s, dt: nc.dram_tensor(n, s, dt, kind="Internal").ap()
    if DEBUG_DUMP:
        dscr = lambda n, s, dt: nc.dram_tensor(n, s, dt, kind="ExternalOutput" if n in ("KT", "VS", "QT", "OT") else "Internal").ap()
    KT = dscr("KT", [16, 128, S], BF16)
    VS = dscr("VS", [S, 2048], BF16)
    QT = dscr("QT", [16, 128, TOWN], BF16)
    OT = dscr("OT", [16, 128, TOWN], BF16)

    with ExitStack() as ges:
        pool = SemPool(nc, ges)

        def common_consts(S_, es, need_rope, own):
            sb = lambda n, s, d: es.enter_context(nc.sbuf_tensor(S_.name + "_" + n, s, d))
            c = {}
            ident = sb("ident", [128, 128], BF16)
            ones_bf = sb("ones_bf", [128, 128], BF16)
            S_.add("vector", MS(ones_bf[:], 1.0), writes=["ones_bf"])
            S_.add("gpsimd", lambda e: e.affine_select(out=ident[:], in_=ones_bf[:], pattern=[[-1, 128]],
                                                       compare_op=ALU.is_equal, fill=0.0, base=0, channel_multiplier=1),
                   reads=["ones_bf"], writes=["ident"])
            c["ident"] = ident
            epst = sb("epst", [128, 1], F32)
            S_.add("vector", MS(epst[:], EPS), writes=["epst"])
            c["eps"] = epst
            ci = sb("ci", [128, 4], F32)
            S_.dma("sync", ci[:], bcast_row(cinfo, 4), writes=["ci"], key="ci")
            c["ci"] = ci
            if not need_rope:
                return c
            pswA = sb("pswA", [32, 32], F32)
            pswB = sb("pswB", [32, 32], F32)
            psw = sb("psw", [32, 32], F32)
            ones_f = sb("ones_f", [32, 32], F32)
            S_.add("vector", MS(ones_f[:], 1.0), writes=["ones_f"])
            for t, base in ((pswA, -16), (pswB, 16)):
                S_.add("gpsimd", lambda e, t=t, base=base: e.affine_select(
                    out=t[:], in_=ones_f[:], pattern=[[-1, 32]], compare_op=ALU.is_equal, fill=0.0,
                    base=base, channel_multiplier=1), reads=["ones_f"], writes=[id(t)])
            S_.add("vector", TT(psw[:], pswA[:], pswB[:], ALU.add), reads=[id(pswA), id(pswB)], writes=["psw"])
            c["psw"] = psw
            ri = sb("ri", [32, 1], I32)
            rf = sb("rf", [32, 1], F32)
            r16 = sb("r16", [32, 1], F32)
            tmp = sb("rtmp", [32, 1], F32)
            invf = sb("invf", [32, 1], F32)
            sgn = sb("sgn", [32, 1], F32)
            S_.add("gpsimd", lambda e: e.iota(ri[:], pattern=[[0, 1]], base=0, channel_multiplier=1), writes=["ri"])
            S_.add("vector", CP(rf[:], ri[:]), reads=["ri"], writes=["rf"])
            S_.add("vector", TS(tmp[:], rf[:], 16.0, -16.0, ALU.is_ge, ALU.mult), reads=["rf"], writes=["rtmp"])
            S_.add("vector", TT(r16[:], rf[:], tmp[:], ALU.add), reads=["rf", "rtmp"], writes=["r16"])
            S_.add("vector", TS(sgn[:], rf[:], 16.0, 2.0, ALU.is_ge, ALU.mult), reads=["rf"], writes=["sgn"])
            S_.add("vector", TS(sgn[:], sgn[:], -1.0, None, ALU.add), reads=["sgn"], writes=["sgn"])
            S_.add("vector", MS(invf[:], 0.0), writes=["invf"])
            for j in range(16):
                cj = float(np.float32(1.0) / (np.float32(ROPE_THETA) ** (np.float32(j * 2.0) / np.float32(32.0))))
                S_.add("vector", TS(tmp[:], r16[:], float(j), cj, ALU.is_equal, ALU.mult), reads=["r16", "rtmp"], writes=["rtmp"])
                S_.add("vector", TT(invf[:], invf[:], tmp[:], ALU.add), reads=["invf", "rtmp"], writes=["invf"])
            c["invf"] = invf
            c["sgn"] = sgn
            ioi = sb("ioi", [32, 512], I32)
            iof = sb("iof", [32, 512], F32)
            if own:
                S_.add("gpsimd", lambda e: e.iota(ioi[:], pattern=[[1024, 2], [1, 256]], base=0, channel_multiplier=0), writes=["ioi"])
                S_.add("vector", CP(iof[:], ioi[:]), reads=["ioi"], writes=["iof"])
                S_.add("vector", TS(iof[:], iof[:], ci[0:32, 1:2], None, ALU.add), reads=["iof", "ci"], writes=["iof"])
            else:
                S_.add("gpsimd", lambda e: e.iota(ioi[:], pattern=[[1, 512]], base=0, channel_multiplier=0), writes=["ioi"])
                S_.add("vector", CP(iof[:], ioi[:]), reads=["ioi"], writes=["iof"])
            c["iof"] = iof
            return c

        def rope_tables(S_, rt, c, base, slot):
            ang, ki, kf, red, sa, ca, cosT, sinT = rt
            k = ("rt", slot)
            TWO_PI = 2.0 * math.pi
            c1 = 6.28125
            c2 = float(np.float32(TWO_PI - c1))
            c3 = float(np.float32(TWO_PI - c1 - float(np.float32(TWO_PI - c1))))
            S_.add("vector", TS(ang[:], c["iof"][:], float(base), c["invf"][:, 0:1], ALU.add, ALU.mult),
                   reads=["iof", "invf"], writes=[k + ("ang",)])
            S_.add("vector", TS(ki[:], ang[:], 1.0 / TWO_PI, None, ALU.mult), reads=[k + ("ang",)], writes=[k + ("ki",)])
            S_.add("vector", CP(kf[:], ki[:]), reads=[k + ("ki",)], writes=[k + ("kf",)])
            S_.add("vector", STT(red[:], kf[:], -c1, ang[:], ALU.mult, ALU.add), reads=[k + ("ang",), k + ("kf",)], writes=[k + ("red",)])
            S_.add("vector", STT(red[:], kf[:], -c2, red[:], ALU.mult, ALU.add), reads=[k + ("red",), k + ("kf",)], writes=[k + ("red",)])
            S_.add("vector", STT(red[:], kf[:], -c3, red[:], ALU.mult, ALU.add), reads=[k + ("red",), k + ("kf",)], writes=[k + ("red",)])
            PI = math.pi
            S_.add("vector", TS(sa[:], red[:], PI, -TWO_PI, ALU.is_gt, ALU.mult), reads=[k + ("red",)], writes=[k + ("sa",)])
            S_.add("vector", TT(sa[:], sa[:], red[:], ALU.add), reads=[k + ("red",), k + ("sa",)], writes=[k + ("sa",)])
            S_.add("vector", TS(sa[:], sa[:], -PI, PI, ALU.max, ALU.min), reads=[k + ("sa",)], writes=[k + ("sa",)])
            S_.add("vector", TS(ca[:], red[:], PI / 2, -TWO_PI, ALU.is_gt, ALU.mult), reads=[k + ("red",)], writes=[k + ("ca",)])
            S_.add("vector", STT(ca[:], red[:], PI / 2, ca[:], ALU.add, ALU.add), reads=[k + ("red",), k + ("ca",)], writes=[k + ("ca",)])
            S_.add("vector", TS(ca[:], ca[:], -PI, PI, ALU.max, ALU.min), reads=[k + ("ca",)], writes=[k + ("ca",)])
            S_.add("scalar", ACT(sinT[:], sa[:], AF.Sin), reads=[k + ("sa",)], writes=[k + ("sin",)])
            S_.add("scalar", ACT(cosT[:], ca[:], AF.Sin), reads=[k + ("ca",)], writes=[k + ("cos",)])
            S_.add("vector", TS(sinT[:], sinT[:], c["sgn"][:, 0:1], None, ALU.mult), reads=[k + ("sin",), "sgn"],
                   writes=[k + ("sin",)])
            return cosT, sinT, k + ("cos",), k + ("sin",)

        def norm_tile(S_, c, xt, xkey, grep, gkey, ub, ubkey, st, stkey, nfeat=D):
            S_.add("scalar", ACT(ub, xt, AF.Square, accum_out=st[:, 0:1]), reads=[xkey], writes=[ubkey, stkey])
            S_.add("scalar", ACT(st[:, 1:2], st[:, 0:1], AF.Sqrt, bias=c["eps"][:, 0:1], scale=1.0 / nfeat),
                   reads=[stkey, "epst"], writes=[stkey])
            S_.add("vector", RCP(st[:, 2:3], st[:, 1:2]), reads=[stkey], writes=[stkey])
            S_.add("vector", STT(ub, xt, st[:, 2:3], grep, ALU.mult, ALU.mult), reads=[xkey, stkey, gkey], writes=[ubkey])

        def transpose_tile(S_, c, ub, ubkey, nch, ptr, ptrkeys, dst_fn, dstkeys):
            nh = (nch + 7) // 8
            for hf in range(nh):
                cn = min(8, nch - hf * 8)
                for cc in range(cn):
                    ch = hf * 8 + cc
                    S_.add("tensor", TR(ptr[hf][:, cc, :], ub[:, ch * 128:(ch + 1) * 128], c["ident"][:]),
                           reads=[ubkey, "ident"], writes=[ptrkeys[hf]])
                eng = "scalar" if hf % 2 == 0 else "vector"
                if eng == "scalar":
                    S_.add("scalar", ACT(dst_fn(hf, cn), ptr[hf][:, 0:cn, :], AF.Copy), reads=[ptrkeys[hf]], writes=dstkeys)
                else:
                    S_.add("vector", CP(dst_fn(hf, cn), ptr[hf][:, 0:cn, :]), reads=[ptrkeys[hf]], writes=dstkeys)

        def phase_proj(name, kind, xsrc, ntiles, wsrc, dst):
            own = (name == "B")
            with ExitStack() as es:
                S_ = Sched(nc, pool, name)
                sb = lambda n, s, d: es.enter_context(nc.sbuf_tensor(S_.name + "_" + n, s, d))
                ps = lambda n, s, d: es.enter_context(nc.psum_tensor(S_.name + "_" + n, s, d))
                c = common_consts(S_, es, kind == "KQ", own)
                w = sb("w", [128, NCH, 2048], BF16)
                wv = wsrc.rearrange("(c p) n -> p c n", p=128)
                for j in range(4):
                    S_.dma("gpsimd", w[:, :, j * 512:(j + 1) * 512], wv[:, :, j * 512:(j + 1) * 512], writes=[("w", j)], key=("w", j))
                wkeys = [("w", j) for j in range(4)]
                grep = sb("grep", [128, D], F32)
                S_.dma("sync", grep[:], bcast_row(gains, D, off=0), writes=["grep"], key="grep")
                xt = [sb(f"xt{i}", [128, D], F32) for i in range(2)]
                ub = [sb(f"ub{i}", [128, D], BF16) for i in range(4)]
                st = [sb(f"st{i}", [128, 4], F32) for i in range(2)]
                uT = [sb(f"uT{i}", [128, NCH, 512], BF16) for i in range(2)]
                ptr = [ps(f"ptr{i}", [128, 8, 128], BF16) for i in range(2)]
                pk = [ps(f"pk{i}", [128, 512], F32) for i in range(3)]
                if kind == "KQ":
                    pp = [ps(f"pp{i}", [128, 512], F32) for i in range(2)]
                    rt = [[sb(f"rt{s}_{n}", [32, 512], I32 if n == 1 else F32) for n in range(8)] for s in range(2)]
                    r32 = [sb(f"r32_{i}", [32, 512], F32) for i in range(2)]
                    t1 = sb("t1", [32, 512], F32)
                    t2 = sb("t2", [32, 512], F32)
                    osb = [sb(f"osb{i}", [128, 512], BF16) for i in range(3)]
                else:
                    vsb = [sb(f"vsb{i}", [128, 2048], BF16) for i in range(2)]
                ngroups = ntiles // 4
                ropes = {}

                def prepA(G):
                    gs = G % 2
                    for tt in range(4):
                        t = G * 4 + tt
                        xs = t % 2
                        S_.dma("sync", xt[xs][:], xsrc[t * 128:(t + 1) * 128, :], writes=[("xt", xs)], key=("xt", xs))
                        norm_tile(S_, c, xt[xs][:], ("xt", xs), grep[:], "grep", ub[tt][:], ("ub", tt), st[xs], ("st", xs))
                    if kind == "KQ":
                        base = (2048 * G) if own else (512 * G)
                        ropes[G] = rope_tables(S_, rt[gs], c, base, gs)

                def prepB(G):
                    gs = G % 2
                    for tt in range(4):
                        transpose_tile(S_, c, ub[tt], ("ub", tt), NCH, ptr, [("ptr", 0), ("ptr", 1)],
                                       lambda hf, cn, gs=gs, tt=tt: uT[gs][:, hf * 8:hf * 8 + cn, tt * 128:(tt + 1) * 128],
                                       [("uT", gs, tt)])

                prepA(0)
                prepB(0)
                for G in range(ngroups):
                    gs = G % 2
                    uTk = [("uT", gs, tt) for tt in range(4)]
                    if kind == "KQ":
                        cosT, sinT, ck, sk = ropes.pop(G)

                        def fin(h, G=G, cosT=cosT, sinT=sinT, ck=ck, sk=sk):
                            r = h % 2
                            o = h % 3
                            S_.add("tensor", MM(pp[r][0:32, :], c["psw"][:, :], r32[r][:, :], True, True),
                                   reads=[("r32", r), "psw"], writes=[("pp", r)])
                            S_.add("vector", TT(t1[:], r32[r][:], cosT[:], ALU.mult), reads=[("r32", r), ck], writes=["t1"])
                            S_.add("vector", TT(t2[:], pp[r][0:32, :], sinT[:], ALU.mult), reads=[("pp", r), sk], writes=["t2"])
                            S_.add("vector", TT(osb[o][0:32, :], t1[:], t2[:], ALU.add), reads=["t1", "t2", ("osb", o, 0)], writes=[("osb", o, 0)])
                            S_.dma("sync", dst[h][:, G * 512:(G + 1) * 512], osb[o][:], reads=[("osb", o, 0)],
                                   key=("osb", o))

                        for h in range(16):
                            if G + 1 < ngroups:
                                if h == 3:
                                    prepA(G + 1)
                                elif h == 11:
                                    prepB(G + 1)
                            p = h % 3
                            for ch in range(NCH):
                                S_.add("tensor", MM(pk[p][:], w[:, ch, h * 128:(h + 1) * 128], uT[gs][:, ch, :], ch == 0, ch == NCH - 1),
                                       reads=uTk + [("w", h // 4)] if ch in (0, NCH - 1) else (), writes=[("pk", p)])
                            S_.add("scalar", ACT(osb[h % 3][:, :], pk[p][:, :], AF.Copy), reads=[("pk", p)],
                                   writes=[("osb", h % 3, 0)])
                            S_.add("scalar", ACT(r32[h % 2][:], pk[p][0:32, :], AF.Copy), reads=[("pk", p)], writes=[("r32", h % 2)])
                            if h >= 1:
                                fin(h - 1)
                        fin(15)
                    else:
                        for tt in range(4):
                            if G + 1 < ngroups:
                                if tt == 1:
                                    prepA(G + 1)
                                elif tt == 3:
                                    prepB(G + 1)
                            t = G * 4 + tt
                            vs = t % 2
                            for cg in range(4):
                                p = (tt * 4 + cg) % 3
                                for ch in range(NCH):
                                    S_.add("tensor", MM(pk[p][:], uT[gs][:, ch, tt * 128:(tt + 1) * 128], w[:, ch, cg * 512:(cg + 1) * 512],
                                                        ch == 0, ch == NCH - 1),
                                           reads=[("uT", gs, tt), ("w", cg)] if ch in (0, NCH - 1) else (), writes=[("pk", p)])
                                if cg % 2 == 0:
                                    S_.add("scalar", ACT(vsb[vs][:, cg * 512:(cg + 1) * 512], pk[p][:], AF.Copy), reads=[("pk", p)],
                                           writes=[("vsb", vs, cg)])
                                else:
                                    S_.add("vector", CP(vsb[vs][:, cg * 512:(cg + 1) * 512], pk[p][:]), reads=[("pk", p)],
                                           writes=[("vsb", vs, cg)])
                            S_.dma("sync", dst[t * 128:(t + 1) * 128, :], vsb[vs][:], reads=[("vsb", vs, cg) for cg in range(4)],
                                   key=("vsb", vs))
                S_.finish()

        def attn_consts(S_, es, c):
            sb = lambda n, s, d: es.enter_context(nc.sbuf_tensor(S_.name + "_" + n, s, d))
            ci = c["ci"]
            pmi = sb("pmi", [128, 128], I32)
            pmf = sb("pmf", [128, 128], F32)
            pmask = sb("pmask", [128, 16, 128], BF16)
            for d in range(4):
                for ktl in range(2):
                    for qs in range(2):
                        idx = d * 4 + ktl * 2 + qs
                        base = 256 * d + 128 * ktl - 128 * qs
                        S_.add("gpsimd", lambda e, base=base: e.iota(pmi[:], pattern=[[-1, 128]], base=base, channel_multiplier=1),
                               writes=["pmi"])
                        S_.add("vector", CP(pmf[:], pmi[:]), reads=["pmi"], writes=["pmf"])
                        S_.add("vector", TS(pmf[:], pmf[:], ci[:, 1:2], 0.0, ALU.subtract, ALU.is_gt), reads=["pmf", "ci"], writes=["pmf"])
                        S_.add("vector", TS(pmask[:, idx, :], pmf[:], -BIG, None, ALU.mult), reads=["pmf"], writes=["pmask"])
            c["pmask"] = pmask
            return c

        def phase_moba():
            with ExitStack() as es:
                S_ = Sched(nc, pool, "CM")
                sb = lambda n, s, d: es.enter_context(nc.sbuf_tensor(S_.name + "_" + n, s, d))
                ps = lambda n, s, d: es.enter_context(nc.psum_tensor(S_.name + "_" + n, s, d))
                c = common_consts(S_, es, False, False)
                attn_consts(S_, es, c)
                ci = c["ci"]
                ident = c["ident"]
                pmask = c["pmask"]
                esel = sb("esel", [128, NB, 128], BF16)
                eones = sb("eones", [128, NB, 128], BF16)
                S_.add("vector", MS(eones[:], 1.0), writes=["eones"])
                S_.add("gpsimd", lambda e: e.affine_select(out=esel[:], in_=eones[:], pattern=[[-1, NB], [0, 128]],
                                                           compare_op=ALU.is_equal, fill=0.0, base=0, channel_multiplier=1),
                       reads=["eones"], writes=["esel"])
                ni = sb("ni", [128, NB], I32)
                nf = sb("nf", [128, NB], F32)
                vv = sb("vv", [128, NB], F32)
                gbA = sb("gbA", [128, NQB, NB], F32)
                ownb = sb("ownb", [128, NQB, NB], F32)
                futb = sb("futb", [128, NQB, NB], F32)
                S_.add("gpsimd", lambda e: e.iota(ni[:], pattern=[[1, NB]], base=0, channel_multiplier=0), writes=["ni"])
                S_.add("vector", CP(nf[:], ni[:]), reads=["ni"], writes=["nf"])
                for i in range(NQB):
                    S_.add("vector", TS(vv[:], nf[:], ci[:, 0:1], float(4 * i), ALU.subtract, ALU.subtract), reads=["nf", "ci"], writes=["vv"])
                    S_.add("vector", TS(gbA[:, i, :], vv[:], 0.0, -BIG, ALU.is_ge, ALU.mult), reads=["vv"], writes=["gconst"])
                    S_.add("vector", TS(ownb[:, i, :], vv[:], 0.0, 1.0, ALU.is_equal, ALU.subtract), reads=["vv"], writes=["gconst"])
                    S_.add("vector", TS(ownb[:, i, :], ownb[:, i, :], BIG, None, ALU.mult), reads=["gconst"], writes=["gconst"])
                    S_.add("vector", TS(futb[:, i, :], vv[:], 0.0, -BIG, ALU.is_gt, ALU.mult), reads=["vv"], writes=["gconst"])
                kt_sb = sb("kt_sb", [128, S], BF16)
                vx = sb("vx", [128, NT, 129], BF16)
                qt_sb = sb("qt_sb", [128, TOWN], BF16)
                S_.add("vector", MS(vx[:, :, 128:129], 1.0), writes=["vx1"])
                km = sb("km", [128, NB], F32)
                kmT = sb("kmT", [128, NB], BF16)
                gm = [sb(f"gm{q}", [128, NB], F32) for q in range(2)]
                mx8 = [sb(f"mx8{q}", [128, 8], F32) for q in range(2)]
                sel = [sb(f"sel{q}", [128, NB], F32) for q in range(2)]
                fb = [sb(f"fb{q}", [128, NB], BF16) for q in range(2)]
                biasTb = [sb(f"biasT{q}", [128, 256], BF16) for q in range(2)]
                for q in range(2):
                    S_.add("vector", MS(biasTb[q][:], 0.0), writes=[("biasT", q, 0), ("biasT", q, 1)])

                def gateA(i, qs):
                    q0 = i * 256 + qs * 128
                    S_.add("tensor", MM(pg[:, 0:NB], qt_sb[:, q0:q0 + 128], kmT[:, :], True, True), reads=["qt", "kmT"], writes=["pg"])
                    S_.add("vector", TT(gm[qs][:], pg[:, 0:NB], gbA[:, i, :], ALU.add), reads=["pg", "gconst"], writes=[("gm", qs)])
                    S_.add("vector", lambda e: e.max(out=mx8[qs][:], in_=gm[qs][:]), reads=[("gm", qs)], writes=[("mx8", qs)])
                    S_.add("vector", TS(sel[qs][:], gm[qs][:], mx8[qs][:, 2:3], 1.0, ALU.is_ge, ALU.subtract), reads=[("gm", qs), ("mx8", qs)],
                           writes=[("sel", qs)])
                    S_.add("vector", STT(sel[qs][:], sel[qs][:], BIG, ownb[:, i, :], ALU.mult, ALU.max), reads=[("sel", qs), "gconst"],
                           writes=[("sel", qs)])
                    S_.add("vector", TT(fb[qs][:], sel[qs][:], futb[:, i, :], ALU.min), reads=[("sel", qs), "gconst"], writes=[("fb", qs)])

                def gateB(i, qs):
                    b_ = i % 2
                    S_.add("tensor", TR(ptb[0:NB, 0, :], fb[qs][:, :], ident[:]), reads=[("fb", qs), "ident"], writes=["ptb"])
                    S_.add("scalar", ACT(biasTb[b_][0:NB, qs * 128:(qs + 1) * 128], ptb[0:NB, 0, :], AF.Copy), reads=["ptb"],
                           writes=[("biasT", b_, qs)])
                pT = [sb(f"pT{i}", [128, 256], BF16) for i in range(4)]
                rinv = sb("rinv", [128, 2], F32)
                obf = sb("obf", [128, 128], BF16)
                oT = [sb(f"oT{i}", [128, 256], BF16) for i in range(2)]
                pob = [ps(f"po{i}", [128, 512], F32) for i in range(2)]
                pstb = [ps(f"pst{i}", [128, 512], F32) for i in range(4)]
                pst = [pstb[i][:, 0:256] for i in range(4)]
                pg = ps("pg", [128, 512], F32)
                ptb = ps("ptb", [128, 8, 128], BF16)
                VSv = VS.rearrange("(t p) n -> p t n", p=128)
                npst = 0
                nkv = 8 if NT >= 64 else 4
                for h in range(8):
                    nkp = 4
                    for j in range(nkp):
                        a, b_ = j * (S // nkp), (j + 1) * (S // nkp)
                        S_.dma("sync", kt_sb[:, a:b_], KT[h][:, a:b_], writes=[("kt", j)], key=("kt", j))
                    for j in range(nkv):
                        a, b_ = j * (NT // nkv), (j + 1) * (NT // nkv)
                        S_.dma("sync", vx[:, a:b_, 0:128], VSv[:, a:b_, h * 128:(h + 1) * 128], writes=[("vx", j)], key=("vx", j))
                    S_.dma("sync", qt_sb[:], QT[h][:, :], writes=["qt"], key="qt")
                    ktk = [("kt", j) for j in range(nkp)]

                    S_.add("vector", lambda e: e.tensor_reduce(out=km[:], in_=kt_sb[:].rearrange("p (n k) -> p n k", k=256), axis=AX.X, op=ALU.add),
                           reads=ktk, writes=["km"])
                    S_.add("vector", TS(kmT[:], km[:], 1.0 / 256.0, None, ALU.mult), reads=["km"], writes=["kmT"])
                    for i in range(NQB):
                        if i == 0:
                            for qs in range(2):
                                gateA(0, qs)
                                gateB(0, qs)
                        biasT = biasTb[i % 2]
                        nkt = 8 * i + 8
                        LA = 3
                        slots = {}
                        for step in range(nkt + LA):
                            if i + 1 < NQB:
                                if step == 0:
                                    gateA(i + 1, 0)
                                elif step == 2:
                                    gateA(i + 1, 1)
                                elif step == 4:
                                    gateB(i + 1, 0)
                                elif step == 6:
                                    gateB(i + 1, 1)
                            if step < nkt:
                                kt = step
                                n = kt // 2
                                d = n - 4 * i
                                p = npst % 4
                                npst += 1
                                slots[kt] = p
                                S_.add("tensor", MM(pst[p], kt_sb[:, kt * 128:(kt + 1) * 128], qt_sb[:, i * 256:(i + 1) * 256], True, False),
                                       reads=[("kt", kt * 128 // (S // nkp)), "qt"], writes=[("pst", p)])
                                S_.add("tensor", MM(pst[p], esel[:, n, :], biasT[:, :], False, d < 0),
                                       reads=["esel", ("biasT", i % 2, 0), ("biasT", i % 2, 1)], writes=[("pst", p)])
                                if d >= 0:
                                    for qs in range(2):
                                        S_.add("tensor", MM(pst[p][:, qs * 128:(qs + 1) * 128], ident[:], pmask[:, d * 4 + (kt % 2) * 2 + qs, :],
                                                            False, qs == 1), reads=["ident", "pmask"], writes=[("pst", p)])
                                S_.add("scalar", ACT(pT[p][:], pst[p], AF.Exp, scale=SCALE), reads=[("pst", p)], writes=[("pT", p)])
                            if step >= LA:
                                kt = step - LA
                                p = slots[kt]
                                for qs in range(2):
                                    S_.add("tensor", MM(pob[qs][:, 0:129], pT[p][:, qs * 128:(qs + 1) * 128], vx[:, kt, :], kt == 0, kt == nkt - 1),
                                           reads=[("pT", p), ("vx", kt // (NT // nkv)), "vx1"], writes=[("po", qs)])
                        os_ = i % 2
                        for qs in range(2):
                            S_.add("vector", RCP(rinv[:, qs:qs + 1], pob[qs][:, 128:129]), reads=[("po", qs)], writes=[("rinv", qs)])
                            S_.add("vector", TS(obf[:], pob[qs][:, 0:128], rinv[:, qs:qs + 1], None, ALU.mult), reads=[("po", qs), ("rinv", qs)],
                                   writes=["obf"])
                            S_.add("tensor", TR(ptb[:, 1 + qs, :], obf[:], ident[:]), reads=["obf", "ident"], writes=[("ptbo", qs)])
                            S_.add("scalar", ACT(oT[os_][:, qs * 128:(qs + 1) * 128], ptb[:, 1 + qs, :], AF.Copy), reads=[("ptbo", qs)],
                                   writes=[("oT", os_, qs)])
                        S_.dma("sync", OT[h][:, i * 256:(i + 1) * 256], oT[os_][:], reads=[("oT", os_, 0), ("oT", os_, 1)], key=("oT", os_))
                S_.finish()

        def phase_diff():
            with ExitStack() as es:
                S_ = Sched(nc, pool, "CD")
                sb = lambda n, s, d: es.enter_context(nc.sbuf_tensor(S_.name + "_" + n, s, d))
                ps = lambda n, s, d: es.enter_context(nc.psum_tensor(S_.name + "_" + n, s, d))
                c = common_consts(S_, es, False, False)
                attn_consts(S_, es, c)
                ident = c["ident"]
                pmask = c["pmask"]
                lq = sb("lq", [128, 4, 128], F32)
                S_.dma("sync", lq[:], bass.AP(lam4.tensor, 0, [[0, 128], [128, 4], [1, 128]]), writes=["lq"], key="lq")
                lj = sb("lj", [128, 128], F32)
                ld = sb("ld", [128, 4], F32)
                S_.add("vector", TT(lj[:], lq[:, 0, :], lq[:, 1, :], ALU.mult), reads=["lq"], writes=["lj"])
                S_.add("vector", lambda e: e.tensor_reduce(out=ld[:, 0:1], in_=lj[:], axis=AX.X, op=ALU.add), reads=["lj"], writes=["ld0"])
                S_.add("vector", TT(lj[:], lq[:, 2, :], lq[:, 3, :], ALU.mult), reads=["lq", "ld0"], writes=["lj"])
                S_.add("vector", lambda e: e.tensor_reduce(out=ld[:, 1:2], in_=lj[:], axis=AX.X, op=ALU.add), reads=["lj"], writes=["ld1"])
                S_.add("scalar", ACT(ld[:, 2:4], ld[:, 0:2], AF.Exp), reads=["ld0", "ld1"], writes=["ld2"])
                nlam = sb("nlam", [128, 1], F32)
                S_.add("vector", TT(nlam[:], ld[:, 3:4], ld[:, 2:3], ALU.subtract), reads=["ld2"], writes=["nlam"])
                S_.add("vector", TS(nlam[:], nlam[:], -0.2, None, ALU.add), reads=["nlam"], writes=["nlam"])
                sg = sb("sg", [128, 256], F32)
                S_.dma("sync", sg[:], bcast_row(subg, 256), writes=["sg"], key="sg")
                S_.add("vector", TS(sg[:], sg[:], 0.8, None, ALU.mult), reads=["sg"], writes=["sg"])
                kt_sb = [sb(f"kt_sb{s}", [128, S], BF16) for s in range(2)]
                qt_sb = [sb(f"qt_sb{s}", [128, TOWN], BF16) for s in range(2)]
                vx = sb("vx", [128, NT, 257], BF16)
                S_.add("vector", MS(vx[:, :, 256:257], 1.0), writes=["vx1"])
                pT = [sb(f"pT{i}", [128, 256], BF16) for i in range(4)]
                rinv = sb("rinv", [128, 4], F32)
                tq = sb("tq", [128, 256], F32)
                oq = sb("oq", [128, 256], F32)
                st = sb("stt", [128, 4], F32)
                junk = sb("junk", [128, 256], BF16)
                obf = sb("obf", [128, 256], BF16)
                oT = [sb(f"oT{i}", [128, 2, 256], BF16) for i in range(2)]
                pob = [[ps(f"po{s}{q}", [128, 512], F32) for q in range(2)] for s in range(2)]
                pstb = [ps(f"pst{i}", [128, 512], F32) for i in range(3)]
                pst = [pstb[i][:, 0:256] for i in range(3)]
                ptb = ps("ptb", [128, 8, 128], BF16)
                VSv = VS.rearrange("(t p) n -> p t n", p=128)
                npst = 0
                nkp = 4
                nkv = 8 if NT >= 64 else 4
                for hd in range(4):
                    for s in range(2):
                        for j in range(nkp):
                            a, b_ = j * (S // nkp), (j + 1) * (S // nkp)
                            S_.dma("sync", kt_sb[s][:, a:b_], KT[8 + 2 * hd + s][:, a:b_], writes=[("kt", s, j)], key=("kt", s, j))
                        S_.dma("sync", qt_sb[s][:], QT[8 + 2 * hd + s][:, :], writes=[("qt", s)], key=("qt", s))
                    for j in range(nkv):
                        a, b_ = j * (NT // nkv), (j + 1) * (NT // nkv)
                        S_.dma("sync", vx[:, a:b_, 0:256], VSv[:, a:b_, 1024 + hd * 256:1024 + (hd + 1) * 256], writes=[("vx", j)], key=("vx", j))
                    for i in range(NQB):
                        nkt = 8 * i + 8
                        LA = 2
                        slots = {}
                        nsteps = nkt * 2
                        for step in range(nsteps + LA):
                            if step < nsteps:
                                kt, s = step // 2, step % 2
                                d = kt // 2 - 4 * i
                                p = npst % 3
                                tq_ = npst % 4
                                npst += 1
                                slots[step] = tq_
                                S_.add("tensor", MM(pst[p], kt_sb[s][:, kt * 128:(kt + 1) * 128], qt_sb[s][:, i * 256:(i + 1) * 256], True, d < 0),
                                       reads=[("kt", s, kt * 128 // (S // nkp)), ("qt", s)], writes=[("pst", p)])
                                if d >= 0:
                                    for qs in range(2):
                                        S_.add("tensor", MM(pst[p][:, qs * 128:(qs + 1) * 128], ident[:], pmask[:, d * 4 + (kt % 2) * 2 + qs, :],
                                                            False, qs == 1), reads=["ident", "pmask"], writes=[("pst", p)])
                                S_.add("scalar", ACT(pT[tq_][:], pst[p], AF.Exp, scale=SCALE), reads=[("pst", p)], writes=[("pT", tq_)])
                            if step >= LA:
                                st2 = step - LA
                                kt, s = st2 // 2, st2 % 2
                                p = slots[st2]
                                for qs in range(2):
                                    S_.add("tensor", MM(pob[s][qs][:, 0:257], pT[p][:, qs * 128:(qs + 1) * 128], vx[:, kt, :], kt == 0, kt == nkt - 1),
                                           reads=[("pT", p), ("vx", kt // (NT // nkv)), "vx1"], writes=[("po", s, qs)])
                        os_ = i % 2
                        for qs in range(2):
                            S_.add("vector", RCP(rinv[:, 0:1], pob[0][qs][:, 256:257]), reads=[("po", 0, qs)], writes=["rinv0"])
                            S_.add("vector", RCP(rinv[:, 1:2], pob[1][qs][:, 256:257]), reads=[("po", 1, qs)], writes=["rinv1"])
                            S_.add("vector", TS(rinv[:, 2:3], rinv[:, 1:2], nlam[:, 0:1], None, ALU.mult), reads=["rinv1", "nlam"], writes=["rinv2"])
                            S_.add("vector", TS(tq[:], pob[1][qs][:, 0:256], rinv[:, 2:3], None, ALU.mult), reads=[("po", 1, qs), "rinv2"], writes=["tq"])
                            S_.add("vector", STT(oq[:], pob[0][qs][:, 0:256], rinv[:, 0:1], tq[:], ALU.mult, ALU.add),
                                   reads=[("po", 0, qs), "rinv0", "tq"], writes=["oq"])
                            norm_tile(S_, c, oq[:], "oq", sg[:], "sg", obf[:], "obf", st, "stt", nfeat=256)
                            for cc in range(2):
                                S_.add("tensor", TR(ptb[:, qs * 2 + cc, :], obf[:, cc * 128:(cc + 1) * 128], ident[:]), reads=["obf", "ident"],
                                       writes=[("ptbo", qs, cc)])
                                S_.add("scalar", ACT(oT[os_][:, cc, qs * 128:(qs + 1) * 128], ptb[:, qs * 2 + cc, :], AF.Copy),
                                       reads=[("ptbo", qs, cc)], writes=[("oT", os_, qs, cc)])
                        for cc in range(2):
                            S_.dma("sync", OT[8 + hd * 2 + cc][:, i * 256:(i + 1) * 256], oT[os_][:, cc, :],
                                   reads=[("oT", os_, 0, cc), ("oT", os_, 1, cc)], key=("oT", os_, cc))
                S_.finish()

        def phase_dense():
            with ExitStack() as es:
                S_ = Sched(nc, pool, "DD")
                sb = lambda n, s, d: es.enter_context(nc.sbuf_tensor(S_.name + "_" + n, s, d))
                ps = lambda n, s, d: es.enter_context(nc.psum_tensor(S_.name + "_" + n, s, d))
                c = common_consts(S_, es, False, False)
                R = sb("R", [128, 64, 512], BF16)
                yall = sb("yall", [128, 4, D], F32)
                uT = sb("uT", [128, NCH, 512], BF16)
                wb = [sb(f"wb{i}", [128, NCH, 512], BF16) for i in range(3)]
                ht = [sb(f"ht{i}", [128, D], F32) for i in range(2)]
                ub = sb("ub", [128, D], BF16)
                grep = sb("grep", [128, D], F32)
                st = [sb(f"st{i}", [128, 4], F32) for i in range(2)]
                rl = [sb(f"rl{i}", [128, 512], F32) for i in range(2)]
                pt_f = sb("pt_f", [128, 256], F32)
                pt_b = sb("pt_b", [128, 256], BF16)
                pTt = sb("pTt", [128, 2, 512], BF16)
                sgt = sb("sgt", [128, 512], F32)
                ptr = [ps(f"ptr{i}", [128, 8, 128], BF16) for i in range(2)]
                pk = [ps(f"pk{i}", [128, 512], F32) for i in range(6)]
                H1 = dscr("H1", [TOWN, D], F32)
                H2 = dscr("H2", [TOWN, D], F32)
                wcnt = [0]
                pkc = [0]

                def load_w(src_view, nchunks):
                    sl = wcnt[0] % 3
                    wcnt[0] += 1
                    S_.dma("gpsimd", wb[sl][:, 0:nchunks, :], src_view, writes=[("wb", sl)], key=("wb", sl))
                    return wb[sl], ("wb", sl)

                def load_g(idx):
                    S_.dma("sync", grep[:], bcast_row(gains, D, off=idx * D), writes=["grep"], key="grep")

                def next_pk():
                    p = pkc[0] % 6
                    pkc[0] += 1
                    return p

                def tok_major_layer(lhs_fn, lhs_keys, nk, wsrc_fn, evac_fn):
                    for cg in range(4):
                        nsl = (nk + NCH - 1) // NCH
                        if nsl == 1:
                            wt, wk = load_w(wsrc_fn(cg, 0, nk), nk)
                            for tt in range(4):
                                p = next_pk()
                                for k in range(nk):
                                    S_.add("tensor", MM(pk[p][:], lhs_fn(tt, k), wt[:, k, :], k == 0, k == nk - 1),
                                           reads=lhs_keys(tt) + [wk] if k in (0, nk - 1) else (), writes=[("pk", p)])
                                evac_fn(tt, cg, p)
                        else:
                            pp_ = [next_pk() for _ in range(4)]
                            for sl in range(nsl):
                                wt, wk = load_w(wsrc_fn(cg, sl * NCH, NCH), NCH)
                                for tt in range(4):
                                    for k in range(NCH):
                                        kk = sl * NCH + k
                                        S_.add("tensor", MM(pk[pp_[tt]][:], lhs_fn(tt, kk), wt[:, k, :], kk == 0, kk == nk - 1),
                                               reads=lhs_keys(tt) + [wk] if k in (0, NCH - 1) else (), writes=[("pk", pp_[tt])])
                            for tt in range(4):
                                evac_fn(tt, cg, pp_[tt])

                def post_norm_residual(G, hsrc, hdst, stage):
                    for tt in range(4):
                        t = G * 4 + tt
                        hs = tt % 2
                        S_.dma("sync", ht[hs][:], hsrc[t * 128:(t + 1) * 128, :], writes=[("ht", hs)], key=("ht", hs))
                        s_ = st[hs]
                        S_.add("scalar", ACT(ub[:], yall[:, tt, :], AF.Square, accum_out=s_[:, 0:1]), reads=[("yall", tt)], writes=["ub", ("st", hs)])
                        S_.add("scalar", ACT(s_[:, 1:2], s_[:, 0:1], AF.Sqrt, bias=c["eps"][:, 0:1], scale=1.0 / D), reads=[("st", hs), "epst"],
                               writes=[("st", hs)])
                        S_.add("vector", RCP(s_[:, 2:3], s_[:, 1:2]), reads=[("st", hs)], writes=[("st", hs)])
                        S_.add("vector", STT(yall[:, tt, :], yall[:, tt, :], s_[:, 2:3], grep[:], ALU.mult, ALU.mult),
                               reads=[("yall", tt), ("st", hs), "grep"], writes=[("yall", tt)])
                        S_.add("vector", TT(ht[hs][:], ht[hs][:], yall[:, tt, :], ALU.add), reads=[("ht", hs), ("yall", tt)], writes=[("ht", hs)])
                        yield tt, t, hs

                def prep_from_h(tt, hs, gidx_loaded):
                    norm_tile(S_, c, ht[hs][:], ("ht", hs), grep[:], "grep", ub[:], "ub", st[hs], ("st", hs))
                    transpose_tile(S_, c, ub, "ub", NCH, ptr, [("ptr", 0), ("ptr", 1)],
                                   lambda hf, cn, tt=tt: uT[:, hf * 8:hf * 8 + cn, tt * 128:(tt + 1) * 128], [("uT", tt)])

                uTk = [("uT", tt) for tt in range(4)]
                wg_v = w_g.rearrange("(c p) n -> p c n", p=128)
                wbm_v = w_bm.rearrange("(c p) n -> p c n", p=128)
                wbd_v = w_bd.rearrange("(c p) n -> p c n", p=128)
                wout_v = w_out.rearrange("(c p) n -> p c n", p=128)
                wup_v = w_up.rearrange("(c p) n -> p c n", p=128)
                wdn_v = w_down.rearrange("(c p) n -> p c n", p=128)
                wpg_v = w_pg.rearrange("(c p) n -> p c n", p=128)
                wpp_v = w_pp.rearrange("(c p) n -> p c n", p=128)
                for G in range(NGO):
                    tok = slice(G * 512, (G + 1) * 512)
                    load_g(0)
                    for tt in range(4):
                        t = G * 4 + tt
                        hs = tt % 2
                        S_.dma("sync", ht[hs][:], xo[t * 128:(t + 1) * 128, :], writes=[("ht", hs)], key=("ht", hs))
                        prep_from_h(tt, hs, 0)
                    for cg in range(8):
                        wt, wk = load_w(wg_v[:, :, cg * 512:(cg + 1) * 512], NCH)
                        for jj in range(4):
                            j = cg * 4 + jj
                            p = next_pk()
                            for k in range(NCH):
                                S_.add("tensor", MM(pk[p][:], wt[:, k, jj * 128:(jj + 1) * 128], uT[:, k, :], k == 0, k == NCH - 1),
                                       reads=uTk + [wk] if k in (0, NCH - 1) else (), writes=[("pk", p)])
                            S_.add("scalar", ACT(R[:, j, :], pk[p][:], AF.Sigmoid), reads=[("pk", p)], writes=[("R", j)])
                    S_.dma("sync", R[:, 48:64, :], OT[:, :, tok].rearrange("c p t -> p c t"), writes=[("R", 48 + k) for k in range(16)], key="oab")
                    for cg in range(4):
                        wa, wak = load_w(wbm_v[:, :, cg * 512:(cg + 1) * 512], 8)
                        wd, wdk = load_w(wbd_v[:, :, cg * 512:(cg + 1) * 512], 8)
                        for jj in range(4):
                            j = cg * 4 + jj
                            pa = next_pk()
                            pb = next_pk()
                            for k in range(8):
                                S_.add("tensor", MM(pk[pa][:], wa[:, k, jj * 128:(jj + 1) * 128], R[:, 48 + k, :], k == 0, k == 7),
                                       reads=[("R", 48 + kk) for kk in range(8)] + [wak] if k in (0, 7) else (), writes=[("pk", pa)])
                            for k in range(8):
                                S_.add("tensor", MM(pk[pb][:], wd[:, k, jj * 128:(jj + 1) * 128], R[:, 56 + k, :], k == 0, k == 7),
                                       reads=[("R", 56 + kk) for kk in range(8)] + [wdk] if k in (0, 7) else (), writes=[("pk", pb)])
                            r_ = rl[j % 2]
                            S_.add("vector", TT(r_[:], pk[pa][:], R[:, j, :], ALU.mult), reads=[("pk", pa), ("R", j)], writes=[("rl", j % 2)])
                            S_.add("vector", TT(sgt[:], pk[pb][:], R[:, 16 + j, :], ALU.mult), reads=[("pk", pb), ("R", 16 + j)], writes=["sgt"])
                            S_.add("vector", TT(R[:, 32 + j, :], r_[:], sgt[:], ALU.add), reads=[("rl", j % 2), "sgt"], writes=[("R", 32 + j)])
                    load_g(1)

                    def evac_y(tt, cg, p):
                        if cg % 2 == 0:
                            S_.add("scalar", ACT(yall[:, tt, cg * 512:(cg + 1) * 512], pk[p][:], AF.Copy), reads=[("pk", p)], writes=[("yall", tt)])
                        else:
                            S_.add("vector", CP(yall[:, tt, cg * 512:(cg + 1) * 512], pk[p][:]), reads=[("pk", p)], writes=[("yall", tt)])

                    tok_major_layer(lambda tt, k: R[:, 32 + k, tt * 128:(tt + 1) * 128], lambda tt: [("R", 32 + k) for k in range(16)], NCH,
                                    lambda cg, k0, nk: wout_v[:, k0:k0 + nk, cg * 512:(cg + 1) * 512], evac_y)
                    hts = []
                    for tt, t, hs in post_norm_residual(G, xo, None, 1):
                        S_.dma("sync", H1[t * 128:(t + 1) * 128, :], ht[hs][:], reads=[("ht", hs)], writes=[("H1", t)], key=("h1s", hs))
                        hts.append((tt, hs))
                        if tt == 0:
                            pass
                    load_g(2)
                    for tt in range(4):
                        t = G * 4 + tt
                        hs = tt % 2
                        S_.dma("sync", ht[hs][:], H1[t * 128:(t + 1) * 128, :], reads=[("H1", t)], writes=[("ht", hs)], key=("ht", hs))
                        prep_from_h(tt, hs, 2)
                    for cg in range(16):
                        wt, wk = load_w(wup_v[:, :, cg * 512:(cg + 1) * 512], NCH)
                        for jj in range(4):
                            j = cg * 4 + jj
                            p = next_pk()
                            for k in range(NCH):
                                S_.add("tensor", MM(pk[p][:], wt[:, k, jj * 128:(jj + 1) * 128], uT[:, k, :], k == 0, k == NCH - 1),
                                       reads=uTk + [wk] if k in (0, NCH - 1) else (), writes=[("pk", p)])
                            r_ = rl[j % 2]
                            S_.add("scalar", ACT(r_[:], pk[p][:], AF.Relu), reads=[("pk", p)], writes=[("rl", j % 2)])
                            S_.add("vector", STT(R[:, j, :], pk[p][:], 0.0, r_[:], ALU.max, ALU.mult), reads=[("pk", p), ("rl", j % 2)],
                                   writes=[("R", j)])
                    load_g(3)
                    tok_major_layer(lambda tt, k: R[:, k, tt * 128:(tt + 1) * 128], lambda tt: [("R", k) for k in range(64)], 64,
                                    lambda cg, k0, nk: wdn_v[:, k0:k0 + nk, cg * 512:(cg + 1) * 512], evac_y)
                    for tt, t, hs in post_norm_residual(G, H1, None, 3):
                        S_.dma("sync", H2[t * 128:(t + 1) * 128, :], ht[hs][:], reads=[("ht", hs)], writes=[("H2", t)], key=("h2s", hs))
                    load_g(4)
                    for tt in range(4):
                        t = G * 4 + tt
                        hs = tt % 2
                        S_.dma("sync", ht[hs][:], H2[t * 128:(t + 1) * 128, :], reads=[("H2", t)], writes=[("ht", hs)], key=("ht", hs))
                        prep_from_h(tt, hs, 4)
                        S_.dma("sync", pt_f[:], po[t * 128:(t + 1) * 128, :], writes=["pt_f"], key="pt_f")
                        S_.add("vector", CP(pt_b[:], pt_f[:]), reads=["pt_f"], writes=["pt_b"])
                        transpose_tile(S_, c, pt_b, "pt_b", 2, ptr, [("ptr", 0), ("ptr", 1)],
                                       lambda hf, cn, tt=tt: pTt[:, 0:cn, tt * 128:(tt + 1) * 128], [("pTt", tt)])
                    load_g(5)
                    for cg in range(4):
                        wt, wk = load_w(wpg_v[:, :, cg * 512:(cg + 1) * 512], NCH)
                        wp, wpk = load_w(wpp_v[:, :, cg * 512:(cg + 1) * 512], 2)
                        for tt in range(4):
                            pa = next_pk()
                            pb = next_pk()
                            for k in range(NCH):
                                S_.add("tensor", MM(pk[pa][:], uT[:, k, tt * 128:(tt + 1) * 128], wt[:, k, :], k == 0, k == NCH - 1),
                                       reads=[("uT", tt), wk] if k in (0, NCH - 1) else (), writes=[("pk", pa)])
                            for k in range(2):
                                S_.add("tensor", MM(pk[pb][:], pTt[:, k, tt * 128:(tt + 1) * 128], wp[:, k, :], k == 0, k == 1),
                                       reads=[("pTt", tt), wpk] if k in (0, 1) else (), writes=[("pk", pb)])
                            S_.add("scalar", ACT(sgt[:], pk[pa][:], AF.Sigmoid), reads=[("pk", pa)], writes=["sgt"])
                            S_.add("vector", TT(yall[:, tt, cg * 512:(cg + 1) * 512], pk[pb][:], sgt[:], ALU.mult), reads=[("pk", pb), "sgt"],
                                   writes=[("yall", tt)])
                    for tt, t, hs in post_norm_residual(G, H2, None, 5):
                        S_.dma("sync", out[t * 128:(t + 1) * 128, :], ht[hs][:], reads=[("ht", hs)], key=("outs", hs))
                S_.finish()

        phase_proj("AK", "KQ", xb, NT, w_k, KT)
        phase_proj("AV", "V", xb, NT, w_v, VS)
        phase_proj("B", "KQ", xo, NTO, w_q, QT)
        phase_moba()
        phase_diff()
        phase_dense()
    return nc


def _prep_inputs(S, x, p, w_in, w_br_moba, w_br_diff, w_out, lambda_q1, lambda_k1, lambda_q2, lambda_k2,
                 diff_subln_g, g_mix_pre, g_mix_post, w_up, w_down, g_mlp_pre, g_mlp_post,
                 w_ple_proj, w_ple_gate, g_ple_pre, g_ple_post):
    f = lambda a: np.ascontiguousarray(np.asarray(a, dtype=np.float32))
    x = np.asarray(x, dtype=np.float32)
    p = np.asarray(p, dtype=np.float32)
    w_in = np.asarray(w_in, dtype=np.float32)[0]
    NB = S // 256
    wq = f(np.concatenate([w_in[:, 0:1024], w_in[:, 3072:4096]], axis=1))
    wk = f(np.concatenate([w_in[:, 1024:2048], w_in[:, 4096:5120]], axis=1))
    wv = f(np.concatenate([w_in[:, 2048:3072], w_in[:, 5120:6144]], axis=1))
    wg = f(w_in[:, 6144:10240])
    shared = {
        "w_k": wk, "w_v": wv, "w_q": wq, "w_g": wg,
        "w_bm": f(w_br_moba[0]), "w_bd": f(w_br_diff[0]), "w_out": f(w_out[0]),
        "w_up": f(w_up[0]), "w_down": f(w_down[0]), "w_pp": f(w_ple_proj[0]), "w_pg": f(w_ple_gate[0]),
        "lam4": f(np.stack([np.asarray(lambda_q1)[0], np.asarray(lambda_k1)[0], np.asarray(lambda_q2)[0], np.asarray(lambda_k2)[0]])),
        "subg": f(np.asarray(diff_subln_g)[0:1]),
        "gains": f(np.stack([np.asarray(a)[0] for a in (g_mix_pre, g_mix_post, g_mlp_pre, g_mlp_post, g_ple_pre, g_ple_post)])),
    }
    in_maps = []
    for c in range(8):
        b, g = c // 4, c % 4
        xbb = f(x[b])
        xo = f(xbb.reshape(NB, 256, D)[g::4].reshape(-1, D))
        pob = f(p[0, b].reshape(NB, 256, 256)[g::4].reshape(-1, 256))
        m = dict(shared)
        m["xb"] = xbb
        m["xo"] = xo
        m["po"] = pob
        m["cinfo"] = np.array([[g, 256 * g, 0, 0]], dtype=np.float32)
        in_maps.append(m)
    return in_maps


_NC_CACHE = {}


def kernel(**inputs):
    S = int(np.asarray(inputs["x"]).shape[1])
    if S not in _NC_CACHE:
        _NC_CACHE[S] = build(S)
    nc = _NC_CACHE[S]
    in_maps = _prep_inputs(S, **inputs)
    res = run_bass_kernel_spmd(nc, in_maps, core_ids=list(range(8)))
    _LAST_RES[0] = res
    NB = S // 256
    outp = np.empty((2, S, D), dtype=np.float32)
    for c in range(8):
        b, g = c // 4, c % 4
        o = np.asarray(res.results[c]["out"], dtype=np.float32).reshape(NB // 4, 256, D)
        outp[b].reshape(NB, 256, D)[g::4] = o
    return outp
```
